# Optimizing a Trainium2 kernel written in Bass

```python
import jax, jax.numpy as jnp
from jax import lax
import numpy as np

D_MODEL = 2048
BATCH = 8
SEQ = 2048
DEPTH = 1

HEAD_DIM = 64
N_Q_HEADS = D_MODEL // HEAD_DIM
N_KV_HEADS = 4
GQA_GROUP = N_Q_HEADS // N_KV_HEADS
WINDOW = 128

GLA_HEADS = 4
GLA_DK = (D_MODEL // 2) // GLA_HEADS
GLA_DV = D_MODEL // GLA_HEADS
GLA_GATE_RANK = 16
GLA_GATE_NORMALIZER = 16.0
GLA_CHUNK = 64

FFN_HIDDEN = ((8 * D_MODEL // 3 + 255) // 256) * 256

RMS_EPS = 1e-6
MASK_VALUE = -1e30

IN_WIDTHS = (
    N_Q_HEADS * HEAD_DIM,
    N_KV_HEADS * HEAD_DIM,
    N_KV_HEADS * HEAD_DIM,
    GLA_HEADS * GLA_DK,
    GLA_HEADS * GLA_DK,
    GLA_HEADS * GLA_DV,
    GLA_GATE_RANK,
    GLA_HEADS * GLA_DV,
    D_MODEL,
    D_MODEL,
)
D_IN = sum(IN_WIDTHS)

kernel_name = "hybrid_swa_sink_gla_swiglu_block"


def _rmsnorm(x, w):
    xf = x.astype(jnp.float32)
    y = xf * lax.rsqrt(jnp.mean(xf * xf, axis=-1, keepdims=True) + RMS_EPS)
    return (y * w.astype(jnp.float32)).astype(x.dtype)


def _split_offsets():
    offs, acc = [], 0
    for w in IN_WIDTHS[:-1]:
        acc += w
        offs.append(acc)
    return offs


def _swa_sink_attention(q, k, v, sinks):
    B, T = q.shape[0], q.shape[1]
    nb = T // WINDOW
    qb = q.reshape(B, nb, WINDOW, N_KV_HEADS, GQA_GROUP, HEAD_DIM)
    kb = k.reshape(B, nb, WINDOW, N_KV_HEADS, HEAD_DIM)
    vb = v.reshape(B, nb, WINDOW, N_KV_HEADS, HEAD_DIM)
    k_prev = jnp.concatenate([jnp.zeros_like(kb[:, :1]), kb[:, :-1]], axis=1)
    v_prev = jnp.concatenate([jnp.zeros_like(vb[:, :1]), vb[:, :-1]], axis=1)
    kk = jnp.concatenate([k_prev, kb], axis=2)
    vv = jnp.concatenate([v_prev, vb], axis=2)
    s = jnp.einsum('bnqhgd,bnkhd->bhgnqk', qb, kk).astype(jnp.float32) * (HEAD_DIM ** -0.5)
    qi = jnp.arange(WINDOW)[:, None]
    ki = jnp.arange(2 * WINDOW)[None, :]
    rel = qi + WINDOW - ki
    band = (rel >= 0) & (rel < WINDOW)
    blk = jnp.arange(nb)[:, None, None]
    mask = band[None] & ((blk > 0) | (ki[None] >= WINDOW))
    s = jnp.where(mask, s, MASK_VALUE)
    sink = sinks.astype(jnp.float32).reshape(N_KV_HEADS, GQA_GROUP)[None, :, :, None, None, None]
    sink = jnp.broadcast_to(sink, s.shape[:-1] + (1,))
    p = jax.nn.softmax(jnp.concatenate([s, sink], axis=-1), axis=-1)[..., :-1]
    o = jnp.einsum('bhgnqk,bnkhd->bnqhgd', p.astype(vv.dtype), vv)
    return o.reshape(B, T, N_Q_HEADS * HEAD_DIM)


def _gla(q, k, v, log_a):
    B, T = q.shape[0], q.shape[1]
    nc = T // GLA_CHUNK

    def chunk(t, d):
        return t.astype(jnp.float32).reshape(B, nc, GLA_CHUNK, GLA_HEADS, d).transpose(0, 3, 1, 2, 4)

    qc = chunk(q, GLA_DK) * (GLA_DK ** -0.5)
    kc = chunk(k, GLA_DK)
    vc = chunk(v, GLA_DV)
    g = jnp.cumsum(chunk(log_a, GLA_DK), axis=3)
    g_last = g[..., -1:, :]
    q_dec = qc * jnp.exp(g)
    k_inv = kc * jnp.exp(-g)
    k_to_end = kc * jnp.exp(g_last - g)
    causal = jnp.tril(jnp.ones((GLA_CHUNK, GLA_CHUNK), dtype=bool))
    att = jnp.where(causal, jnp.einsum('bhnid,bhnjd->bhnij', q_dec, k_inv), 0.0)
    o_intra = jnp.einsum('bhnij,bhnjv->bhniv', att, vc)
    upd = jnp.einsum('bhnjd,bhnjv->bhndv', k_to_end, vc)
    decay = jnp.exp(g_last[..., 0, :])

    def step(state, inp):
        d_c, u_c = inp
        return d_c[..., None] * state + u_c, state

    s0 = jnp.zeros((B, GLA_HEADS, GLA_DK, GLA_DV), jnp.float32)
    _, s_prev = lax.scan(step, s0, (jnp.moveaxis(decay, 2, 0), jnp.moveaxis(upd, 2, 0)))
    s_prev = jnp.moveaxis(s_prev, 0, 2)
    o_inter = jnp.einsum('bhnid,bhndv->bhniv', q_dec, s_prev)
    o = o_intra + o_inter
    return o.transpose(0, 2, 3, 1, 4).reshape(B, T, GLA_HEADS, GLA_DV)


def setup_inputs(seed: int = 0) -> dict:
    key = jax.random.key(seed)
    ks = jax.random.split(key, 14)
    f32 = jnp.float32

    def normal(k, shape, scale):
        return jax.random.normal(k, shape, f32) * scale

    return {
        "x": jax.random.normal(ks[0], (BATCH, SEQ, D_MODEL), f32),
        "norm1_w": 1.0 + normal(ks[1], (DEPTH, D_MODEL), 0.02),
        "w_in": normal(ks[2], (DEPTH, D_MODEL, D_IN), D_MODEL ** -0.5),
        "gla_gate_w2": normal(ks[3], (DEPTH, GLA_GATE_RANK, GLA_HEADS * GLA_DK), GLA_GATE_RANK ** -0.5),
        "gla_gate_b": normal(ks[4], (DEPTH, GLA_HEADS * GLA_DK), 0.02),
        "attn_sinks": normal(ks[5], (DEPTH, N_Q_HEADS), 0.5),
        "gla_norm_w": 1.0 + normal(ks[6], (DEPTH, GLA_DV), 0.02),
        "w_out": normal(ks[7], (DEPTH, D_MODEL, D_MODEL), D_MODEL ** -0.5),
        "norm2_w": 1.0 + normal(ks[8], (DEPTH, D_MODEL), 0.02),
        "w_ffn_gate": normal(ks[9], (DEPTH, D_MODEL, FFN_HIDDEN), D_MODEL ** -0.5),
        "w_ffn_up": normal(ks[10], (DEPTH, D_MODEL, FFN_HIDDEN), D_MODEL ** -0.5),
        "w_ffn_down": normal(ks[11], (DEPTH, FFN_HIDDEN, D_MODEL), FFN_HIDDEN ** -0.5),
        "final_norm_w": 1.0 + normal(ks[12], (D_MODEL,), 0.02),
    }


def reference(x, norm1_w, w_in, gla_gate_w2, gla_gate_b, attn_sinks, gla_norm_w, w_out,
              norm2_w, w_ffn_gate, w_ffn_up, w_ffn_down, final_norm_w):
    B, T, _ = x.shape
    offs = _split_offsets()
    h = x
    for l in range(DEPTH):
        u = _rmsnorm(h, norm1_w[l])
        proj = u @ w_in[l]
        aq, ak, av, gq, gk, gv, g_lr, g_r, gate_a, gate_b = jnp.split(proj, offs, axis=-1)

        attn_o = _swa_sink_attention(aq, ak, av, attn_sinks[l])

        gate_logit = (g_lr @ gla_gate_w2[l] + gla_gate_b[l]).astype(jnp.float32)
        log_a = jax.nn.log_sigmoid(gate_logit) / GLA_GATE_NORMALIZER
        gla_o = _gla(gq.reshape(B, T, GLA_HEADS, GLA_DK),
                     gk.reshape(B, T, GLA_HEADS, GLA_DK),
                     gv.reshape(B, T, GLA_HEADS, GLA_DV),
                     log_a.reshape(B, T, GLA_HEADS, GLA_DK))
        gla_o = _rmsnorm(gla_o, gla_norm_w[l]).astype(x.dtype)
        gla_o = gla_o.reshape(B, T, GLA_HEADS * GLA_DV) * jax.nn.silu(g_r)

        merged = jax.nn.sigmoid(gate_a) * attn_o + jax.nn.sigmoid(gate_b) * gla_o
        h = h + merged @ w_out[l]

        v2 = _rmsnorm(h, norm2_w[l])
        ff = jax.nn.silu(v2 @ w_ffn_gate[l]) * (v2 @ w_ffn_up[l])
        h = h + ff @ w_ffn_down[l]
    return _rmsnorm(h, final_norm_w)
```

```python
import numpy as np
import concourse.bass as bass
import concourse.mybir as mybir
from concourse.bass_utils import run_bass_kernel_spmd

F32 = mybir.dt.float32
BF16 = mybir.dt.bfloat16
AF = mybir.ActivationFunctionType
ALU = mybir.AluOpType

D = 2048
T = 2048
TT = 512
NT = TT // 128
NCHUNK = T // TT
KC = D // 128
FF = 5632
HC = FF // 128
DIN = 12816
O_AQ, O_AK, O_AV, O_GQ, O_GK, O_GV, O_GLR, O_GR, O_GA, O_GB = (
    0, 2048, 2304, 2560, 3584, 4608, 6656, 6672, 8720, 10768)
EPS = 1e-6
NEG = -30000.0
NSLOT = 2
NPS = 6

ENGS = ("pe", "act", "dve", "pool", "sp")


class Op:
    __slots__ = ("eng", "fn", "deps", "chan", "inc", "signal", "count", "is_dma", "glast")


class Sched:
    def __init__(self, nc):
        self.nc = nc
        self.streams = {e: [] for e in ENGS}
        self.last_w = {}
        self.readers = {}
        self.chan_ops = {}
        self.final_waits = []

    @staticmethod
    def _is_arena(r):
        key = r[0] if isinstance(r, tuple) else r
        return isinstance(key, str) and (key.startswith("a_") or key.startswith("g_") or key in ("ffT", "mk_tmp"))

    def op(self, eng, fn, reads=(), writes=(), dma=False, chan=None):
        reads = list(reads)
        writes = list(writes)
        if any(self._is_arena(r) for r in reads) or any(self._is_arena(r) for r in writes):
            if "ARENA" not in writes:
                reads.append("ARENA")
        o = Op()
        o.eng = eng
        o.fn = fn
        o.is_dma = dma
        o.inc = 16 if dma else 1
        if dma:
            o.chan = ("dma", chan if chan is not None else tuple(writes)[0])
        else:
            o.chan = eng
        o.signal = bool(dma)
        o.count = None
        o.glast = None
        deps = []
        for r in reads:
            w = self.last_w.get(r)
            if w is not None:
                deps.append((w, "raw"))
        for r in writes:
            w = self.last_w.get(r)
            if w is not None:
                deps.append((w, "waw"))
            for rd in self.readers.get(r, ()):
                deps.append((rd, "war"))
        keep = []
        seen = set()
        for d, kind in deps:
            if d is o or id(d) in seen:
                continue
            if (not d.is_dma) and (not dma) and d.eng == eng:
                if eng == "pe":
                    continue
            seen.add(id(d))
            keep.append(d)
        o.deps = keep
        wset = set(writes)
        for r in writes:
            self.last_w[r] = o
            self.readers[r] = []
        for r in reads:
            if r in wset:
                continue
            self.readers.setdefault(r, []).append(o)
        self.streams[eng].append(o)
        self.chan_ops.setdefault(o.chan, []).append(o)
        return o

    def finalize_on(self, eng, ops):
        self.final_waits.append((eng, list(ops)))

    def emit(self):
        nc = self.nc
        for e in ENGS:
            for o in self.streams[e]:
                for d in o.deps:
                    d.signal = True
        for eng, ops in self.final_waits:
            for d in ops:
                d.signal = True
        sems = {}
        nsem = 0
        for chan, ops in self.chan_ops.items():
            c = 0
            any_sig = False
            for o in ops:
                if o.signal:
                    c += o.inc
                    o.count = c
                    any_sig = True
            if any_sig:
                sems[chan] = nc.alloc_semaphore(name="sm%d" % nsem)
                nsem += 1
        finals = {}
        for eng, ops in self.final_waits:
            finals.setdefault(eng, []).extend(ops)
        handles = {"pe": "tensor", "act": "scalar", "dve": "vector", "pool": "gpsimd", "sp": "sync"}
        with nc.Block() as block:
            for e in ENGS:
                stream = self.streams[e]
                fin = finals.get(e, [])
                if not stream and not fin:
                    continue

                def body(engine, stream=stream, fin=fin):
                    known = {}
                    for o in stream:
                        need = {}
                        for d in o.deps:
                            dc = d.glast.count if d.glast is not None else d.count
                            if dc > need.get(d.chan, 0):
                                need[d.chan] = dc
                        for ch, cnt in need.items():
                            if known.get(ch, 0) >= cnt:
                                continue
                            engine.wait_ge(sems[ch], cnt)
                            known[ch] = cnt
                        ins = o.fn(engine)
                        if o.signal:
                            ins.then_inc(sems[o.chan], o.inc)
                    for d in fin:
                        if known.get(d.chan, 0) >= d.count:
                            continue
                        engine.wait_ge(sems[d.chan], d.count)
                        known[d.chan] = d.count

                getattr(block, handles[e])(body)


class _Stop(Exception):
    pass


def build_nc(nchunk=NCHUNK, debug=False, stage=None):
    nc = bass.Bass("TRN2", target_bir_lowering=False)
    S = Sched(nc)

    def din(name, shape):
        return nc.dram_tensor(name, list(shape), F32, kind="ExternalInput").ap()

    x = din("x", [T, D])
    w_in = din("w_in", [D, DIN])
    w_out = din("w_out", [D, D])
    w_g = din("w_g", [D, FF])
    w_u = din("w_u", [D, FF])
    w_d = din("w_d", [FF, D])
    n1 = din("n1", [128, KC])
    n2 = din("n2", [128, KC])
    sink_l = din("sink_l", [128, 16])
    fnw = din("fnw", [128, D])
    gnw = din("gnw", [128, 512])
    w2b_d = din("w2b", [17, 1024])
    cst = din("cst", [128, 5, 128])
    mk = din("mk", [128, 2, 512])
    out = nc.dram_tensor("out", [T, D], F32, kind="ExternalOutput").ap()
    dbg = {}
    if debug:
        dbg["mT"] = nc.dram_tensor("dbg_mT", [128, KC, TT], F32, kind="ExternalOutput").ap()
        dbg["h1"] = nc.dram_tensor("dbg_h1", [128, NT, D], F32, kind="ExternalOutput").ap()
        dbg["xnT"] = nc.dram_tensor("dbg_xnT", [128, KC, TT], F32, kind="ExternalOutput").ap()

    w_in_v = w_in.rearrange("(k p) c -> p k c", p=128)
    w_out_v = w_out.rearrange("(k p) c -> p k c", p=128)
    w_g_v = w_g.rearrange("(k p) c -> p k c", p=128)
    w_u_v = w_u.rearrange("(k p) c -> p k c", p=128)
    w_d_v = w_d.rearrange("(k p) c -> p k c", p=128)

    sb = nc.alloc_sbuf_tensor
    xt = sb("xt", [128, NT, D], F32)
    xn = sb("xn", [128, D], BF16)
    actT = sb("actT", [128, KC, TT], BF16)
    wsl = [sb("wsl%d" % i, [128, KC, 512], BF16) for i in range(NSLOT)]
    mT = sb("mT", [128, KC, TT], BF16)
    kT = sb("kT", [128, 4, 128 + TT], BF16)
    vAB = sb("vAB", [128, 4, NT + 1, 2, 128], BF16)
    Sst = sb("Sst", [128, 4, 2, 512], F32)
    Sbf = sb("Sbf", [128, 4, 2, 512], BF16)
    glrT = sb("glrT", [17, TT], F32)
    c_f32 = sb("c_f32", [128, 5, 128], F32)
    identb = sb("identb", [128, 128], BF16)
    maskb = sb("maskb", [128, 2, 512], BF16)
    onesAB = sb("onesAB", [128, 2, 128], BF16)
    sinkexp = sb("sinkexp", [128, 16], F32)
    fnw_sb = sb("fnw_sb", [128, D], F32)
    gnw_sb = sb("gnw_sb", [128, 512], F32)
    w2b = sb("w2b_sb", [17, 1024], F32)
    n1_sb = sb("n1_sb", [128, KC], F32)
    n2_sb = sb("n2_sb", [128, KC], F32)
    epsc = sb("epsc", [128, 1], F32)
    stat = sb("stat", [128, 8], F32)
    bar = sb("bar", [128, 1], F32)
    ffT = sb("ffT", [128, HC, TT], BF16)
    ARENA = HC * TT * 2
    off = [0]
    ffT_off = None

    def carve(name, shape, dt, base):
        n = 1
        for s_ in shape[1:]:
            n *= s_
        nbytes = n * (4 if dt == F32 else 2)
        o_ = base[0]
        base[0] += (nbytes + 63) // 64 * 64
        return (name, shape, dt, o_, nbytes)

    def bview(col0, shape):
        n = 1
        for s_ in shape[1:]:
            n *= s_
        flat = ffT[:, :, :].rearrange("p a b -> p (a b)")[:, col0:col0 + n]
        if len(shape) == 2:
            return flat
        if len(shape) == 3:
            return flat.rearrange("p (a b) -> p a b", a=shape[1])
        raise ValueError

    def fview(col0, shape):
        n = 1
        for s_ in shape[1:]:
            n *= s_
        flat = ffT[:, :, :].rearrange("p a b -> p (a b)")[:, col0:col0 + 2 * n].bitcast(F32)
        if len(shape) == 2:
            return flat
        if len(shape) == 3:
            return flat.rearrange("p (a b) -> p a b", a=shape[1])
        raise ValueError

    a_qT = bview(0, [128, 4, TT])
    a_sga = bview(2048, [128, 4, TT])
    a_pT = [[bview(4096 + (b * 4 + j) * 512, [128, 512]) for j in range(4)] for b in range(2)]
    a_t1 = fview(8192, [128, 512])
    a_t2 = fview(9216, [128, 512])
    g_qT = bview(0, [128, 2, TT])
    g_kT = bview(1024, [128, 2, TT])
    g_ktok = bview(2048, [128, NT, 256])
    g_v = bview(3072, [128, NT, 512])
    g_rs = bview(5120, [128, NT, 512])
    g_bs = bview(7168, [128, NT, 512])
    g_attm = bview(9216, [128, NT, 128])
    g_G = bview(9728, [128, NT, 512])
    g_la = fview(11776, [128, NT, 256])
    g_eg = fview(13824, [128, 2, TT])
    g_ei = fview(15872, [128, 2, TT])
    g_es = fview(17920, [128, NT, 256])

    ps = [nc.alloc_psum_tensor("ps%d" % i, [128, 512], F32) for i in range(NPS)]
    ptrs = [nc.alloc_psum_tensor("ptr%d" % i, [128, 8, 128], BF16) for i in range(2)]
    st = {"ps": 0, "w": 0, "ev": 0, "tr": 0}

    def nps():
        b = st["ps"] % NPS
        st["ps"] += 1
        return b

    WSUB = 8
    wsl.append(mT)
    MT_ALL = [("mT", g_, i_) for g_ in range(4) for i_ in range(NT)]

    def wres(slot):
        if slot == 2:
            return MT_ALL
        return [("w", slot, q) for q in range(WSUB)]

    def load_w(parts, ring=(0, 1)):
        slot = ring[st["w"] % len(ring)]
        st["w"] += 1
        P = len(parts)
        ops = []
        alln = wres(slot)
        for i, (dst_fn, src) in enumerate(parts):
            names = [alln[q] for q in range(len(alln)) if q % P == i]
            ops.append(S.op("pool", (lambda e, dst_fn=dst_fn, src=src, slot=slot: e.dma_start(out=dst_fn(wsl[slot]),
                                                                                           in_=src)),
                            writes=names, dma=True, chan=("w", slot)))
        for o_ in ops:
            o_.glast = ops[-1]
        return slot

    def mm(mms, reads, writes):
        def fn(e):
            ins = None
            for (o_, l_, r_, s0, s1) in mms:
                ins = e.matmul(o_, lhsT=l_, rhs=r_, start=s0, stop=s1)
            return ins
        return S.op("pe", fn, reads=reads, writes=writes)

    def ev_engine():
        st["ev"] += 1
        return "act" if st["ev"] % 2 == 0 else "dve"

    def copy_ev(out_ap, in_ap, reads, writes, scale=None, eng=None):
        eng = eng or ev_engine()
        if eng == "act":
            if scale is None:
                S.op("act", lambda e: e.copy(out=out_ap, in_=in_ap), reads=reads, writes=writes)
            else:
                S.op("act", lambda e: e.activation(out=out_ap, in_=in_ap, func=AF.Copy, scale=scale),
                     reads=reads, writes=writes)
        else:
            if scale is None:
                S.op("dve", lambda e: e.tensor_copy(out=out_ap, in_=in_ap), reads=reads, writes=writes)
            else:
                S.op("dve", lambda e: e.tensor_scalar(out=out_ap, in0=in_ap, scalar1=scale, scalar2=None,
                                                        op0=ALU.mult), reads=reads, writes=writes)

    arena_names = set()

    def ar(*names):
        for n_ in names:
            arena_names.add(n_)
        return list(names)

    def barrier():
        S.op("dve", lambda e: e.memset(bar[:], 0.0), writes=["ARENA", "bar"])

    S.op("sp", lambda e: e.dma_start(out=c_f32[:], in_=cst), writes=["c_f32"], dma=True)
    S.op("sp", lambda e: e.dma_start(out=n1_sb[:], in_=n1), writes=["n1"], dma=True)
    S.op("sp", lambda e: e.dma_start(out=n2_sb[:], in_=n2), writes=["n2"], dma=True)
    S.op("sp", lambda e: e.dma_start(out=sinkexp[:], in_=sink_l), writes=["sinkexp"], dma=True)
    S.op("sp", lambda e: e.dma_start(out=fnw_sb[:], in_=fnw), writes=["fnw"], dma=True)
    S.op("sp", lambda e: e.dma_start(out=gnw_sb[:], in_=gnw), writes=["gnw"], dma=True)
    S.op("sp", lambda e: e.dma_start(out=w2b[:], in_=w2b_d), writes=["w2b"], dma=True)
    mk_tmp = fview(0, [128, 2, 512])
    S.op("sp", lambda e: e.dma_start(out=mk_tmp, in_=mk), writes=ar("mk_tmp"), dma=True)
    S.op("dve", lambda e: e.tensor_copy(out=maskb[:], in_=mk_tmp), reads=["mk_tmp"], writes=["maskb"])
    S.op("dve", lambda e: e.tensor_copy(out=identb[:], in_=c_f32[:, 0, :]), reads=["c_f32"], writes=["identb"])
    S.op("act", lambda e: e.activation(out=sinkexp[:], in_=sinkexp[:], func=AF.Exp), reads=["sinkexp"],
         writes=["sinkexp"])
    S.op("dve", lambda e: e.memset(epsc[:], EPS), writes=["epsc"])
    S.op("dve", lambda e: e.memset(onesAB[:].rearrange("p a b -> p (a b)"), 0.0), writes=["onesAB"])
    S.op("dve", lambda e: e.memset(onesAB[:, 0, 0:64], 1.0), writes=["onesAB"])
    S.op("dve", lambda e: e.memset(onesAB[:, 1, 64:128], 1.0), writes=["onesAB"])
    S.op("dve", lambda e: e.memset(vAB[:].rearrange("p a b c d -> p (a b c d)"), 0.0), writes=[("v", k_, t_) for k_ in range(4) for t_ in range(NT + 1)])
    S.op("dve", lambda e: e.memset(kT[:].rearrange("p a b -> p (a b)"), 0.0), writes=[("kT", k_, t_) for k_ in range(4) for t_ in range(NT + 1)])
    S.op("dve", lambda e: e.memset(Sst[:].rearrange("p a b c -> p (a b c)"), 0.0), writes=[("S", h_, d_) for h_ in range(4) for d_ in range(2)])
    S.op("dve", lambda e: e.memset(Sbf[:].rearrange("p a b c -> p (a b c)"), 0.0), writes=[("Sb", h_, d_) for h_ in range(4) for d_ in range(2)])
    S.op("dve", lambda e: e.memset(glrT[:], 1.0), writes=["glrT"])
    barrier()

    identf = c_f32[:, 0, :]
    Uc = c_f32[:, 1, :]
    Usuf = c_f32[:, 2, :]
    tril = c_f32[:, 3, :]

    def transposes_to_actT(i, nw_sb, nwname):
        for g in range(4):
            half = st["tr"] % 2
            st["tr"] += 1

            def tr(e, g=g, half=half):
                ins = None
                for j in range(4):
                    k = g * 4 + j
                    ins = e.transpose(out=ptrs[half][:, j, :], in_=xn[:, k * 128:(k + 1) * 128], identity=identb[:])
                return ins
            S.op("pe", tr, reads=["xn", "identb"], writes=[("ptr", half)])
            for j in range(4):
                k = g * 4 + j
                eng = ev_engine()
                o_ap = actT[:, k, i * 128:(i + 1) * 128]
                i_ap = ptrs[half][:, j, :]
                sc = nw_sb[:, k:k + 1]
                if eng == "act":
                    S.op("act", lambda e, o_ap=o_ap, i_ap=i_ap, sc=sc: e.activation(out=o_ap, in_=i_ap, func=AF.Copy,
                                                                                     scale=sc),
                         reads=[("ptr", half), nwname], writes=[("actT", i)])
                else:
                    S.op("dve", lambda e, o_ap=o_ap, i_ap=i_ap, sc=sc: e.tensor_scalar(out=o_ap, in0=i_ap, scalar1=sc,
                                                                                        scalar2=None, op0=ALU.mult),
                         reads=[("ptr", half), nwname], writes=[("actT", i)])

    def rstd_from(ss_col, out_col, n):
        S.op("act", lambda e: e.activation(out=stat[:, 1:2], in_=ss_col, func=AF.Sqrt, scale=1.0 / n,
                                           bias=epsc[:]), reads=["st0", "epsc"], writes=["st1"])
        S.op("dve", lambda e: e.reciprocal(out=out_col, in_=stat[:, 1:2]), reads=["st1"], writes=["st2"])

    def rms_to_actT(i, nw_sb, nwname):
        S.op("act", lambda e: e.activation(out=xn[:], in_=xt[:, i, :], func=AF.Square, accum_out=stat[:, 0:1]),
             reads=[("x", i)], writes=["xn", "st0"])
        rstd_from(stat[:, 0:1], stat[:, 2:3], D)
        S.op("act", lambda e: e.activation(out=xn[:], in_=xt[:, i, :], func=AF.Copy, scale=stat[:, 2:3]),
             reads=[("x", i), "st2"], writes=["xn"])
        transposes_to_actT(i, nw_sb, nwname)

    def norm1_from_dram(cc, i):
        r0 = cc * TT + i * 128
        S.op("pool", lambda e: e.dma_start(out=xn[:], in_=x[r0:r0 + 128, :]), writes=["xn"], dma=True, chan="xn")
        S.op("act", lambda e: e.activation(out=actT[:, :, i * 128:(i + 1) * 128],
                                           in_=xn[:].rearrange("p (a b) -> p a b", a=KC), func=AF.Square,
                                           accum_out=stat[:, 0:1]),
             reads=["xn"], writes=[("actT", i), "st0"])
        rstd_from(stat[:, 0:1], stat[:, 2:3], D)
        S.op("act", lambda e: e.activation(out=xn[:], in_=xn[:], func=AF.Copy, scale=stat[:, 2:3]),
             reads=["xn", "st2"], writes=["xn"])
        transposes_to_actT(i, n1_sb, "n1")

    actT_all = [("actT", i) for i in range(NT)]

    def proj_feat(slot, col0, ncols, evac):
        b = nps()
        mm([(ps[b][0:ncols, :], wsl[slot][:, k, col0:col0 + ncols], actT[:, k, :], k == 0, k == KC - 1)
            for k in range(KC)], reads=wres(slot) + actT_all, writes=[("ps", b)])
        evac(b)

    def proj_tok(slot, col0, ncols, i, evac):
        b = nps()
        mm([(ps[b][:, 0:ncols], actT[:, k, i * 128:(i + 1) * 128], wsl[slot][:, k, col0:col0 + ncols], k == 0,
             k == KC - 1) for k in range(KC)], reads=wres(slot) + [("actT", i)], writes=[("ps", b)])
        evac(b)

    pending = []
    out_dmas = []

    def drain_final():
        if not pending:
            return
        cc, i = pending.pop(0)
        r0 = cc * TT + i * 128
        S.op("act", lambda e: e.activation(out=xn[:], in_=xt[:, i, :], func=AF.Square, accum_out=stat[:, 0:1]),
             reads=[("x", i)], writes=["xn", "st0"])
        rstd_from(stat[:, 0:1], stat[:, 2:3], D)
        S.op("act", lambda e: e.activation(out=xt[:, i, :], in_=xt[:, i, :], func=AF.Copy, scale=stat[:, 2:3]),
             reads=[("x", i), "st2"], writes=[("x", i)])
        S.op("dve", lambda e: e.tensor_tensor(out=xt[:, i, :], in0=xt[:, i, :], in1=fnw_sb[:], op=ALU.mult),
             reads=[("x", i), "fnw"], writes=[("x", i)])
        od = S.op("sp", lambda e: e.dma_start(out=out[r0:r0 + 128, :], in_=xt[:, i, :]),
                  reads=[("x", i)], writes=[("out", cc, i)], dma=True, chan=("out", i))
        out_dmas.append(od)

    def ckpt(name):
        if stage == name:
            raise _Stop()

    try:
      for c in range(nchunk):
          t0 = c * TT
          if c == 0:
              for i in range(NT):
                  norm1_from_dram(0, i)
          if debug and c == 0:
              S.op("pool", lambda e: e.dma_start(out=dbg["xnT"], in_=actT[:]), reads=actT_all, writes=["dbg_xnT"],
                   dma=True)

          ckpt('A')
          kparts = []
          for kvh in range(4):
              for dup in range(2):
                  kparts.append((
                      (lambda w_, kvh=kvh, dup=dup: w_[:, :, kvh * 128 + dup * 64:kvh * 128 + dup * 64 + 64]),
                      w_in_v[:, :, O_AK + kvh * 64:O_AK + (kvh + 1) * 64]))
          slot = load_w(kparts)
          for kvh in range(4):
              def evk(b, kvh=kvh):
                  copy_ev(kT[:, kvh, 128:128 + TT], ps[b][:, :], reads=[("ps", b)],
                          writes=[("kT", kvh, t_) for t_ in range(1, NT + 1)])
              proj_feat(slot, kvh * 128, 128, evk)
              drain_final()
          slot = load_w([((lambda w_: w_[:, :, 0:256]), w_in_v[:, :, O_AV:O_AV + 256]),
                         ((lambda w_: w_[:, :, 256:272]), w_in_v[:, :, O_GLR:O_GLR + 16])])
          for i in range(NT):
              def evv(b, i=i):
                  src = ps[b][:, 0:256].rearrange("p (a b) -> p a b", a=4)
                  wr = [("v", k_, 1 + i) for k_ in range(4)]
                  S.op("act", lambda e: e.copy(out=vAB[:, :, 1 + i, 0, 0:64], in_=src), reads=[("ps", b)], writes=wr)
                  S.op("dve", lambda e: e.tensor_copy(out=vAB[:, :, 1 + i, 1, 64:128], in_=src), reads=[("ps", b)],
                       writes=wr)
              proj_tok(slot, 0, 256, i, evv)

          def evg(b):
              copy_ev(glrT[0:16, :], ps[b][0:16, :], reads=[("ps", b)], writes=["glrT"])
          proj_feat(slot, 256, 16, evg)
          while pending:
              drain_final()
          for i in range(NT):
              S.op("sp", lambda e, i=i, t0=t0: e.dma_start(out=xt[:, i, :], in_=x[t0 + i * 128:t0 + (i + 1) * 128, :]),
                   writes=[("x", i)], dma=True, chan=("x", i))

          ckpt('B')
          for kvh in range(4):
              slot = load_w([((lambda w_: w_[:, :, :]), w_in_v[:, :, O_AQ + kvh * 512:O_AQ + (kvh + 1) * 512])])
              for pr in range(4):
                  def evq(b, pr=pr):
                      copy_ev(a_qT[:, pr, :], ps[b][:, :], reads=[("ps", b)], writes=ar(("a_qT", pr)))
                  proj_feat(slot, pr * 128, 128, evq)
              slot = load_w([((lambda w_: w_[:, :, :]), w_in_v[:, :, O_GA + kvh * 512:O_GA + (kvh + 1) * 512])])
              for pr in range(4):
                  def evga(b, pr=pr):
                      S.op("act", lambda e: e.activation(out=a_sga[:, pr, :], in_=ps[b][:, :], func=AF.Sigmoid),
                           reads=[("ps", b)], writes=ar(("a_sga", pr)))
                  proj_feat(slot, pr * 128, 128, evga)
              qres = [("a_qT", pr) for pr in range(4)]
              gres = [("a_sga", pr) for pr in range(4)]
              for n in range(NT):
                  gn = c * NT + n
                  kbs = ([0] if gn > 0 else []) + [1]
                  buf = n % 2
                  for half in range(2):
                      hs = slice(half * 64, half * 64 + 64)
                      for kb in kbs:
                          kt_idx = n + kb
                          b = nps()
                          mm([(ps[b][:, :], kT[hs, kvh, kt_idx * 128:(kt_idx + 1) * 128],
                               a_qT[hs, :, n * 128:(n + 1) * 128], True, True)],
                             reads=[("kT", kvh, kt_idx)] + qres, writes=[("ps", b)])
                          pt = a_pT[buf][half * 2 + kb]
                          S.op("act", lambda e, pt=pt, b=b: e.activation(out=pt, in_=ps[b][:, :], func=AF.Exp,
                                                                          scale=0.125),
                               reads=[("ps", b)], writes=ar(("a_pT", buf, half * 2 + kb)))
                          S.op("dve", lambda e, pt=pt, kb=kb: e.tensor_tensor(out=pt, in0=pt, in1=maskb[:, kb, :],
                                                                              op=ALU.mult),
                               reads=[("a_pT", buf, half * 2 + kb), "maskb"], writes=[("a_pT", buf, half * 2 + kb)])
                  bo = nps()
                  bd = nps()
                  seq = [(half, kb) for half in range(2) for kb in kbs]
                  mm([(ps[bo][:, :], vAB[:, kvh, n + kb, half, :], a_pT[buf][half * 2 + kb], idx == 0,
                       idx == len(seq) - 1) for idx, (half, kb) in enumerate(seq)],
                     reads=[("v", kvh, n + kb) for kb in kbs] + [("a_pT", buf, half * 2 + kb) for half, kb in seq],
                     writes=[("ps", bo)])
                  mm([(ps[bd][:, :], onesAB[:, half, :], a_pT[buf][half * 2 + kb], idx == 0, idx == len(seq) - 1)
                      for idx, (half, kb) in enumerate(seq)],
                     reads=["onesAB"] + [("a_pT", buf, half * 2 + kb) for half, kb in seq], writes=[("ps", bd)])
                  sk = sinkexp[:, kvh * 4:(kvh + 1) * 4].unsqueeze(2).to_broadcast([128, 4, 128])
                  t1v = a_t1.rearrange("p (a b) -> p a b", a=4)
                  S.op("dve", lambda e, bd=bd, sk=sk, t1v=t1v: e.tensor_tensor(
                      out=t1v, in0=ps[bd][:, :].rearrange("p (a b) -> p a b", a=4), in1=sk, op=ALU.add),
                      reads=[("ps", bd), "sinkexp"], writes=ar("a_t1"))
                  S.op("dve", lambda e: e.reciprocal(out=a_t1, in_=a_t1), reads=["a_t1"], writes=["a_t1"])
                  S.op("dve", lambda e, bo=bo: e.tensor_tensor(out=a_t2, in0=ps[bo][:, :], in1=a_t1, op=ALU.mult),
                       reads=[("ps", bo), "a_t1"], writes=ar("a_t2"))
                  S.op("dve", lambda e, n=n, kvh=kvh: e.tensor_tensor(
                      out=mT[:, kvh * 4:(kvh + 1) * 4, n * 128:(n + 1) * 128],
                      in0=a_t2.rearrange("p (a b) -> p a b", a=4), in1=a_sga[:, :, n * 128:(n + 1) * 128], op=ALU.mult),
                      reads=["a_t2"] + gres, writes=[("mT", kvh, n)])
          ckpt('C')
          S.op("act", lambda e: e.copy(out=kT[:, :, 0:128], in_=kT[:, :, TT:TT + 128]),
               reads=[("kT", k_, NT) for k_ in range(4)], writes=[("kT", k_, 0) for k_ in range(4)])
          S.op("act", lambda e: e.copy(out=vAB[:, :, 0, :, :], in_=vAB[:, :, NT, :, :]),
               reads=[("v", k_, NT) for k_ in range(4)], writes=[("v", k_, 0) for k_ in range(4)])
          barrier()

          ckpt('C2')
          for h in range(4):
              slot = load_w([((lambda w_: w_[:, :, 0:256]), w_in_v[:, :, O_GQ + h * 256:O_GQ + (h + 1) * 256]),
                             ((lambda w_: w_[:, :, 256:512]), w_in_v[:, :, O_GK + h * 256:O_GK + (h + 1) * 256])])
              for dk in range(2):
                  def evq(b, dk=dk):
                      copy_ev(g_qT[:, dk, :], ps[b][:, :], reads=[("ps", b)], writes=ar(("g_qT", dk)), scale=1.0 / 16.0)
                  proj_feat(slot, dk * 128, 128, evq)

                  def evk(b, dk=dk):
                      copy_ev(g_kT[:, dk, :], ps[b][:, :], reads=[("ps", b)], writes=ar(("g_kT", dk)))
                  proj_feat(slot, 256 + dk * 128, 128, evk)
              for i in range(NT):
                  def evkt(b, i=i):
                      copy_ev(g_ktok[:, i, :], ps[b][:, 0:256], reads=[("ps", b)], writes=ar(("g_ktok", i)))
                  proj_tok(slot, 256, 256, i, evkt)
              slot = load_w([((lambda w_: w_[:, :, :]), w_in_v[:, :, O_GV + h * 512:O_GV + (h + 1) * 512])])
              for i in range(NT):
                  def evv(b, i=i):
                      copy_ev(g_v[:, i, :], ps[b][:, :], reads=[("ps", b)], writes=ar(("g_v", i)))
                  proj_tok(slot, 0, 512, i, evv)
              slot = load_w([((lambda w_: w_[:, :, :]), w_in_v[:, :, O_GR + h * 512:O_GR + (h + 1) * 512])])
              for i in range(NT):
                  def evr(b, i=i):
                      S.op("act", lambda e: e.activation(out=g_rs[:, i, :], in_=ps[b][:, :], func=AF.Silu),
                           reads=[("ps", b)], writes=ar(("g_rs", i)))
                  proj_tok(slot, 0, 512, i, evr)
              slot = load_w([((lambda w_: w_[:, :, :]), w_in_v[:, :, O_GB + h * 512:O_GB + (h + 1) * 512])])
              for i in range(NT):
                  def evb(b, i=i):
                      S.op("act", lambda e: e.activation(out=g_bs[:, i, :], in_=ps[b][:, :], func=AF.Sigmoid),
                           reads=[("ps", b)], writes=ar(("g_bs", i)))
                      S.op("dve", lambda e: e.tensor_tensor(out=g_rs[:, i, :], in0=g_rs[:, i, :], in1=g_bs[:, i, :],
                                                            op=ALU.mult),
                           reads=[("g_rs", i), ("g_bs", i)], writes=[("g_rs", i)])
                      S.op("dve", lambda e: e.tensor_tensor(out=g_rs[:, i, :], in0=g_rs[:, i, :], in1=gnw_sb[:],
                                                            op=ALU.mult),
                           reads=[("g_rs", i), "gnw"], writes=[("g_rs", i)])
                  proj_tok(slot, 0, 512, i, evb)
              for i in range(NT):
                  b = nps()
                  mm([(ps[b][:, 0:256], glrT[0:17, i * 128:(i + 1) * 128], w2b[0:17, h * 256:(h + 1) * 256], True, True)],
                     reads=["glrT", "w2b"], writes=[("ps", b)])
                  S.op("act", lambda e, b=b, i=i: e.activation(out=g_la[:, i, :], in_=ps[b][:, 0:256], func=AF.Exp,
                                                               scale=-1.0),
                       reads=[("ps", b)], writes=ar(("g_la", i)))
                  S.op("act", lambda e, i=i: e.activation(out=g_la[:, i, :], in_=g_la[:, i, :], func=AF.Ln, bias=1.0),
                       reads=[("g_la", i)], writes=[("g_la", i)])
              la_all = [("g_la", i) for i in range(NT)]
              for dk in range(2):
                  b = nps()
                  mm([(ps[b][:, i * 128:(i + 1) * 128], g_la[:, i, dk * 128:(dk + 1) * 128], Uc, True, True)
                      for i in range(NT)], reads=la_all + ["c_f32"], writes=[("ps", b)])
                  S.op("act", lambda e, b=b, dk=dk: e.activation(out=g_eg[:, dk, :], in_=ps[b][:, :], func=AF.Exp),
                       reads=[("ps", b)], writes=ar(("g_eg", dk)))
                  S.op("act", lambda e, b=b, dk=dk: e.activation(out=g_ei[:, dk, :], in_=ps[b][:, :], func=AF.Exp,
                                                                 scale=-1.0),
                       reads=[("ps", b)], writes=ar(("g_ei", dk)))
                  S.op("dve", lambda e, dk=dk: e.tensor_tensor(out=g_qT[:, dk, :], in0=g_qT[:, dk, :],
                                                               in1=g_eg[:, dk, :], op=ALU.mult),
                       reads=[("g_qT", dk), ("g_eg", dk)], writes=[("g_qT", dk)])
                  S.op("dve", lambda e, dk=dk: e.tensor_tensor(out=g_kT[:, dk, :], in0=g_kT[:, dk, :],
                                                               in1=g_ei[:, dk, :], op=ALU.mult),
                       reads=[("g_kT", dk), ("g_ei", dk)], writes=[("g_kT", dk)])
              for i2 in range(NT // 2):
                  b = nps()
                  mm([(ps[b][:, j * 256:(j + 1) * 256], Usuf, g_la[:, i2 * 2 + j, :], True, True) for j in range(2)],
                     reads=la_all + ["c_f32"], writes=[("ps", b)])
                  S.op("act", lambda e, b=b, i2=i2: e.activation(
                      out=g_es[:, i2 * 2:i2 * 2 + 2, :], in_=ps[b][:, :].rearrange("p (a b) -> p a b", a=2), func=AF.Exp),
                      reads=[("ps", b)], writes=ar(("g_es", i2)))
                  S.op("dve", lambda e, i2=i2: e.tensor_tensor(out=g_ktok[:, i2 * 2:i2 * 2 + 2, :],
                                                               in0=g_ktok[:, i2 * 2:i2 * 2 + 2, :],
                                                               in1=g_es[:, i2 * 2:i2 * 2 + 2, :], op=ALU.mult),
                       reads=[("g_ktok", i2 * 2), ("g_ktok", i2 * 2 + 1), ("g_es", i2)],
                       writes=[("g_ktok", i2 * 2), ("g_ktok", i2 * 2 + 1)])
              b = nps()
              mm([(ps[b][:, i * 128:(i + 1) * 128], g_kT[:, dk, i * 128:(i + 1) * 128],
                   g_qT[:, dk, i * 128:(i + 1) * 128], dk == 0, dk == 1) for i in range(NT) for dk in range(2)],
                 reads=[("g_qT", 0), ("g_qT", 1), ("g_kT", 0), ("g_kT", 1)], writes=[("ps", b)])
              trb = tril.unsqueeze(1).to_broadcast([128, NT, 128])
              S.op("dve", lambda e, b=b, trb=trb: e.tensor_tensor(
                  out=g_attm[:, :, :], in0=ps[b][:, :].rearrange("p (a b) -> p a b", a=NT), in1=trb, op=ALU.mult),
                  reads=[("ps", b), "c_f32"], writes=ar("g_attm"))
              for i in range(NT):
                  bo = nps()
                  mm([(ps[bo][:, :], g_attm[:, i, :], g_v[:, i, :], True, False),
                      (ps[bo][:, :], g_qT[:, 0, i * 128:(i + 1) * 128], Sbf[:, h, 0, :], False, False),
                      (ps[bo][:, :], g_qT[:, 1, i * 128:(i + 1) * 128], Sbf[:, h, 1, :], False, True)],
                     reads=["g_attm", ("g_v", i), ("g_qT", 0), ("g_qT", 1), ("Sb", h, 0), ("Sb", h, 1)],
                     writes=[("ps", bo)])
                  S.op("act", lambda e, bo=bo, i=i: e.activation(out=g_G[:, i, :], in_=ps[bo][:, :], func=AF.Square,
                                                                 accum_out=stat[:, 4:5]),
                       reads=[("ps", bo)], writes=ar(("g_G", i)) + ["st4"])
                  S.op("act", lambda e: e.activation(out=stat[:, 5:6], in_=stat[:, 4:5], func=AF.Sqrt, scale=1.0 / 512,
                                                     bias=epsc[:]), reads=["st4", "epsc"], writes=["st5"])
                  S.op("dve", lambda e: e.reciprocal(out=stat[:, 6:7], in_=stat[:, 5:6]), reads=["st5"], writes=["st6"])
                  S.op("dve", lambda e, bo=bo, i=i: e.scalar_tensor_tensor(
                      out=g_G[:, i, :], in0=ps[bo][:, :], scalar=stat[:, 6:7], in1=g_rs[:, i, :], op0=ALU.mult,
                      op1=ALU.mult), reads=[("ps", bo), "st6", ("g_rs", i)], writes=[("g_G", i)])
                  half = st["tr"] % 2
                  st["tr"] += 1

                  def trg(e, i=i, half=half):
                      ins = None
                      for j in range(4):
                          ins = e.transpose(out=ptrs[half][:, j, :], in_=g_G[:, i, j * 128:(j + 1) * 128],
                                            identity=identb[:])
                      return ins
                  S.op("pe", trg, reads=[("g_G", i), "identb"], writes=[("ptr", half)])
                  S.op("dve", lambda e, i=i, half=half, h=h: e.tensor_tensor(
                      out=mT[:, h * 4:(h + 1) * 4, i * 128:(i + 1) * 128], in0=ptrs[half][:, 0:4, :],
                      in1=mT[:, h * 4:(h + 1) * 4, i * 128:(i + 1) * 128], op=ALU.add),
                      reads=[("ptr", half), ("mT", h, i)], writes=[("mT", h, i)])
                  for dk in range(2):
                      bu = nps()
                      mm([(ps[bu][:, :], g_ktok[:, i, dk * 128:(dk + 1) * 128], g_v[:, i, :], True, True)],
                         reads=[("g_ktok", i), ("g_v", i)], writes=[("ps", bu)])
                      S.op("dve", lambda e, bu=bu, dk=dk, i=i, h=h: e.scalar_tensor_tensor(
                          out=Sst[:, h, dk, :], in0=Sst[:, h, dk, :], scalar=g_eg[:, dk, i * 128 + 127:i * 128 + 128],
                          in1=ps[bu][:, :], op0=ALU.mult, op1=ALU.add),
                          reads=[("ps", bu), ("S", h, dk), ("g_eg", dk)], writes=[("S", h, dk)])
                      S.op("act", lambda e, dk=dk, h=h: e.copy(out=Sbf[:, h, dk, :], in_=Sst[:, h, dk, :]),
                           reads=[("S", h, dk)], writes=[("Sb", h, dk)])
          barrier()
          if debug and c == 0:
              S.op("pool", lambda e: e.dma_start(out=dbg["mT"], in_=mT[:]),
                   reads=[("mT", g_, i_) for g_ in range(4) for i_ in range(NT)], writes=["dbg_mT"], dma=True)

          ckpt('D')
          for g in range(4):
              slot = load_w([((lambda w_: w_[:, :, :]), w_out_v[:, :, g * 512:(g + 1) * 512])])
              for i in range(NT):
                  b = nps()
                  mm([(ps[b][:, :], mT[:, k, i * 128:(i + 1) * 128], wsl[slot][:, k, :], k == 0, k == KC - 1)
                      for k in range(KC)], reads=wres(slot) + [("mT", g_, i) for g_ in range(4)], writes=[("ps", b)])
                  S.op("dve", lambda e, b=b, i=i, g=g: e.tensor_tensor(
                      out=xt[:, i, g * 512:(g + 1) * 512], in0=ps[b][:, :], in1=xt[:, i, g * 512:(g + 1) * 512],
                      op=ALU.add), reads=[("ps", b), ("x", i)], writes=[("x", i)])
          if debug and c == 0:
              S.op("sp", lambda e: e.dma_start(out=dbg["h1"], in_=xt[:]), reads=[("x", i_) for i_ in range(NT)],
                   writes=["dbg_h1"], dma=True)
          for i in range(NT):
              rms_to_actT(i, n2_sb, "n2")

          ckpt('E')
          for j in range(HC // 2):
              slot = load_w([((lambda w_: w_[:, :, 0:256]), w_g_v[:, :, j * 256:(j + 1) * 256]),
                             ((lambda w_: w_[:, :, 256:512]), w_u_v[:, :, j * 256:(j + 1) * 256])])
              for sub in range(2):
                  hc = j * 2 + sub
                  bg = nps()
                  mm([(ps[bg][:, :], wsl[slot][:, k, sub * 128:(sub + 1) * 128], actT[:, k, :], k == 0, k == KC - 1)
                      for k in range(KC)], reads=wres(slot) + actT_all, writes=[("ps", bg)])
                  bu = nps()
                  mm([(ps[bu][:, :], wsl[slot][:, k, 256 + sub * 128:256 + (sub + 1) * 128], actT[:, k, :], k == 0,
                       k == KC - 1) for k in range(KC)], reads=wres(slot) + actT_all, writes=[("ps", bu)])
                  S.op("act", lambda e, bg=bg, hc=hc: e.activation(out=ffT[:, hc, :], in_=ps[bg][:, :], func=AF.Silu),
                       reads=[("ps", bg)], writes=ar(("ffT", hc)))
                  S.op("dve", lambda e, bu=bu, hc=hc: e.tensor_tensor(out=ffT[:, hc, :], in0=ps[bu][:, :],
                                                                      in1=ffT[:, hc, :], op=ALU.mult),
                       reads=[("ps", bu), ("ffT", hc)], writes=[("ffT", hc)])

          ckpt('F')
          NPIECE = 4
          PH = HC // NPIECE
          for g in range(4):
              banks = [nps() for _ in range(NT)]
              for p in range(NPIECE):
                  slot = load_w([((lambda w_: w_[:, 0:PH, :]), w_d_v[:, p * PH:(p + 1) * PH, g * 512:(g + 1) * 512])],
                              ring=(0, 1, 2))
                  for i in range(NT):
                      b = banks[i]
                      mm([(ps[b][:, :], ffT[:, p * PH + q, i * 128:(i + 1) * 128], wsl[slot][:, q, :],
                           (p == 0 and q == 0), (p == NPIECE - 1 and q == PH - 1)) for q in range(PH)],
                         reads=wres(slot) + [("ffT", p * PH + q) for q in range(PH)], writes=[("ps", b)])
              for i in range(NT):
                  b = banks[i]
                  S.op("dve", lambda e, b=b, i=i, g=g: e.tensor_tensor(
                      out=xt[:, i, g * 512:(g + 1) * 512], in0=ps[b][:, :], in1=xt[:, i, g * 512:(g + 1) * 512],
                      op=ALU.add), reads=[("ps", b), ("x", i)], writes=[("x", i)])
              if c + 1 < nchunk:
                  norm1_from_dram(c + 1, g)

          ckpt('G')
          for i in range(NT):
              pending.append((c, i))
          if c == nchunk - 1:
              while pending:
                  drain_final()
          barrier()

    except _Stop:
        pass
    S.finalize_on("sp", out_dmas)
    if stage is not None:
        S.finalize_on("sp", [o for o in S.streams["sp"] if o.is_dma])
        S.finalize_on("pool", [o for o in S.streams["pool"] if o.is_dma])
    if debug:
        dd = [o for o in S.streams["pool"] if o.is_dma and o.chan[1] in ("dbg_mT", "dbg_xnT")]
        S.finalize_on("pool", dd)
        dd2 = [o for o in S.streams["sp"] if o.is_dma and o.chan[1] == "dbg_h1"]
        S.finalize_on("sp", dd2)
    S.emit()
    return nc


def _consts():
    j = np.arange(128)[:, None]
    i = np.arange(128)[None, :]
    cst = np.zeros((128, 5, 128), np.float32)
    cst[:, 0, :] = np.eye(128, dtype=np.float32)
    cst[:, 1, :] = np.where(j <= i, -1.0 / 16.0, 0.0)
    cst[:, 2, :] = np.where(j > i, -1.0 / 16.0, 0.0)
    cst[:, 3, :] = np.where(j <= i, 1.0, 0.0)
    mk = np.zeros((128, 2, 512), np.float32)
    mprev = np.where(j > i, 1.0, 0.0).astype(np.float32)
    mcur = np.where(j <= i, 1.0, 0.0).astype(np.float32)
    mk[:, 0, :] = np.tile(mprev, (1, 4))
    mk[:, 1, :] = np.tile(mcur, (1, 4))
    return cst, mk


def _col_layout(v):
    return np.ascontiguousarray(np.asarray(v, np.float32).reshape(KC, 128).T)


def make_in_maps(inputs, ncores=8):
    f = lambda a: np.ascontiguousarray(np.asarray(a, dtype=np.float32))
    x = f(inputs["x"])
    cst, mk = _consts()
    sinks = f(inputs["attn_sinks"])[0]
    sink_l = np.zeros((128, 16), np.float32)
    for cch in range(16):
        sink_l[0:64, cch] = sinks[2 * cch]
        sink_l[64:128, cch] = sinks[2 * cch + 1]
    shared = {
        "w_in": f(inputs["w_in"])[0],
        "w_out": f(inputs["w_out"])[0],
        "w_g": f(inputs["w_ffn_gate"])[0],
        "w_u": f(inputs["w_ffn_up"])[0],
        "w_d": f(inputs["w_ffn_down"])[0],
        "n1": _col_layout(f(inputs["norm1_w"])[0]),
        "n2": _col_layout(f(inputs["norm2_w"])[0]),
        "sink_l": sink_l,
        "fnw": np.ascontiguousarray(np.broadcast_to(f(inputs["final_norm_w"])[None, :], (128, D))),
        "gnw": np.ascontiguousarray(np.broadcast_to(f(inputs["gla_norm_w"])[0][None, :], (128, 512))),
        "w2b": np.ascontiguousarray(np.concatenate([f(inputs["gla_gate_w2"])[0], f(inputs["gla_gate_b"])[0][None, :]],
                                                   axis=0)),
        "cst": cst,
        "mk": mk,
    }
    maps = []
    for b in range(ncores):
        m = dict(shared)
        m["x"] = np.ascontiguousarray(x[b])
        maps.append(m)
    return maps


def kernel(**inputs):
    nc = build_nc()
    in_maps = make_in_maps(inputs, 8)
    res = run_bass_kernel_spmd(nc, in_maps, core_ids=list(range(8)))
    return np.stack([np.asarray(r["out"], dtype=np.float32) for r in res.results], axis=0)
```

```python
import numpy as np
import concourse.bass as bass
import concourse.mybir as mybir
from concourse.bass_utils import run_bass_kernel_spmd

F32 = mybir.dt.float32
BF16 = mybir.dt.bfloat16
AF = mybir.ActivationFunctionType
ALU = mybir.AluOpType

D = 2048
T = 2048
TT = 512
NT = TT // 128
NCHUNK = T // TT
KC = D // 128
FF = 5632
HC = FF // 128
DIN = 12816
O_AQ, O_AK, O_AV, O_GQ, O_GK, O_GV, O_GLR, O_GR, O_GA, O_GB = (
    0, 2048, 2304, 2560, 3584, 4608, 6656, 6672, 8720, 10768)
EPS = 1e-6
NEG = -30000.0
NSLOT = 2
NPS = 6

ENGS = ("pe", "act", "dve", "pool", "sp")


class Op:
    __slots__ = ("eng", "fn", "deps", "chan", "inc", "signal", "count", "is_dma", "glast")


class Sched:
    def __init__(self, nc):
        self.nc = nc
        self.streams = {e: [] for e in ENGS}
        self.last_w = {}
        self.readers = {}
        self.chan_ops = {}
        self.final_waits = []

    @staticmethod
    def _is_arena(r):
        key = r[0] if isinstance(r, tuple) else r
        return isinstance(key, str) and (key.startswith("a_") or key.startswith("g_") or key in ("ffT", "mk_tmp"))

    def op(self, eng, fn, reads=(), writes=(), dma=False, chan=None):
        reads = list(reads)
        writes = list(writes)
        if any(self._is_arena(r) for r in reads) or any(self._is_arena(r) for r in writes):
            if "ARENA" not in writes:
                reads.append("ARENA")
        o = Op()
        o.eng = eng
        o.fn = fn
        o.is_dma = dma
        o.inc = 16 if dma else 1
        if dma:
            o.chan = ("dma", chan if chan is not None else tuple(writes)[0])
        else:
            o.chan = eng
        o.signal = bool(dma)
        o.count = None
        o.glast = None
        deps = []
        for r in reads:
            w = self.last_w.get(r)
            if w is not None:
                deps.append((w, "raw"))
        for r in writes:
            w = self.last_w.get(r)
            if w is not None:
                deps.append((w, "waw"))
            for rd in self.readers.get(r, ()):
                deps.append((rd, "war"))
        keep = []
        seen = set()
        for d, kind in deps:
            if d is o or id(d) in seen:
                continue
            if (not d.is_dma) and (not dma) and d.eng == eng:
                if eng == "pe":
                    continue
            seen.add(id(d))
            keep.append(d)
        o.deps = keep
        wset = set(writes)
        for r in writes:
            self.last_w[r] = o
            self.readers[r] = []
        for r in reads:
            if r in wset:
                continue
            self.readers.setdefault(r, []).append(o)
        self.streams[eng].append(o)
        self.chan_ops.setdefault(o.chan, []).append(o)
        return o

    def finalize_on(self, eng, ops):
        self.final_waits.append((eng, list(ops)))

    def emit(self):
        nc = self.nc
        for e in ENGS:
            for o in self.streams[e]:
                for d in o.deps:
                    d.signal = True
        for eng, ops in self.final_waits:
            for d in ops:
                d.signal = True
        sems = {}
        nsem = 0
        for chan, ops in self.chan_ops.items():
            c = 0
            any_sig = False
            for o in ops:
                if o.signal:
                    c += o.inc
                    o.count = c
                    any_sig = True
            if any_sig:
                sems[chan] = nc.alloc_semaphore(name="sm%d" % nsem)
                nsem += 1
        finals = {}
        for eng, ops in self.final_waits:
            finals.setdefault(eng, []).extend(ops)
        handles = {"pe": "tensor", "act": "scalar", "dve": "vector", "pool": "gpsimd", "sp": "sync"}
        with nc.Block() as block:
            for e in ENGS:
                stream = self.streams[e]
                fin = finals.get(e, [])
                if not stream and not fin:
                    continue

                def body(engine, stream=stream, fin=fin):
                    known = {}
                    for o in stream:
                        need = {}
                        for d in o.deps:
                            dc = d.glast.count if d.glast is not None else d.count
                            if dc > need.get(d.chan, 0):
                                need[d.chan] = dc
                        for ch, cnt in need.items():
                            if known.get(ch, 0) >= cnt:
                                continue
                            engine.wait_ge(sems[ch], cnt)
                            known[ch] = cnt
                        ins = o.fn(engine)
                        if o.signal:
                            ins.then_inc(sems[o.chan], o.inc)
                    for d in fin:
                        if known.get(d.chan, 0) >= d.count:
                            continue
                        engine.wait_ge(sems[d.chan], d.count)
                        known[d.chan] = d.count

                getattr(block, handles[e])(body)


class _Stop(Exception):
    pass


def build_nc(nchunk=NCHUNK, debug=False, stage=None):
    nc = bass.Bass("TRN2", target_bir_lowering=False)
    S = Sched(nc)

    def din(name, shape):
        return nc.dram_tensor(name, list(shape), F32, kind="ExternalInput").ap()

    x = din("x", [T, D])
    w_in = din("w_in", [D, DIN])
    w_out = din("w_out", [D, D])
    w_g = din("w_g", [D, FF])
    w_u = din("w_u", [D, FF])
    w_d = din("w_d", [FF, D])
    n1 = din("n1", [128, KC])
    n2 = din("n2", [128, KC])
    sink_l = din("sink_l", [128, 16])
    fnw = din("fnw", [128, D])
    gnw = din("gnw", [128, 512])
    w2b_d = din("w2b", [17, 1024])
    cst = din("cst", [128, 5, 128])
    mk = din("mk", [128, 2, 512])
    out = nc.dram_tensor("out", [T, D], F32, kind="ExternalOutput").ap()
    dbg = {}
    if debug:
        dbg["mT"] = nc.dram_tensor("dbg_mT", [128, KC, TT], F32, kind="ExternalOutput").ap()
        dbg["h1"] = nc.dram_tensor("dbg_h1", [128, NT, D], F32, kind="ExternalOutput").ap()
        dbg["xnT"] = nc.dram_tensor("dbg_xnT", [128, KC, TT], F32, kind="ExternalOutput").ap()

    w_in_v = w_in.rearrange("(k p) c -> p k c", p=128)
    w_out_v = w_out.rearrange("(k p) c -> p k c", p=128)
    w_g_v = w_g.rearrange("(k p) c -> p k c", p=128)
    w_u_v = w_u.rearrange("(k p) c -> p k c", p=128)
    w_d_v = w_d.rearrange("(k p) c -> p k c", p=128)

    sb = nc.alloc_sbuf_tensor
    xt = sb("xt", [128, NT, D], F32)
    xn = sb("xn", [128, D], BF16)
    actT = sb("actT", [128, KC, TT], BF16)
    wsl = [sb("wsl%d" % i, [128, KC, 512], BF16) for i in range(NSLOT)]
    mT = sb("mT", [128, KC, TT], BF16)
    kT = sb("kT", [128, 4, 128 + TT], BF16)
    vAB = sb("vAB", [128, 4, NT + 1, 2, 128], BF16)
    Sst = sb("Sst", [128, 4, 2, 512], F32)
    Sbf = sb("Sbf", [128, 4, 2, 512], BF16)
    glrT = sb("glrT", [17, TT], F32)
    c_f32 = sb("c_f32", [128, 5, 128], F32)
    identb = sb("identb", [128, 128], BF16)
    maskb = sb("maskb", [128, 2, 512], BF16)
    onesAB = sb("onesAB", [128, 2, 128], BF16)
    sinkexp = sb("sinkexp", [128, 16], F32)
    fnw_sb = sb("fnw_sb", [128, D], F32)
    gnw_sb = sb("gnw_sb", [128, 512], F32)
    w2b = sb("w2b_sb", [17, 1024], F32)
    n1_sb = sb("n1_sb", [128, KC], F32)
    n2_sb = sb("n2_sb", [128, KC], F32)
    epsc = sb("epsc", [128, 1], F32)
    stat = sb("stat", [128, 8], F32)
    bar = sb("bar", [128, 1], F32)
    ffT = sb("ffT", [128, HC, TT], BF16)
    ARENA = HC * TT * 2
    off = [0]
    ffT_off = None

    def carve(name, shape, dt, base):
        n = 1
        for s_ in shape[1:]:
            n *= s_
        nbytes = n * (4 if dt == F32 else 2)
        o_ = base[0]
        base[0] += (nbytes + 63) // 64 * 64
        return (name, shape, dt, o_, nbytes)

    def bview(col0, shape):
        n = 1
        for s_ in shape[1:]:
            n *= s_
        flat = ffT[:, :, :].rearrange("p a b -> p (a b)")[:, col0:col0 + n]
        if len(shape) == 2:
            return flat
        if len(shape) == 3:
            return flat.rearrange("p (a b) -> p a b", a=shape[1])
        raise ValueError

    def fview(col0, shape):
        n = 1
        for s_ in shape[1:]:
            n *= s_
        flat = ffT[:, :, :].rearrange("p a b -> p (a b)")[:, col0:col0 + 2 * n].bitcast(F32)
        if len(shape) == 2:
            return flat
        if len(shape) == 3:
            return flat.rearrange("p (a b) -> p a b", a=shape[1])
        raise ValueError

    a_qT = bview(0, [128, 4, TT])
    a_sga = bview(2048, [128, 4, TT])
    a_pT = [[bview(4096 + (b * 4 + j) * 512, [128, 512]) for j in range(4)] for b in range(2)]
    a_t1 = fview(8192, [128, 512])
    a_t2 = fview(9216, [128, 512])
    g_qT = bview(0, [128, 2, TT])
    g_kT = bview(1024, [128, 2, TT])
    g_ktok = bview(2048, [128, NT, 256])
    g_v = bview(3072, [128, NT, 512])
    g_rs = bview(5120, [128, NT, 512])
    g_bs = bview(7168, [128, NT, 512])
    g_attm = bview(9216, [128, NT, 128])
    g_G = bview(9728, [128, NT, 512])
    g_la = fview(11776, [128, NT, 256])
    g_eg = fview(13824, [128, 2, TT])
    g_ei = fview(15872, [128, 2, TT])
    g_es = fview(17920, [128, NT, 256])

    ps = [nc.alloc_psum_tensor("ps%d" % i, [128, 512], F32) for i in range(NPS)]
    ptrs = [nc.alloc_psum_tensor("ptr%d" % i, [128, 8, 128], BF16) for i in range(2)]
    st = {"ps": 0, "w": 0, "ev": 0, "tr": 0}

    def nps():
        b = st["ps"] % NPS
        st["ps"] += 1
        return b

    WSUB = 8
    wsl.append(mT)
    MT_ALL = [("mT", g_, i_) for g_ in range(4) for i_ in range(NT)]

    def wres(slot):
        if slot == 2:
            return MT_ALL
        return [("w", slot, q) for q in range(WSUB)]

    def load_w(parts, ring=(0, 1)):
        slot = ring[st["w"] % len(ring)]
        st["w"] += 1
        P = len(parts)
        ops = []
        alln = wres(slot)
        for i, (dst_fn, src) in enumerate(parts):
            names = [alln[q] for q in range(len(alln)) if q % P == i]
            ops.append(S.op("pool", (lambda e, dst_fn=dst_fn, src=src, slot=slot: e.dma_start(out=dst_fn(wsl[slot]),
                                                                                           in_=src)),
                            writes=names, dma=True, chan=("w", slot)))
        for o_ in ops:
            o_.glast = ops[-1]
        return slot

    def mm(mms, reads, writes):
        def fn(e):
            ins = None
            for (o_, l_, r_, s0, s1) in mms:
                ins = e.matmul(o_, lhsT=l_, rhs=r_, start=s0, stop=s1)
            return ins
        return S.op("pe", fn, reads=reads, writes=writes)

    def ev_engine():
        st["ev"] += 1
        return "act" if st["ev"] % 2 == 0 else "dve"

    def copy_ev(out_ap, in_ap, reads, writes, scale=None, eng=None):
        eng = eng or ev_engine()
        if eng == "act":
            if scale is None:
                S.op("act", lambda e: e.copy(out=out_ap, in_=in_ap), reads=reads, writes=writes)
            else:
                S.op("act", lambda e: e.activation(out=out_ap, in_=in_ap, func=AF.Copy, scale=scale),
                     reads=reads, writes=writes)
        else:
            if scale is None:
                S.op("dve", lambda e: e.tensor_copy(out=out_ap, in_=in_ap), reads=reads, writes=writes)
            else:
                S.op("dve", lambda e: e.tensor_scalar(out=out_ap, in0=in_ap, scalar1=scale, scalar2=None,
                                                        op0=ALU.mult), reads=reads, writes=writes)

    arena_names = set()

    def ar(*names):
        for n_ in names:
            arena_names.add(n_)
        return list(names)

    def barrier():
        S.op("dve", lambda e: e.memset(bar[:], 0.0), writes=["ARENA", "bar"])

    S.op("sp", lambda e: e.dma_start(out=c_f32[:], in_=cst), writes=["c_f32"], dma=True)
    S.op("sp", lambda e: e.dma_start(out=n1_sb[:], in_=n1), writes=["n1"], dma=True)
    S.op("sp", lambda e: e.dma_start(out=n2_sb[:], in_=n2), writes=["n2"], dma=True)
    S.op("sp", lambda e: e.dma_start(out=sinkexp[:], in_=sink_l), writes=["sinkexp"], dma=True)
    S.op("sp", lambda e: e.dma_start(out=fnw_sb[:], in_=fnw), writes=["fnw"], dma=True)
    S.op("sp", lambda e: e.dma_start(out=gnw_sb[:], in_=gnw), writes=["gnw"], dma=True)
    S.op("sp", lambda e: e.dma_start(out=w2b[:], in_=w2b_d), writes=["w2b"], dma=True)
    mk_tmp = fview(0, [128, 2, 512])
    S.op("sp", lambda e: e.dma_start(out=mk_tmp, in_=mk), writes=ar("mk_tmp"), dma=True)
    S.op("dve", lambda e: e.tensor_copy(out=maskb[:], in_=mk_tmp), reads=["mk_tmp"], writes=["maskb"])
    S.op("dve", lambda e: e.tensor_copy(out=identb[:], in_=c_f32[:, 0, :]), reads=["c_f32"], writes=["identb"])
    S.op("act", lambda e: e.activation(out=sinkexp[:], in_=sinkexp[:], func=AF.Exp), reads=["sinkexp"],
         writes=["sinkexp"])
    S.op("dve", lambda e: e.memset(epsc[:], EPS), writes=["epsc"])
    S.op("dve", lambda e: e.memset(onesAB[:].rearrange("p a b -> p (a b)"), 0.0), writes=["onesAB"])
    S.op("dve", lambda e: e.memset(onesAB[:, 0, 0:64], 1.0), writes=["onesAB"])
    S.op("dve", lambda e: e.memset(onesAB[:, 1, 64:128], 1.0), writes=["onesAB"])
    S.op("dve", lambda e: e.memset(vAB[:].rearrange("p a b c d -> p (a b c d)"), 0.0), writes=[("v", k_, t_) for k_ in range(4) for t_ in range(NT + 1)])
    S.op("dve", lambda e: e.memset(kT[:].rearrange("p a b -> p (a b)"), 0.0), writes=[("kT", k_, t_) for k_ in range(4) for t_ in range(NT + 1)])
    S.op("dve", lambda e: e.memset(Sst[:].rearrange("p a b c -> p (a b c)"), 0.0), writes=[("S", h_, d_) for h_ in range(4) for d_ in range(2)])
    S.op("dve", lambda e: e.memset(Sbf[:].rearrange("p a b c -> p (a b c)"), 0.0), writes=[("Sb", h_, d_) for h_ in range(4) for d_ in range(2)])
    S.op("dve", lambda e: e.memset(glrT[:], 1.0), writes=["glrT"])
    barrier()

    identf = c_f32[:, 0, :]
    Uc = c_f32[:, 1, :]
    Usuf = c_f32[:, 2, :]
    tril = c_f32[:, 3, :]

    def transposes_to_actT(i, nw_sb, nwname):
        for g in range(4):
            half = st["tr"] % 2
            st["tr"] += 1

            def tr(e, g=g, half=half):
                ins = None
                for j in range(4):
                    k = g * 4 + j
                    ins = e.transpose(out=ptrs[half][:, j, :], in_=xn[:, k * 128:(k + 1) * 128], identity=identb[:])
                return ins
            S.op("pe", tr, reads=["xn", "identb"], writes=[("ptr", half)])
            for j in range(4):
                k = g * 4 + j
                eng = ev_engine()
                o_ap = actT[:, k, i * 128:(i + 1) * 128]
                i_ap = ptrs[half][:, j, :]
                sc = nw_sb[:, k:k + 1]
                if eng == "act":
                    S.op("act", lambda e, o_ap=o_ap, i_ap=i_ap, sc=sc: e.activation(out=o_ap, in_=i_ap, func=AF.Copy,
                                                                                     scale=sc),
                         reads=[("ptr", half), nwname], writes=[("actT", i)])
                else:
                    S.op("dve", lambda e, o_ap=o_ap, i_ap=i_ap, sc=sc: e.tensor_scalar(out=o_ap, in0=i_ap, scalar1=sc,
                                                                                        scalar2=None, op0=ALU.mult),
                         reads=[("ptr", half), nwname], writes=[("actT", i)])

    def rstd_from(ss_col, out_col, n):
        S.op("act", lambda e: e.activation(out=stat[:, 1:2], in_=ss_col, func=AF.Sqrt, scale=1.0 / n,
                                           bias=epsc[:]), reads=["st0", "epsc"], writes=["st1"])
        S.op("dve", lambda e: e.reciprocal(out=out_col, in_=stat[:, 1:2]), reads=["st1"], writes=["st2"])

    def rms_to_actT(i, nw_sb, nwname):
        S.op("act", lambda e: e.activation(out=xn[:], in_=xt[:, i, :], func=AF.Square, accum_out=stat[:, 0:1]),
             reads=[("x", i)], writes=["xn", "st0"])
        rstd_from(stat[:, 0:1], stat[:, 2:3], D)
        S.op("act", lambda e: e.activation(out=xn[:], in_=xt[:, i, :], func=AF.Copy, scale=stat[:, 2:3]),
             reads=[("x", i), "st2"], writes=["xn"])
        transposes_to_actT(i, nw_sb, nwname)

    def norm1_from_dram(cc, i):
        r0 = cc * TT + i * 128
        S.op("pool", lambda e: e.dma_start(out=xn[:], in_=x[r0:r0 + 128, :]), writes=["xn"], dma=True, chan="xn")
        S.op("act", lambda e: e.activation(out=actT[:, :, i * 128:(i + 1) * 128],
                                           in_=xn[:].rearrange("p (a b) -> p a b", a=KC), func=AF.Square,
                                           accum_out=stat[:, 0:1]),
             reads=["xn"], writes=[("actT", i), "st0"])
        rstd_from(stat[:, 0:1], stat[:, 2:3], D)
        S.op("act", lambda e: e.activation(out=xn[:], in_=xn[:], func=AF.Copy, scale=stat[:, 2:3]),
             reads=["xn", "st2"], writes=["xn"])
        transposes_to_actT(i, n1_sb, "n1")

    actT_all = [("actT", i) for i in range(NT)]

    def proj_feat(slot, col0, ncols, evac):
        b = nps()
        mm([(ps[b][0:ncols, :], wsl[slot][:, k, col0:col0 + ncols], actT[:, k, :], k == 0, k == KC - 1)
            for k in range(KC)], reads=wres(slot) + actT_all, writes=[("ps", b)])
        evac(b)

    def proj_tok(slot, col0, ncols, i, evac):
        b = nps()
        mm([(ps[b][:, 0:ncols], actT[:, k, i * 128:(i + 1) * 128], wsl[slot][:, k, col0:col0 + ncols], k == 0,
             k == KC - 1) for k in range(KC)], reads=wres(slot) + [("actT", i)], writes=[("ps", b)])
        evac(b)

    pending = []
    out_dmas = []

    def drain_final():
        if not pending:
            return
        cc, i = pending.pop(0)
        r0 = cc * TT + i * 128
        S.op("act", lambda e: e.activation(out=xn[:], in_=xt[:, i, :], func=AF.Square, accum_out=stat[:, 0:1]),
             reads=[("x", i)], writes=["xn", "st0"])
        rstd_from(stat[:, 0:1], stat[:, 2:3], D)
        S.op("act", lambda e: e.activation(out=xt[:, i, :], in_=xt[:, i, :], func=AF.Copy, scale=stat[:, 2:3]),
             reads=[("x", i), "st2"], writes=[("x", i)])
        S.op("dve", lambda e: e.tensor_tensor(out=xt[:, i, :], in0=xt[:, i, :], in1=fnw_sb[:], op=ALU.mult),
             reads=[("x", i), "fnw"], writes=[("x", i)])
        od = S.op("sp", lambda e: e.dma_start(out=out[r0:r0 + 128, :], in_=xt[:, i, :]),
                  reads=[("x", i)], writes=[("out", cc, i)], dma=True, chan=("out", i))
        out_dmas.append(od)

    def ckpt(name):
        if stage == name:
            raise _Stop()

    try:
      for c in range(nchunk):
          t0 = c * TT
          if c == 0:
              for i in range(NT):
                  norm1_from_dram(0, i)
          if debug and c == 0:
              S.op("pool", lambda e: e.dma_start(out=dbg["xnT"], in_=actT[:]), reads=actT_all, writes=["dbg_xnT"],
                   dma=True)

          ckpt('A')
          kparts = []
          for kvh in range(4):
              for dup in range(2):
                  kparts.append((
                      (lambda w_, kvh=kvh, dup=dup: w_[:, :, kvh * 128 + dup * 64:kvh * 128 + dup * 64 + 64]),
                      w_in_v[:, :, O_AK + kvh * 64:O_AK + (kvh + 1) * 64]))
          slot = load_w(kparts)
          for kvh in range(4):
              def evk(b, kvh=kvh):
                  copy_ev(kT[:, kvh, 128:128 + TT], ps[b][:, :], reads=[("ps", b)],
                          writes=[("kT", kvh, t_) for t_ in range(1, NT + 1)])
              proj_feat(slot, kvh * 128, 128, evk)
              drain_final()
          slot = load_w([((lambda w_: w_[:, :, 0:256]), w_in_v[:, :, O_AV:O_AV + 256]),
                         ((lambda w_: w_[:, :, 256:272]), w_in_v[:, :, O_GLR:O_GLR + 16])])
          for i in range(NT):
              def evv(b, i=i):
                  src = ps[b][:, 0:256].rearrange("p (a b) -> p a b", a=4)
                  wr = [("v", k_, 1 + i) for k_ in range(4)]
                  S.op("act", lambda e: e.copy(out=vAB[:, :, 1 + i, 0, 0:64], in_=src), reads=[("ps", b)], writes=wr)
                  S.op("dve", lambda e: e.tensor_copy(out=vAB[:, :, 1 + i, 1, 64:128], in_=src), reads=[("ps", b)],
                       writes=wr)
              proj_tok(slot, 0, 256, i, evv)

          def evg(b):
              copy_ev(glrT[0:16, :], ps[b][0:16, :], reads=[("ps", b)], writes=["glrT"])
          proj_feat(slot, 256, 16, evg)
          while pending:
              drain_final()
          for i in range(NT):
              S.op("sp", lambda e, i=i, t0=t0: e.dma_start(out=xt[:, i, :], in_=x[t0 + i * 128:t0 + (i + 1) * 128, :]),
                   writes=[("x", i)], dma=True, chan=("x", i))

          ckpt('B')
          for kvh in range(4):
              slot = load_w([((lambda w_: w_[:, :, :]), w_in_v[:, :, O_AQ + kvh * 512:O_AQ + (kvh + 1) * 512])])
              for pr in range(4):
                  def evq(b, pr=pr):
                      copy_ev(a_qT[:, pr, :], ps[b][:, :], reads=[("ps", b)], writes=ar(("a_qT", pr)))
                  proj_feat(slot, pr * 128, 128, evq)
              slot = load_w([((lambda w_: w_[:, :, :]), w_in_v[:, :, O_GA + kvh * 512:O_GA + (kvh + 1) * 512])])
              for pr in range(4):
                  def evga(b, pr=pr):
                      S.op("act", lambda e: e.activation(out=a_sga[:, pr, :], in_=ps[b][:, :], func=AF.Sigmoid),
                           reads=[("ps", b)], writes=ar(("a_sga", pr)))
                  proj_feat(slot, pr * 128, 128, evga)
              qres = [("a_qT", pr) for pr in range(4)]
              gres = [("a_sga", pr) for pr in range(4)]
              for n in range(NT):
                  gn = c * NT + n
                  kbs = ([0] if gn > 0 else []) + [1]
                  buf = n % 2
                  for half in range(2):
                      hs = slice(half * 64, half * 64 + 64)
                      for kb in kbs:
                          kt_idx = n + kb
                          b = nps()
                          mm([(ps[b][:, :], kT[hs, kvh, kt_idx * 128:(kt_idx + 1) * 128],
                               a_qT[hs, :, n * 128:(n + 1) * 128], True, False),
                              (ps[b][:, :], identb[:], maskb[:, kb, :], False, True)],
                             reads=[("kT", kvh, kt_idx), "identb", "maskb"] + qres, writes=[("ps", b)])
                          pt = a_pT[buf][half * 2 + kb]
                          S.op("act", lambda e, pt=pt, b=b: e.activation(out=pt, in_=ps[b][:, :], func=AF.Exp,
                                                                          scale=0.125),
                               reads=[("ps", b)], writes=ar(("a_pT", buf, half * 2 + kb)))
                  bo = nps()
                  bd = nps()
                  seq = [(half, kb) for half in range(2) for kb in kbs]
                  mm([(ps[bo][:, :], vAB[:, kvh, n + kb, half, :], a_pT[buf][half * 2 + kb], idx == 0,
                       idx == len(seq) - 1) for idx, (half, kb) in enumerate(seq)],
                     reads=[("v", kvh, n + kb) for kb in kbs] + [("a_pT", buf, half * 2 + kb) for half, kb in seq],
                     writes=[("ps", bo)])
                  mm([(ps[bd][:, :], onesAB[:, half, :], a_pT[buf][half * 2 + kb], idx == 0, idx == len(seq) - 1)
                      for idx, (half, kb) in enumerate(seq)],
                     reads=["onesAB"] + [("a_pT", buf, half * 2 + kb) for half, kb in seq], writes=[("ps", bd)])
                  sk = sinkexp[:, kvh * 4:(kvh + 1) * 4].unsqueeze(2).to_broadcast([128, 4, 128])
                  t1v = a_t1.rearrange("p (a b) -> p a b", a=4)
                  S.op("dve", lambda e, bd=bd, sk=sk, t1v=t1v: e.tensor_tensor(
                      out=t1v, in0=ps[bd][:, :].rearrange("p (a b) -> p a b", a=4), in1=sk, op=ALU.add),
                      reads=[("ps", bd), "sinkexp"], writes=ar("a_t1"))
                  S.op("dve", lambda e: e.reciprocal(out=a_t1, in_=a_t1), reads=["a_t1"], writes=["a_t1"])
                  S.op("dve", lambda e, bo=bo: e.tensor_tensor(out=a_t2, in0=ps[bo][:, :], in1=a_t1, op=ALU.mult),
                       reads=[("ps", bo), "a_t1"], writes=ar("a_t2"))
                  S.op("dve", lambda e, n=n, kvh=kvh: e.tensor_tensor(
                      out=mT[:, kvh * 4:(kvh + 1) * 4, n * 128:(n + 1) * 128],
                      in0=a_t2.rearrange("p (a b) -> p a b", a=4), in1=a_sga[:, :, n * 128:(n + 1) * 128], op=ALU.mult),
                      reads=["a_t2"] + gres, writes=[("mT", kvh, n)])
          ckpt('C')
          S.op("act", lambda e: e.copy(out=kT[:, :, 0:128], in_=kT[:, :, TT:TT + 128]),
               reads=[("kT", k_, NT) for k_ in range(4)], writes=[("kT", k_, 0) for k_ in range(4)])
          S.op("act", lambda e: e.copy(out=vAB[:, :, 0, :, :], in_=vAB[:, :, NT, :, :]),
               reads=[("v", k_, NT) for k_ in range(4)], writes=[("v", k_, 0) for k_ in range(4)])
          barrier()

          ckpt('C2')
          for h in range(4):
              for i in range(NT):
                  b = nps()
                  mm([(ps[b][:, 0:256], glrT[0:17, i * 128:(i + 1) * 128], w2b[0:17, h * 256:(h + 1) * 256], True, True)],
                     reads=["glrT", "w2b"], writes=[("ps", b)])
                  S.op("act", lambda e, b=b, i=i: e.activation(out=g_la[:, i, :], in_=ps[b][:, 0:256], func=AF.Exp,
                                                               scale=-1.0),
                       reads=[("ps", b)], writes=ar(("g_la", i)))
                  S.op("act", lambda e, i=i: e.activation(out=g_la[:, i, :], in_=g_la[:, i, :], func=AF.Ln, bias=1.0),
                       reads=[("g_la", i)], writes=[("g_la", i)])
              la_all = [("g_la", i) for i in range(NT)]
              for dk in range(2):
                  b = nps()
                  mm([(ps[b][:, i * 128:(i + 1) * 128], g_la[:, i, dk * 128:(dk + 1) * 128], Uc, True, True)
                      for i in range(NT)], reads=la_all + ["c_f32"], writes=[("ps", b)])
                  S.op("act", lambda e, b=b, dk=dk: e.activation(out=g_eg[:, dk, :], in_=ps[b][:, :], func=AF.Exp),
                       reads=[("ps", b)], writes=ar(("g_eg", dk)))
                  S.op("act", lambda e, b=b, dk=dk: e.activation(out=g_ei[:, dk, :], in_=ps[b][:, :], func=AF.Exp,
                                                                 scale=-1.0),
                       reads=[("ps", b)], writes=ar(("g_ei", dk)))
              for i2 in range(NT // 2):
                  b = nps()
                  mm([(ps[b][:, j * 256:(j + 1) * 256], Usuf, g_la[:, i2 * 2 + j, :], True, True) for j in range(2)],
                     reads=la_all + ["c_f32"], writes=[("ps", b)])
                  S.op("act", lambda e, b=b, i2=i2: e.activation(
                      out=g_es[:, i2 * 2:i2 * 2 + 2, :], in_=ps[b][:, :].rearrange("p (a b) -> p a b", a=2), func=AF.Exp),
                      reads=[("ps", b)], writes=ar(("g_es", i2)))
              slot = load_w([((lambda w_: w_[:, :, 0:256]), w_in_v[:, :, O_GQ + h * 256:O_GQ + (h + 1) * 256]),
                             ((lambda w_: w_[:, :, 256:512]), w_in_v[:, :, O_GK + h * 256:O_GK + (h + 1) * 256])])
              for dk in range(2):
                  def evq(b, dk=dk):
                      copy_ev(g_qT[:, dk, :], ps[b][:, :], reads=[("ps", b)], writes=ar(("g_qT", dk)), scale=1.0 / 16.0)
                  proj_feat(slot, dk * 128, 128, evq)

                  def evk(b, dk=dk):
                      copy_ev(g_kT[:, dk, :], ps[b][:, :], reads=[("ps", b)], writes=ar(("g_kT", dk)))
                  proj_feat(slot, 256 + dk * 128, 128, evk)
              for i in range(NT):
                  def evkt(b, i=i):
                      copy_ev(g_ktok[:, i, :], ps[b][:, 0:256], reads=[("ps", b)], writes=ar(("g_ktok", i)))
                  proj_tok(slot, 256, 256, i, evkt)
              for dk in range(2):
                  S.op("dve", lambda e, dk=dk: e.tensor_tensor(out=g_qT[:, dk, :], in0=g_qT[:, dk, :],
                                                               in1=g_eg[:, dk, :], op=ALU.mult),
                       reads=[("g_qT", dk), ("g_eg", dk)], writes=[("g_qT", dk)])
                  S.op("dve", lambda e, dk=dk: e.tensor_tensor(out=g_kT[:, dk, :], in0=g_kT[:, dk, :],
                                                               in1=g_ei[:, dk, :], op=ALU.mult),
                       reads=[("g_kT", dk), ("g_ei", dk)], writes=[("g_kT", dk)])
              for i2 in range(NT // 2):
                  S.op("dve", lambda e, i2=i2: e.tensor_tensor(out=g_ktok[:, i2 * 2:i2 * 2 + 2, :],
                                                               in0=g_ktok[:, i2 * 2:i2 * 2 + 2, :],
                                                               in1=g_es[:, i2 * 2:i2 * 2 + 2, :], op=ALU.mult),
                       reads=[("g_ktok", i2 * 2), ("g_ktok", i2 * 2 + 1), ("g_es", i2)],
                       writes=[("g_ktok", i2 * 2), ("g_ktok", i2 * 2 + 1)])
              b = nps()
              mm([(ps[b][:, i * 128:(i + 1) * 128], g_kT[:, dk, i * 128:(i + 1) * 128],
                   g_qT[:, dk, i * 128:(i + 1) * 128], dk == 0, dk == 1) for i in range(NT) for dk in range(2)],
                 reads=[("g_qT", 0), ("g_qT", 1), ("g_kT", 0), ("g_kT", 1)], writes=[("ps", b)])
              trb = tril.unsqueeze(1).to_broadcast([128, NT, 128])
              S.op("dve", lambda e, b=b, trb=trb: e.tensor_tensor(
                  out=g_attm[:, :, :], in0=ps[b][:, :].rearrange("p (a b) -> p a b", a=NT), in1=trb, op=ALU.mult),
                  reads=[("ps", b), "c_f32"], writes=ar("g_attm"))
              slot = load_w([((lambda w_: w_[:, :, :]), w_in_v[:, :, O_GV + h * 512:O_GV + (h + 1) * 512])])
              for i in range(NT):
                  def evv(b, i=i):
                      copy_ev(g_v[:, i, :], ps[b][:, :], reads=[("ps", b)], writes=ar(("g_v", i)))
                  proj_tok(slot, 0, 512, i, evv)
              slot = load_w([((lambda w_: w_[:, :, :]), w_in_v[:, :, O_GR + h * 512:O_GR + (h + 1) * 512])])
              for i in range(NT):
                  def evr(b, i=i):
                      S.op("act", lambda e: e.activation(out=g_rs[:, i, :], in_=ps[b][:, :], func=AF.Silu),
                           reads=[("ps", b)], writes=ar(("g_rs", i)))
                  proj_tok(slot, 0, 512, i, evr)
              slot = load_w([((lambda w_: w_[:, :, :]), w_in_v[:, :, O_GB + h * 512:O_GB + (h + 1) * 512])])
              for i in range(NT):
                  def evb(b, i=i):
                      S.op("act", lambda e: e.activation(out=g_bs[:, i, :], in_=ps[b][:, :], func=AF.Sigmoid),
                           reads=[("ps", b)], writes=ar(("g_bs", i)))
                      S.op("dve", lambda e: e.tensor_tensor(out=g_rs[:, i, :], in0=g_rs[:, i, :], in1=g_bs[:, i, :],
                                                            op=ALU.mult),
                           reads=[("g_rs", i), ("g_bs", i)], writes=[("g_rs", i)])
                      S.op("dve", lambda e: e.tensor_tensor(out=g_rs[:, i, :], in0=g_rs[:, i, :], in1=gnw_sb[:],
                                                            op=ALU.mult),
                           reads=[("g_rs", i), "gnw"], writes=[("g_rs", i)])
                  proj_tok(slot, 0, 512, i, evb)
              for i in range(NT):
                  bo = nps()
                  mm([(ps[bo][:, :], g_attm[:, i, :], g_v[:, i, :], True, False),
                      (ps[bo][:, :], g_qT[:, 0, i * 128:(i + 1) * 128], Sbf[:, h, 0, :], False, False),
                      (ps[bo][:, :], g_qT[:, 1, i * 128:(i + 1) * 128], Sbf[:, h, 1, :], False, True)],
                     reads=["g_attm", ("g_v", i), ("g_qT", 0), ("g_qT", 1), ("Sb", h, 0), ("Sb", h, 1)],
                     writes=[("ps", bo)])
                  S.op("act", lambda e, bo=bo, i=i: e.activation(out=g_G[:, i, :], in_=ps[bo][:, :], func=AF.Square,
                                                                 accum_out=stat[:, 4:5]),
                       reads=[("ps", bo)], writes=ar(("g_G", i)) + ["st4"])
                  S.op("act", lambda e: e.activation(out=stat[:, 5:6], in_=stat[:, 4:5], func=AF.Sqrt, scale=1.0 / 512,
                                                     bias=epsc[:]), reads=["st4", "epsc"], writes=["st5"])
                  S.op("dve", lambda e: e.reciprocal(out=stat[:, 6:7], in_=stat[:, 5:6]), reads=["st5"], writes=["st6"])
                  S.op("dve", lambda e, bo=bo, i=i: e.scalar_tensor_tensor(
                      out=g_G[:, i, :], in0=ps[bo][:, :], scalar=stat[:, 6:7], in1=g_rs[:, i, :], op0=ALU.mult,
                      op1=ALU.mult), reads=[("ps", bo), "st6", ("g_rs", i)], writes=[("g_G", i)])
                  half = st["tr"] % 2
                  st["tr"] += 1

                  def trg(e, i=i, half=half):
                      ins = None
                      for j in range(4):
                          ins = e.transpose(out=ptrs[half][:, j, :], in_=g_G[:, i, j * 128:(j + 1) * 128],
                                            identity=identb[:])
                      return ins
                  S.op("pe", trg, reads=[("g_G", i), "identb"], writes=[("ptr", half)])
                  S.op("dve", lambda e, i=i, half=half, h=h: e.tensor_tensor(
                      out=mT[:, h * 4:(h + 1) * 4, i * 128:(i + 1) * 128], in0=ptrs[half][:, 0:4, :],
                      in1=mT[:, h * 4:(h + 1) * 4, i * 128:(i + 1) * 128], op=ALU.add),
                      reads=[("ptr", half), ("mT", h, i)], writes=[("mT", h, i)])
                  for dk in range(2):
                      bu = nps()
                      mm([(ps[bu][:, :], g_ktok[:, i, dk * 128:(dk + 1) * 128], g_v[:, i, :], True, True)],
                         reads=[("g_ktok", i), ("g_v", i)], writes=[("ps", bu)])
                      S.op("dve", lambda e, bu=bu, dk=dk, i=i, h=h: e.scalar_tensor_tensor(
                          out=Sst[:, h, dk, :], in0=Sst[:, h, dk, :], scalar=g_eg[:, dk, i * 128 + 127:i * 128 + 128],
                          in1=ps[bu][:, :], op0=ALU.mult, op1=ALU.add),
                          reads=[("ps", bu), ("S", h, dk), ("g_eg", dk)], writes=[("S", h, dk)])
                      S.op("act", lambda e, dk=dk, h=h: e.copy(out=Sbf[:, h, dk, :], in_=Sst[:, h, dk, :]),
                           reads=[("S", h, dk)], writes=[("Sb", h, dk)])
          barrier()
          if debug and c == 0:
              S.op("pool", lambda e: e.dma_start(out=dbg["mT"], in_=mT[:]),
                   reads=[("mT", g_, i_) for g_ in range(4) for i_ in range(NT)], writes=["dbg_mT"], dma=True)

          ckpt('D')
          for g in range(4):
              slot = load_w([((lambda w_: w_[:, :, :]), w_out_v[:, :, g * 512:(g + 1) * 512])])
              for i in range(NT):
                  b = nps()
                  mm([(ps[b][:, :], mT[:, k, i * 128:(i + 1) * 128], wsl[slot][:, k, :], k == 0, k == KC - 1)
                      for k in range(KC)], reads=wres(slot) + [("mT", g_, i) for g_ in range(4)], writes=[("ps", b)])
                  S.op("dve", lambda e, b=b, i=i, g=g: e.tensor_tensor(
                      out=xt[:, i, g * 512:(g + 1) * 512], in0=ps[b][:, :], in1=xt[:, i, g * 512:(g + 1) * 512],
                      op=ALU.add), reads=[("ps", b), ("x", i)], writes=[("x", i)])
          if debug and c == 0:
              S.op("sp", lambda e: e.dma_start(out=dbg["h1"], in_=xt[:]), reads=[("x", i_) for i_ in range(NT)],
                   writes=["dbg_h1"], dma=True)
          for i in range(NT):
              rms_to_actT(i, n2_sb, "n2")

          ckpt('E')
          for j in range(HC // 2):
              slot = load_w([((lambda w_: w_[:, :, 0:256]), w_g_v[:, :, j * 256:(j + 1) * 256]),
                             ((lambda w_: w_[:, :, 256:512]), w_u_v[:, :, j * 256:(j + 1) * 256])])
              for sub in range(2):
                  hc = j * 2 + sub
                  bg = nps()
                  mm([(ps[bg][:, :], wsl[slot][:, k, sub * 128:(sub + 1) * 128], actT[:, k, :], k == 0, k == KC - 1)
                      for k in range(KC)], reads=wres(slot) + actT_all, writes=[("ps", bg)])
                  bu = nps()
                  mm([(ps[bu][:, :], wsl[slot][:, k, 256 + sub * 128:256 + (sub + 1) * 128], actT[:, k, :], k == 0,
                       k == KC - 1) for k in range(KC)], reads=wres(slot) + actT_all, writes=[("ps", bu)])
                  S.op("act", lambda e, bg=bg, hc=hc: e.activation(out=ffT[:, hc, :], in_=ps[bg][:, :], func=AF.Silu),
                       reads=[("ps", bg)], writes=ar(("ffT", hc)))
                  S.op("dve", lambda e, bu=bu, hc=hc: e.tensor_tensor(out=ffT[:, hc, :], in0=ps[bu][:, :],
                                                                      in1=ffT[:, hc, :], op=ALU.mult),
                       reads=[("ps", bu), ("ffT", hc)], writes=[("ffT", hc)])

          ckpt('F')
          NPIECE = 4
          PH = HC // NPIECE
          for g in range(4):
              banks = [nps() for _ in range(NT)]
              for p in range(NPIECE):
                  slot = load_w([((lambda w_: w_[:, 0:PH, :]), w_d_v[:, p * PH:(p + 1) * PH, g * 512:(g + 1) * 512])],
                              ring=(0, 1, 2))
                  for i in range(NT):
                      b = banks[i]
                      mm([(ps[b][:, :], ffT[:, p * PH + q, i * 128:(i + 1) * 128], wsl[slot][:, q, :],
                           (p == 0 and q == 0), (p == NPIECE - 1 and q == PH - 1)) for q in range(PH)],
                         reads=wres(slot) + [("ffT", p * PH + q) for q in range(PH)], writes=[("ps", b)])
              for i in range(NT):
                  b = banks[i]
                  S.op("dve", lambda e, b=b, i=i, g=g: e.tensor_tensor(
                      out=xt[:, i, g * 512:(g + 1) * 512], in0=ps[b][:, :], in1=xt[:, i, g * 512:(g + 1) * 512],
                      op=ALU.add), reads=[("ps", b), ("x", i)], writes=[("x", i)])
              if c + 1 < nchunk:
                  norm1_from_dram(c + 1, g)

          ckpt('G')
          for i in range(NT):
              pending.append((c, i))
          if c == nchunk - 1:
              while pending:
                  drain_final()
          barrier()

    except _Stop:
        pass
    S.finalize_on("sp", out_dmas)
    if stage is not None:
        S.finalize_on("sp", [o for o in S.streams["sp"] if o.is_dma])
        S.finalize_on("pool", [o for o in S.streams["pool"] if o.is_dma])
    if debug:
        dd = [o for o in S.streams["pool"] if o.is_dma and o.chan[1] in ("dbg_mT", "dbg_xnT")]
        S.finalize_on("pool", dd)
        dd2 = [o for o in S.streams["sp"] if o.is_dma and o.chan[1] == "dbg_h1"]
        S.finalize_on("sp", dd2)
    S.emit()
    return nc


def _consts():
    j = np.arange(128)[:, None]
    i = np.arange(128)[None, :]
    cst = np.zeros((128, 5, 128), np.float32)
    cst[:, 0, :] = np.eye(128, dtype=np.float32)
    cst[:, 1, :] = np.where(j <= i, -1.0 / 16.0, 0.0)
    cst[:, 2, :] = np.where(j > i, -1.0 / 16.0, 0.0)
    cst[:, 3, :] = np.where(j <= i, 1.0, 0.0)
    mk = np.zeros((128, 2, 512), np.float32)
    mprev = np.where(j > i, 0.0, NEG).astype(np.float32)
    mcur = np.where(j <= i, 0.0, NEG).astype(np.float32)
    mk[:, 0, :] = np.tile(mprev, (1, 4))
    mk[:, 1, :] = np.tile(mcur, (1, 4))
    return cst, mk


def _col_layout(v):
    return np.ascontiguousarray(np.asarray(v, np.float32).reshape(KC, 128).T)


def make_in_maps(inputs, ncores=8):
    f = lambda a: np.ascontiguousarray(np.asarray(a, dtype=np.float32))
    x = f(inputs["x"])
    cst, mk = _consts()
    sinks = f(inputs["attn_sinks"])[0]
    sink_l = np.zeros((128, 16), np.float32)
    for cch in range(16):
        sink_l[0:64, cch] = sinks[2 * cch]
        sink_l[64:128, cch] = sinks[2 * cch + 1]
    shared = {
        "w_in": f(inputs["w_in"])[0],
        "w_out": f(inputs["w_out"])[0],
        "w_g": f(inputs["w_ffn_gate"])[0],
        "w_u": f(inputs["w_ffn_up"])[0],
        "w_d": f(inputs["w_ffn_down"])[0],
        "n1": _col_layout(f(inputs["norm1_w"])[0]),
        "n2": _col_layout(f(inputs["norm2_w"])[0]),
        "sink_l": sink_l,
        "fnw": np.ascontiguousarray(np.broadcast_to(f(inputs["final_norm_w"])[None, :], (128, D))),
        "gnw": np.ascontiguousarray(np.broadcast_to(f(inputs["gla_norm_w"])[0][None, :], (128, 512))),
        "w2b": np.ascontiguousarray(np.concatenate([f(inputs["gla_gate_w2"])[0], f(inputs["gla_gate_b"])[0][None, :]],
                                                   axis=0)),
        "cst": cst,
        "mk": mk,
    }
    maps = []
    for b in range(ncores):
        m = dict(shared)
        m["x"] = np.ascontiguousarray(x[b])
        maps.append(m)
    return maps


def kernel(**inputs):
    nc = build_nc()
    in_maps = make_in_maps(inputs, 8)
    res = run_bass_kernel_spmd(nc, in_maps, core_ids=list(range(8)))
    return np.stack([np.asarray(r["out"], dtype=np.float32) for r in res.results], axis=0)
```

```python
import numpy as np
import concourse.bass as bass
import concourse.mybir as mybir
from concourse.bass_utils import run_bass_kernel_spmd

F32 = mybir.dt.float32
BF16 = mybir.dt.bfloat16
AF = mybir.ActivationFunctionType
ALU = mybir.AluOpType

D = 2048
T = 2048
TT = 512
NT = TT // 128
NCHUNK = T // TT
KC = D // 128
FF = 5632
HC = FF // 128
DIN = 12816
O_AQ, O_AK, O_AV, O_GQ, O_GK, O_GV, O_GLR, O_GR, O_GA, O_GB = (
    0, 2048, 2304, 2560, 3584, 4608, 6656, 6672, 8720, 10768)
EPS = 1e-6
NEG = -30000.0
NSLOT = 2
NPS = 6

ENGS = ("pe", "act", "dve", "pool", "sp")


class Op:
    __slots__ = ("eng", "fn", "deps", "chan", "inc", "signal", "count", "is_dma", "glast")


class Sched:
    def __init__(self, nc):
        self.nc = nc
        self.streams = {e: [] for e in ENGS}
        self.last_w = {}
        self.readers = {}
        self.chan_ops = {}
        self.final_waits = []

    @staticmethod
    def _is_arena(r):
        key = r[0] if isinstance(r, tuple) else r
        return isinstance(key, str) and (key.startswith("a_") or key.startswith("g_") or key in ("ffT", "mk_tmp"))

    @staticmethod
    def _is_xts(r):
        key = r[0] if isinstance(r, tuple) else r
        return isinstance(key, str) and key.startswith("gx_")

    def op(self, eng, fn, reads=(), writes=(), dma=False, chan=None):
        reads = list(reads)
        writes = list(writes)
        if any(self._is_arena(r) for r in reads) or any(self._is_arena(r) for r in writes):
            if "ARENA" not in writes:
                reads.append("ARENA")
        if any(self._is_xts(r) for r in reads) or any(self._is_xts(r) for r in writes):
            if "XTS" not in writes:
                reads.append("XTS")
        o = Op()
        o.eng = eng
        o.fn = fn
        o.is_dma = dma
        o.inc = 16 if dma else 1
        if dma:
            o.chan = ("dma", chan if chan is not None else tuple(writes)[0])
        else:
            o.chan = eng
        o.signal = bool(dma)
        o.count = None
        o.glast = None
        deps = []
        for r in reads:
            w = self.last_w.get(r)
            if w is not None:
                deps.append((w, "raw"))
        for r in writes:
            w = self.last_w.get(r)
            if w is not None:
                deps.append((w, "waw"))
            for rd in self.readers.get(r, ()):
                deps.append((rd, "war"))
        keep = []
        seen = set()
        for d, kind in deps:
            if d is o or id(d) in seen:
                continue
            if (not d.is_dma) and (not dma) and d.eng == eng:
                if eng == "pe":
                    continue
            seen.add(id(d))
            keep.append(d)
        o.deps = keep
        wset = set(writes)
        for r in writes:
            self.last_w[r] = o
            self.readers[r] = []
        for r in reads:
            if r in wset:
                continue
            self.readers.setdefault(r, []).append(o)
        self.streams[eng].append(o)
        self.chan_ops.setdefault(o.chan, []).append(o)
        return o

    def finalize_on(self, eng, ops):
        self.final_waits.append((eng, list(ops)))

    def emit(self):
        nc = self.nc
        for e in ENGS:
            for o in self.streams[e]:
                for d in o.deps:
                    d.signal = True
        for eng, ops in self.final_waits:
            for d in ops:
                d.signal = True
        sems = {}
        nsem = 0
        for chan, ops in self.chan_ops.items():
            c = 0
            any_sig = False
            for o in ops:
                if o.signal:
                    c += o.inc
                    o.count = c
                    any_sig = True
            if any_sig:
                sems[chan] = nc.alloc_semaphore(name="sm%d" % nsem)
                nsem += 1
        finals = {}
        for eng, ops in self.final_waits:
            finals.setdefault(eng, []).extend(ops)
        handles = {"pe": "tensor", "act": "scalar", "dve": "vector", "pool": "gpsimd", "sp": "sync"}
        with nc.Block() as block:
            for e in ENGS:
                stream = self.streams[e]
                fin = finals.get(e, [])
                if not stream and not fin:
                    continue

                def body(engine, stream=stream, fin=fin):
                    known = {}
                    for o in stream:
                        need = {}
                        for d in o.deps:
                            dc = d.glast.count if d.glast is not None else d.count
                            if dc > need.get(d.chan, 0):
                                need[d.chan] = dc
                        for ch, cnt in need.items():
                            if known.get(ch, 0) >= cnt:
                                continue
                            engine.wait_ge(sems[ch], cnt)
                            known[ch] = cnt
                        ins = o.fn(engine)
                        if o.signal:
                            ins.then_inc(sems[o.chan], o.inc)
                    for d in fin:
                        if known.get(d.chan, 0) >= d.count:
                            continue
                        engine.wait_ge(sems[d.chan], d.count)
                        known[d.chan] = d.count

                getattr(block, handles[e])(body)


class _Stop(Exception):
    pass


def build_nc(nchunk=NCHUNK, debug=False, stage=None):
    nc = bass.Bass("TRN2", target_bir_lowering=False)
    S = Sched(nc)

    def din(name, shape):
        return nc.dram_tensor(name, list(shape), F32, kind="ExternalInput").ap()

    x = din("x", [T, D])
    w_in = din("w_in", [D, DIN])
    w_out = din("w_out", [D, D])
    w_g = din("w_g", [D, FF])
    w_u = din("w_u", [D, FF])
    w_d = din("w_d", [FF, D])
    n1 = din("n1", [128, KC])
    n2 = din("n2", [128, KC])
    sink_l = din("sink_l", [128, 16])
    fnw = din("fnw", [128, D])
    gnw = din("gnw", [128, 512])
    w2b_d = din("w2b", [17, 1024])
    cst = din("cst", [128, 5, 128])
    mk = din("mk", [128, 2, 512])
    out = nc.dram_tensor("out", [T, D], F32, kind="ExternalOutput").ap()
    dbg = {}
    if debug:
        dbg["mT"] = nc.dram_tensor("dbg_mT", [128, KC, TT], F32, kind="ExternalOutput").ap()
        dbg["h1"] = nc.dram_tensor("dbg_h1", [128, NT, D], F32, kind="ExternalOutput").ap()
        dbg["xnT"] = nc.dram_tensor("dbg_xnT", [128, KC, TT], F32, kind="ExternalOutput").ap()

    w_in_v = w_in.rearrange("(k p) c -> p k c", p=128)
    w_out_v = w_out.rearrange("(k p) c -> p k c", p=128)
    w_g_v = w_g.rearrange("(k p) c -> p k c", p=128)
    w_u_v = w_u.rearrange("(k p) c -> p k c", p=128)
    w_d_v = w_d.rearrange("(k p) c -> p k c", p=128)

    sb = nc.alloc_sbuf_tensor
    xt = sb("xt", [128, NT, D], F32)
    xn = sb("xn", [128, D], BF16)
    actT = sb("actT", [128, KC, TT], BF16)
    wsl = [sb("wsl%d" % i, [128, KC, 512], BF16) for i in range(NSLOT)]
    mT = sb("mT", [128, KC, TT], BF16)
    kT = sb("kT", [128, 4, 128 + TT], BF16)
    vAB = sb("vAB", [128, 4, NT + 1, 2, 128], BF16)
    Sst = sb("Sst", [128, 4, 2, 512], F32)
    Sbf = sb("Sbf", [128, 4, 2, 512], BF16)
    glrT = sb("glrT", [17, TT], F32)
    c_f32 = sb("c_f32", [128, 5, 128], F32)
    identb = sb("identb", [128, 128], BF16)
    maskb = sb("maskb", [128, 2, 512], BF16)
    onesAB = sb("onesAB", [128, 2, 128], BF16)
    sinkexp = sb("sinkexp", [128, 16], F32)
    fnw_sb = sb("fnw_sb", [128, D], F32)
    gnw_sb = sb("gnw_sb", [128, 512], F32)
    w2b = sb("w2b_sb", [17, 1024], F32)
    n1_sb = sb("n1_sb", [128, KC], F32)
    n2_sb = sb("n2_sb", [128, KC], F32)
    epsc = sb("epsc", [128, 1], F32)
    stat = sb("stat", [128, 32], F32)
    bar = sb("bar", [128, 1], F32)
    bar2 = sb("bar2", [128, 1], F32)
    ffT = sb("ffT", [128, HC, TT], BF16)
    ARENA = HC * TT * 2
    off = [0]
    ffT_off = None

    def carve(name, shape, dt, base):
        n = 1
        for s_ in shape[1:]:
            n *= s_
        nbytes = n * (4 if dt == F32 else 2)
        o_ = base[0]
        base[0] += (nbytes + 63) // 64 * 64
        return (name, shape, dt, o_, nbytes)

    def bview(col0, shape):
        n = 1
        for s_ in shape[1:]:
            n *= s_
        flat = ffT[:, :, :].rearrange("p a b -> p (a b)")[:, col0:col0 + n]
        if len(shape) == 2:
            return flat
        if len(shape) == 3:
            return flat.rearrange("p (a b) -> p a b", a=shape[1])
        raise ValueError

    def fview(col0, shape):
        n = 1
        for s_ in shape[1:]:
            n *= s_
        flat = ffT[:, :, :].rearrange("p a b -> p (a b)")[:, col0:col0 + 2 * n].bitcast(F32)
        if len(shape) == 2:
            return flat
        if len(shape) == 3:
            return flat.rearrange("p (a b) -> p a b", a=shape[1])
        raise ValueError

    a_qT = bview(0, [128, 4, TT])
    a_sga = bview(2048, [128, 4, TT])
    a_pT = [[bview(4096 + (b * 4 + j) * 512, [128, 512]) for j in range(4)] for b in range(2)]
    a_t1 = fview(8192, [128, 512])
    a_t2 = fview(9216, [128, 512])
    g_qT = bview(0, [128, 2, TT])
    g_kT = bview(1024, [128, 2, TT])
    g_ktok = bview(2048, [128, NT, 256])
    g_v = bview(3072, [128, NT, 512])
    g_rs = bview(5120, [128, NT, 512])
    g_bs = bview(7168, [128, NT, 512])
    g_attm = bview(9216, [128, NT, 128])
    g_G = bview(9728, [128, NT, 512])
    g_la = fview(11776, [128, NT, 256])
    g_eg = fview(13824, [128, 2, TT])
    g_ei = fview(15872, [128, 2, TT])
    g_es = fview(17920, [128, NT, 256])

    xt_b = xt[:, :, :].rearrange("p a b -> p (a b)")

    def xbview(col0, shape):
        n = 1
        for s_ in shape[1:]:
            n *= s_
        flat = xt_b[:, col0 // 2:(col0 + n) // 2].bitcast(BF16)
        return flat.rearrange("p (a b) -> p a b", a=shape[1])

    gsets = [
        {"pf": "g_", "qT": g_qT, "kT": g_kT, "ktok": g_ktok, "v": g_v, "rs": g_rs, "bs": g_bs},
        {"pf": "gx_", "qT": xbview(0, [128, 2, TT]), "kT": xbview(1024, [128, 2, TT]),
         "ktok": xbview(2048, [128, NT, 256]), "v": xbview(3072, [128, NT, 512]),
         "rs": xbview(5120, [128, NT, 512]), "bs": xbview(7168, [128, NT, 512])},
    ]

    ps = [nc.alloc_psum_tensor("ps%d" % i, [128, 512], F32) for i in range(NPS)]
    ptrs = [nc.alloc_psum_tensor("ptr%d" % i, [128, 8, 128], BF16) for i in range(2)]
    st = {"ps": 0, "w": 0, "ev": 0, "tr": 0}

    def nps():
        b = st["ps"] % NPS
        st["ps"] += 1
        return b

    WSUB = 8
    wsl.append(mT)
    MT_ALL = [("mT", g_, i_) for g_ in range(4) for i_ in range(NT)]

    def wres(slot):
        if slot == 2:
            return MT_ALL
        return [("w", slot, q) for q in range(WSUB)]

    def load_w(parts, ring=(0, 1)):
        slot = ring[st["w"] % len(ring)]
        st["w"] += 1
        P = len(parts)
        ops = []
        alln = wres(slot)
        for i, (dst_fn, src) in enumerate(parts):
            names = [alln[q] for q in range(len(alln)) if q % P == i]
            ops.append(S.op("pool", (lambda e, dst_fn=dst_fn, src=src, slot=slot: e.dma_start(out=dst_fn(wsl[slot]),
                                                                                           in_=src)),
                            writes=names, dma=True, chan=("w", slot)))
        for o_ in ops:
            o_.glast = ops[-1]
        return slot

    def mm(mms, reads, writes):
        def fn(e):
            ins = None
            for (o_, l_, r_, s0, s1) in mms:
                ins = e.matmul(o_, lhsT=l_, rhs=r_, start=s0, stop=s1)
            return ins
        return S.op("pe", fn, reads=reads, writes=writes)

    def ev_engine():
        st["ev"] += 1
        return "act" if st["ev"] % 2 == 0 else "dve"

    def copy_ev(out_ap, in_ap, reads, writes, scale=None, eng=None):
        eng = eng or ev_engine()
        if eng == "act":
            if scale is None:
                S.op("act", lambda e: e.copy(out=out_ap, in_=in_ap), reads=reads, writes=writes)
            else:
                S.op("act", lambda e: e.activation(out=out_ap, in_=in_ap, func=AF.Copy, scale=scale),
                     reads=reads, writes=writes)
        else:
            if scale is None:
                S.op("dve", lambda e: e.tensor_copy(out=out_ap, in_=in_ap), reads=reads, writes=writes)
            else:
                S.op("dve", lambda e: e.tensor_scalar(out=out_ap, in0=in_ap, scalar1=scale, scalar2=None,
                                                        op0=ALU.mult), reads=reads, writes=writes)

    arena_names = set()

    def ar(*names):
        for n_ in names:
            arena_names.add(n_)
        return list(names)

    def barrier():
        S.op("dve", lambda e: e.memset(bar[:], 0.0), writes=["ARENA", "bar"])

    S.op("sp", lambda e: e.dma_start(out=c_f32[:], in_=cst), writes=["c_f32"], dma=True)
    S.op("sp", lambda e: e.dma_start(out=n1_sb[:], in_=n1), writes=["n1"], dma=True)
    S.op("sp", lambda e: e.dma_start(out=n2_sb[:], in_=n2), writes=["n2"], dma=True)
    S.op("sp", lambda e: e.dma_start(out=sinkexp[:], in_=sink_l), writes=["sinkexp"], dma=True)
    S.op("sp", lambda e: e.dma_start(out=fnw_sb[:], in_=fnw), writes=["fnw"], dma=True)
    S.op("sp", lambda e: e.dma_start(out=gnw_sb[:], in_=gnw), writes=["gnw"], dma=True)
    S.op("sp", lambda e: e.dma_start(out=w2b[:], in_=w2b_d), writes=["w2b"], dma=True)
    mk_tmp = fview(0, [128, 2, 512])
    S.op("sp", lambda e: e.dma_start(out=mk_tmp, in_=mk), writes=ar("mk_tmp"), dma=True)
    S.op("dve", lambda e: e.tensor_copy(out=maskb[:], in_=mk_tmp), reads=["mk_tmp"], writes=["maskb"])
    S.op("dve", lambda e: e.tensor_copy(out=identb[:], in_=c_f32[:, 0, :]), reads=["c_f32"], writes=["identb"])
    S.op("act", lambda e: e.activation(out=sinkexp[:], in_=sinkexp[:], func=AF.Exp), reads=["sinkexp"],
         writes=["sinkexp"])
    S.op("dve", lambda e: e.memset(epsc[:], EPS), writes=["epsc"])
    S.op("dve", lambda e: e.memset(onesAB[:].rearrange("p a b -> p (a b)"), 0.0), writes=["onesAB"])
    S.op("dve", lambda e: e.memset(onesAB[:, 0, 0:64], 1.0), writes=["onesAB"])
    S.op("dve", lambda e: e.memset(onesAB[:, 1, 64:128], 1.0), writes=["onesAB"])
    S.op("dve", lambda e: e.memset(vAB[:].rearrange("p a b c d -> p (a b c d)"), 0.0), writes=[("v", k_, t_) for k_ in range(4) for t_ in range(NT + 1)])
    S.op("dve", lambda e: e.memset(kT[:].rearrange("p a b -> p (a b)"), 0.0), writes=[("kT", k_, t_) for k_ in range(4) for t_ in range(NT + 1)])
    S.op("dve", lambda e: e.memset(Sst[:].rearrange("p a b c -> p (a b c)"), 0.0), writes=[("S", h_, d_) for h_ in range(4) for d_ in range(2)])
    S.op("dve", lambda e: e.memset(Sbf[:].rearrange("p a b c -> p (a b c)"), 0.0), writes=[("Sb", h_, d_) for h_ in range(4) for d_ in range(2)])
    S.op("dve", lambda e: e.memset(glrT[:], 1.0), writes=["glrT"])
    barrier()

    identf = c_f32[:, 0, :]
    Uc = c_f32[:, 1, :]
    Usuf = c_f32[:, 2, :]
    tril = c_f32[:, 3, :]

    def transposes_to_actT(i, nw_sb, nwname):
        for g in range(4):
            half = st["tr"] % 2
            st["tr"] += 1

            def tr(e, g=g, half=half):
                ins = None
                for j in range(4):
                    k = g * 4 + j
                    ins = e.transpose(out=ptrs[half][:, j, :], in_=xn[:, k * 128:(k + 1) * 128], identity=identb[:])
                return ins
            S.op("pe", tr, reads=["xn", "identb"], writes=[("ptr", half)])
            for j in range(4):
                k = g * 4 + j
                eng = ev_engine()
                o_ap = actT[:, k, i * 128:(i + 1) * 128]
                i_ap = ptrs[half][:, j, :]
                sc = nw_sb[:, k:k + 1]
                if eng == "act":
                    S.op("act", lambda e, o_ap=o_ap, i_ap=i_ap, sc=sc: e.activation(out=o_ap, in_=i_ap, func=AF.Copy,
                                                                                     scale=sc),
                         reads=[("ptr", half), nwname], writes=[("actT", i)])
                else:
                    S.op("dve", lambda e, o_ap=o_ap, i_ap=i_ap, sc=sc: e.tensor_scalar(out=o_ap, in0=i_ap, scalar1=sc,
                                                                                        scalar2=None, op0=ALU.mult),
                         reads=[("ptr", half), nwname], writes=[("actT", i)])

    def rstd_from(ss_col, out_col, n):
        S.op("act", lambda e: e.activation(out=stat[:, 1:2], in_=ss_col, func=AF.Sqrt, scale=1.0 / n,
                                           bias=epsc[:]), reads=["st0", "epsc"], writes=["st1"])
        S.op("dve", lambda e: e.reciprocal(out=out_col, in_=stat[:, 1:2]), reads=["st1"], writes=["st2"])

    def rms_to_actT(i, nw_sb, nwname):
        c0_ = 8 + 3 * i
        xv = xt[:, i, :].rearrange("p (a b) -> p a b", a=KC)
        S.op("dve", lambda e: e.scalar_tensor_tensor(out=actT[:, :, i * 128:(i + 1) * 128], in0=xv, scalar=1.0, in1=xv,
                                                     op0=ALU.mult, op1=ALU.mult, accum_out=stat[:, c0_:c0_ + 1]),
             reads=[("x", i)], writes=[("actT", i), ("nst", i, 0)])
        S.op("act", lambda e: e.activation(out=stat[:, c0_ + 1:c0_ + 2], in_=stat[:, c0_:c0_ + 1], func=AF.Sqrt,
                                           scale=1.0 / D, bias=epsc[:]),
             reads=[("nst", i, 0), "epsc"], writes=[("nst", i, 1)])
        S.op("dve", lambda e: e.reciprocal(out=stat[:, c0_ + 2:c0_ + 3], in_=stat[:, c0_ + 1:c0_ + 2]),
             reads=[("nst", i, 1)], writes=[("nst", i, 2)])
        if i % 2 == 0:
            S.op("act", lambda e: e.activation(out=xn[:], in_=xt[:, i, :], func=AF.Copy, scale=stat[:, c0_ + 2:c0_ + 3]),
                 reads=[("x", i), ("nst", i, 2)], writes=["xn"])
        else:
            S.op("dve", lambda e: e.tensor_scalar(out=xn[:], in0=xt[:, i, :], scalar1=stat[:, c0_ + 2:c0_ + 3],
                                                    scalar2=None, op0=ALU.mult),
                 reads=[("x", i), ("nst", i, 2)], writes=["xn"])
        transposes_to_actT(i, nw_sb, nwname)

    def norm1_from_dram(cc, i):
        r0 = cc * TT + i * 128
        S.op("pool", lambda e: e.dma_start(out=xn[:], in_=x[r0:r0 + 128, :]), writes=["xn"], dma=True, chan="xn")
        S.op("act", lambda e: e.activation(out=actT[:, :, i * 128:(i + 1) * 128],
                                           in_=xn[:].rearrange("p (a b) -> p a b", a=KC), func=AF.Square,
                                           accum_out=stat[:, 0:1]),
             reads=["xn"], writes=[("actT", i), "st0"])
        rstd_from(stat[:, 0:1], stat[:, 2:3], D)
        S.op("act", lambda e: e.activation(out=xn[:], in_=xn[:], func=AF.Copy, scale=stat[:, 2:3]),
             reads=["xn", "st2"], writes=["xn"])
        transposes_to_actT(i, n1_sb, "n1")

    actT_all = [("actT", i) for i in range(NT)]

    def proj_feat(slot, col0, ncols, evac):
        b = nps()
        mm([(ps[b][0:ncols, :], wsl[slot][:, k, col0:col0 + ncols], actT[:, k, :], k == 0, k == KC - 1)
            for k in range(KC)], reads=wres(slot) + actT_all, writes=[("ps", b)])
        evac(b)

    def proj_tok(slot, col0, ncols, i, evac):
        b = nps()
        mm([(ps[b][:, 0:ncols], actT[:, k, i * 128:(i + 1) * 128], wsl[slot][:, k, col0:col0 + ncols], k == 0,
             k == KC - 1) for k in range(KC)], reads=wres(slot) + [("actT", i)], writes=[("ps", b)])
        evac(b)

    pending = []
    out_dmas = []

    def drain_final():
        if not pending:
            return
        cc, i = pending.pop(0)
        r0 = cc * TT + i * 128
        S.op("act", lambda e: e.activation(out=xn[:], in_=xt[:, i, :], func=AF.Square, accum_out=stat[:, 0:1]),
             reads=[("x", i)], writes=["xn", "st0"])
        rstd_from(stat[:, 0:1], stat[:, 2:3], D)
        S.op("act", lambda e: e.activation(out=xt[:, i, :], in_=xt[:, i, :], func=AF.Copy, scale=stat[:, 2:3]),
             reads=[("x", i), "st2"], writes=[("x", i)])
        S.op("dve", lambda e: e.tensor_tensor(out=xt[:, i, :], in0=xt[:, i, :], in1=fnw_sb[:], op=ALU.mult),
             reads=[("x", i), "fnw"], writes=[("x", i)])
        od = S.op("sp", lambda e: e.dma_start(out=out[r0:r0 + 128, :], in_=xt[:, i, :]),
                  reads=[("x", i)], writes=[("out", cc, i)], dma=True, chan=("out", i))
        out_dmas.append(od)

    def ckpt(name):
        if stage == name:
            raise _Stop()

    try:
      for c in range(nchunk):
          t0 = c * TT
          if c == 0:
              for i in range(NT):
                  norm1_from_dram(0, i)
          if debug and c == 0:
              S.op("pool", lambda e: e.dma_start(out=dbg["xnT"], in_=actT[:]), reads=actT_all, writes=["dbg_xnT"],
                   dma=True)

          ckpt('A')
          kparts = []
          for kvh in range(4):
              for dup in range(2):
                  kparts.append((
                      (lambda w_, kvh=kvh, dup=dup: w_[:, :, kvh * 128 + dup * 64:kvh * 128 + dup * 64 + 64]),
                      w_in_v[:, :, O_AK + kvh * 64:O_AK + (kvh + 1) * 64]))
          slot = load_w(kparts)
          for kvh in range(4):
              def evk(b, kvh=kvh):
                  copy_ev(kT[:, kvh, 128:128 + TT], ps[b][:, :], reads=[("ps", b)],
                          writes=[("kT", kvh, t_) for t_ in range(1, NT + 1)])
              proj_feat(slot, kvh * 128, 128, evk)
              drain_final()
          slot = load_w([((lambda w_: w_[:, :, 0:256]), w_in_v[:, :, O_AV:O_AV + 256]),
                         ((lambda w_: w_[:, :, 256:272]), w_in_v[:, :, O_GLR:O_GLR + 16])])
          for i in range(NT):
              def evv(b, i=i):
                  src = ps[b][:, 0:256].rearrange("p (a b) -> p a b", a=4)
                  wr = [("v", k_, 1 + i) for k_ in range(4)]
                  S.op("act", lambda e: e.copy(out=vAB[:, :, 1 + i, 0, 0:64], in_=src), reads=[("ps", b)], writes=wr)
                  S.op("dve", lambda e: e.tensor_copy(out=vAB[:, :, 1 + i, 1, 64:128], in_=src), reads=[("ps", b)],
                       writes=wr)
              proj_tok(slot, 0, 256, i, evv)

          def evg(b):
              copy_ev(glrT[0:16, :], ps[b][0:16, :], reads=[("ps", b)], writes=["glrT"])
          proj_feat(slot, 256, 16, evg)
          while pending:
              drain_final()

          ckpt('B')
          for kvh in range(4):
              slot = load_w([((lambda w_: w_[:, :, :]), w_in_v[:, :, O_AQ + kvh * 512:O_AQ + (kvh + 1) * 512])])
              for pr in range(4):
                  def evq(b, pr=pr):
                      copy_ev(a_qT[:, pr, :], ps[b][:, :], reads=[("ps", b)], writes=ar(("a_qT", pr)))
                  proj_feat(slot, pr * 128, 128, evq)
              slot = load_w([((lambda w_: w_[:, :, :]), w_in_v[:, :, O_GA + kvh * 512:O_GA + (kvh + 1) * 512])])
              for pr in range(4):
                  def evga(b, pr=pr):
                      S.op("act", lambda e: e.activation(out=a_sga[:, pr, :], in_=ps[b][:, :], func=AF.Sigmoid),
                           reads=[("ps", b)], writes=ar(("a_sga", pr)))
                  proj_feat(slot, pr * 128, 128, evga)
              qres = [("a_qT", pr) for pr in range(4)]
              gres = [("a_sga", pr) for pr in range(4)]
              for n in range(NT):
                  gn = c * NT + n
                  kbs = ([0] if gn > 0 else []) + [1]
                  buf = n % 2
                  for half in range(2):
                      hs = slice(half * 64, half * 64 + 64)
                      for kb in kbs:
                          kt_idx = n + kb
                          b = nps()
                          mm([(ps[b][:, :], kT[hs, kvh, kt_idx * 128:(kt_idx + 1) * 128],
                               a_qT[hs, :, n * 128:(n + 1) * 128], True, False),
                              (ps[b][:, :], identb[:], maskb[:, kb, :], False, True)],
                             reads=[("kT", kvh, kt_idx), "identb", "maskb"] + qres, writes=[("ps", b)])
                          pt = a_pT[buf][half * 2 + kb]
                          S.op("act", lambda e, pt=pt, b=b: e.activation(out=pt, in_=ps[b][:, :], func=AF.Exp,
                                                                          scale=0.125),
                               reads=[("ps", b)], writes=ar(("a_pT", buf, half * 2 + kb)))
                  bo = nps()
                  bd = nps()
                  seq = [(half, kb) for half in range(2) for kb in kbs]
                  mm([(ps[bo][:, :], vAB[:, kvh, n + kb, half, :], a_pT[buf][half * 2 + kb], idx == 0,
                       idx == len(seq) - 1) for idx, (half, kb) in enumerate(seq)],
                     reads=[("v", kvh, n + kb) for kb in kbs] + [("a_pT", buf, half * 2 + kb) for half, kb in seq],
                     writes=[("ps", bo)])
                  mm([(ps[bd][:, :], onesAB[:, half, :], a_pT[buf][half * 2 + kb], idx == 0, idx == len(seq) - 1)
                      for idx, (half, kb) in enumerate(seq)],
                     reads=["onesAB"] + [("a_pT", buf, half * 2 + kb) for half, kb in seq], writes=[("ps", bd)])
                  sk = sinkexp[:, kvh * 4:(kvh + 1) * 4].unsqueeze(2).to_broadcast([128, 4, 128])
                  t1v = a_t1.rearrange("p (a b) -> p a b", a=4)
                  S.op("dve", lambda e, bd=bd, sk=sk, t1v=t1v: e.tensor_tensor(
                      out=t1v, in0=ps[bd][:, :].rearrange("p (a b) -> p a b", a=4), in1=sk, op=ALU.add),
                      reads=[("ps", bd), "sinkexp"], writes=ar("a_t1"))
                  S.op("dve", lambda e: e.reciprocal(out=a_t1, in_=a_t1), reads=["a_t1"], writes=["a_t1"])
                  S.op("dve", lambda e, bo=bo: e.tensor_tensor(out=a_t2, in0=ps[bo][:, :], in1=a_t1, op=ALU.mult),
                       reads=[("ps", bo), "a_t1"], writes=ar("a_t2"))
                  S.op("dve", lambda e, n=n, kvh=kvh: e.tensor_tensor(
                      out=mT[:, kvh * 4:(kvh + 1) * 4, n * 128:(n + 1) * 128],
                      in0=a_t2.rearrange("p (a b) -> p a b", a=4), in1=a_sga[:, :, n * 128:(n + 1) * 128], op=ALU.mult),
                      reads=["a_t2"] + gres, writes=[("mT", kvh, n)])
          ckpt('C')
          S.op("act", lambda e: e.copy(out=kT[:, :, 0:128], in_=kT[:, :, TT:TT + 128]),
               reads=[("kT", k_, NT) for k_ in range(4)], writes=[("kT", k_, 0) for k_ in range(4)])
          S.op("act", lambda e: e.copy(out=vAB[:, :, 0, :, :], in_=vAB[:, :, NT, :, :]),
               reads=[("v", k_, NT) for k_ in range(4)], writes=[("v", k_, 0) for k_ in range(4)])
          barrier()

          ckpt('C2')
          S.op("dve", lambda e: e.memset(bar2[:], 0.0), writes=["XTS", "bar2"] + [("x", i_) for i_ in range(NT)])

          def proj_gen(h):
              sv = gsets[(h + 1) % 2]
              pf = sv["pf"]
              slot = load_w([((lambda w_: w_[:, :, 0:256]), w_in_v[:, :, O_GQ + h * 256:O_GQ + (h + 1) * 256]),
                             ((lambda w_: w_[:, :, 256:512]), w_in_v[:, :, O_GK + h * 256:O_GK + (h + 1) * 256])])
              for dk in range(2):
                  def evq(b, dk=dk):
                      copy_ev(sv["qT"][:, dk, :], ps[b][:, :], reads=[("ps", b)], writes=[(pf + "qT", dk)],
                              scale=1.0 / 16.0)
                  proj_feat(slot, dk * 128, 128, evq)
                  yield

                  def evk(b, dk=dk):
                      copy_ev(sv["kT"][:, dk, :], ps[b][:, :], reads=[("ps", b)], writes=[(pf + "kT", dk)])
                  proj_feat(slot, 256 + dk * 128, 128, evk)
                  yield
              for i in range(NT):
                  def evkt(b, i=i):
                      copy_ev(sv["ktok"][:, i, :], ps[b][:, 0:256], reads=[("ps", b)], writes=[(pf + "ktok", i)])
                  proj_tok(slot, 256, 256, i, evkt)
                  yield
              slot = load_w([((lambda w_: w_[:, :, :]), w_in_v[:, :, O_GV + h * 512:O_GV + (h + 1) * 512])])
              for i in range(NT):
                  def evv(b, i=i):
                      copy_ev(sv["v"][:, i, :], ps[b][:, :], reads=[("ps", b)], writes=[(pf + "v", i)])
                  proj_tok(slot, 0, 512, i, evv)
                  yield
              slot = load_w([((lambda w_: w_[:, :, :]), w_in_v[:, :, O_GR + h * 512:O_GR + (h + 1) * 512])])
              for i in range(NT):
                  def evr(b, i=i):
                      S.op("act", lambda e: e.activation(out=sv["rs"][:, i, :], in_=ps[b][:, :], func=AF.Silu),
                           reads=[("ps", b)], writes=[(pf + "rs", i)])
                  proj_tok(slot, 0, 512, i, evr)
                  yield
              slot = load_w([((lambda w_: w_[:, :, :]), w_in_v[:, :, O_GB + h * 512:O_GB + (h + 1) * 512])])
              for i in range(NT):
                  def evb(b, i=i):
                      S.op("act", lambda e: e.activation(out=sv["bs"][:, i, :], in_=ps[b][:, :], func=AF.Sigmoid),
                           reads=[("ps", b)], writes=[(pf + "bs", i)])
                      S.op("dve", lambda e: e.tensor_tensor(out=sv["rs"][:, i, :], in0=sv["rs"][:, i, :],
                                                            in1=sv["bs"][:, i, :], op=ALU.mult),
                           reads=[(pf + "rs", i), (pf + "bs", i)], writes=[(pf + "rs", i)])
                      S.op("dve", lambda e: e.tensor_tensor(out=sv["rs"][:, i, :], in0=sv["rs"][:, i, :],
                                                            in1=gnw_sb[:], op=ALU.mult),
                           reads=[(pf + "rs", i), "gnw"], writes=[(pf + "rs", i)])
                  proj_tok(slot, 0, 512, i, evb)
                  yield

          def chain_gen(h):
              sv = gsets[(h + 1) % 2]
              pf = sv["pf"]
              qT_, kT_, ktok_, v_, rs_ = sv["qT"], sv["kT"], sv["ktok"], sv["v"], sv["rs"]
              for i in range(NT):
                  b = nps()
                  mm([(ps[b][:, 0:256], glrT[0:17, i * 128:(i + 1) * 128], w2b[0:17, h * 256:(h + 1) * 256], True, True)],
                     reads=["glrT", "w2b"], writes=[("ps", b)])
                  S.op("act", lambda e, b=b, i=i: e.activation(out=g_la[:, i, :], in_=ps[b][:, 0:256], func=AF.Exp,
                                                               scale=-1.0),
                       reads=[("ps", b)], writes=[("g_la", i)])
                  S.op("act", lambda e, i=i: e.activation(out=g_la[:, i, :], in_=g_la[:, i, :], func=AF.Ln, bias=1.0),
                       reads=[("g_la", i)], writes=[("g_la", i)])
              la_all = [("g_la", i) for i in range(NT)]
              for dk in range(2):
                  b = nps()
                  mm([(ps[b][:, i * 128:(i + 1) * 128], g_la[:, i, dk * 128:(dk + 1) * 128], Uc, True, True)
                      for i in range(NT)], reads=la_all + ["c_f32"], writes=[("ps", b)])
                  S.op("act", lambda e, b=b, dk=dk: e.activation(out=g_eg[:, dk, :], in_=ps[b][:, :], func=AF.Exp),
                       reads=[("ps", b)], writes=[("g_eg", dk)])
                  S.op("act", lambda e, b=b, dk=dk: e.activation(out=g_ei[:, dk, :], in_=ps[b][:, :], func=AF.Exp,
                                                                 scale=-1.0),
                       reads=[("ps", b)], writes=[("g_ei", dk)])
              for i2 in range(NT // 2):
                  b = nps()
                  mm([(ps[b][:, j * 256:(j + 1) * 256], Usuf, g_la[:, i2 * 2 + j, :], True, True) for j in range(2)],
                     reads=la_all + ["c_f32"], writes=[("ps", b)])
                  S.op("act", lambda e, b=b, i2=i2: e.activation(
                      out=g_es[:, i2 * 2:i2 * 2 + 2, :], in_=ps[b][:, :].rearrange("p (a b) -> p a b", a=2), func=AF.Exp),
                      reads=[("ps", b)], writes=[("g_es", i2)])
              yield
              for dk in range(2):
                  S.op("dve", lambda e, dk=dk: e.tensor_tensor(out=qT_[:, dk, :], in0=qT_[:, dk, :],
                                                               in1=g_eg[:, dk, :], op=ALU.mult),
                       reads=[(pf + "qT", dk), ("g_eg", dk)], writes=[(pf + "qT", dk)])
                  S.op("dve", lambda e, dk=dk: e.tensor_tensor(out=kT_[:, dk, :], in0=kT_[:, dk, :],
                                                               in1=g_ei[:, dk, :], op=ALU.mult),
                       reads=[(pf + "kT", dk), ("g_ei", dk)], writes=[(pf + "kT", dk)])
              for i2 in range(NT // 2):
                  S.op("dve", lambda e, i2=i2: e.tensor_tensor(out=ktok_[:, i2 * 2:i2 * 2 + 2, :],
                                                               in0=ktok_[:, i2 * 2:i2 * 2 + 2, :],
                                                               in1=g_es[:, i2 * 2:i2 * 2 + 2, :], op=ALU.mult),
                       reads=[(pf + "ktok", i2 * 2), (pf + "ktok", i2 * 2 + 1), ("g_es", i2)],
                       writes=[(pf + "ktok", i2 * 2), (pf + "ktok", i2 * 2 + 1)])
              b = nps()
              mm([(ps[b][:, i * 128:(i + 1) * 128], kT_[:, dk, i * 128:(i + 1) * 128],
                   qT_[:, dk, i * 128:(i + 1) * 128], dk == 0, dk == 1) for i in range(NT) for dk in range(2)],
                 reads=[(pf + "qT", 0), (pf + "qT", 1), (pf + "kT", 0), (pf + "kT", 1)], writes=[("ps", b)])
              trb = tril.unsqueeze(1).to_broadcast([128, NT, 128])
              S.op("dve", lambda e, b=b, trb=trb: e.tensor_tensor(
                  out=g_attm[:, :, :], in0=ps[b][:, :].rearrange("p (a b) -> p a b", a=NT), in1=trb, op=ALU.mult),
                  reads=[("ps", b), "c_f32"], writes=["g_attm"])
              yield
              for i in range(NT):
                  bo = nps()
                  mm([(ps[bo][:, :], g_attm[:, i, :], v_[:, i, :], True, False),
                      (ps[bo][:, :], qT_[:, 0, i * 128:(i + 1) * 128], Sbf[:, h, 0, :], False, False),
                      (ps[bo][:, :], qT_[:, 1, i * 128:(i + 1) * 128], Sbf[:, h, 1, :], False, True)],
                     reads=["g_attm", (pf + "v", i), (pf + "qT", 0), (pf + "qT", 1), ("Sb", h, 0), ("Sb", h, 1)],
                     writes=[("ps", bo)])
                  for dk in range(2):
                      bu = nps()
                      mm([(ps[bu][:, :], ktok_[:, i, dk * 128:(dk + 1) * 128], v_[:, i, :], True, True)],
                         reads=[(pf + "ktok", i), (pf + "v", i)], writes=[("ps", bu)])
                      S.op("dve", lambda e, bu=bu, dk=dk, i=i, h=h: e.scalar_tensor_tensor(
                          out=Sst[:, h, dk, :], in0=Sst[:, h, dk, :], scalar=g_eg[:, dk, i * 128 + 127:i * 128 + 128],
                          in1=ps[bu][:, :], op0=ALU.mult, op1=ALU.add),
                          reads=[("ps", bu), ("S", h, dk), ("g_eg", dk)], writes=[("S", h, dk)])
                      S.op("act", lambda e, dk=dk, h=h: e.copy(out=Sbf[:, h, dk, :], in_=Sst[:, h, dk, :]),
                           reads=[("S", h, dk)], writes=[("Sb", h, dk)])
                  yield
                  S.op("act", lambda e, bo=bo, i=i: e.activation(out=g_G[:, i, :], in_=ps[bo][:, :], func=AF.Square,
                                                                 accum_out=stat[:, 4:5]),
                       reads=[("ps", bo)], writes=[("g_G", i), "st4"])
                  S.op("act", lambda e: e.activation(out=stat[:, 5:6], in_=stat[:, 4:5], func=AF.Sqrt, scale=1.0 / 512,
                                                     bias=epsc[:]), reads=["st4", "epsc"], writes=["st5"])
                  S.op("dve", lambda e: e.reciprocal(out=stat[:, 6:7], in_=stat[:, 5:6]), reads=["st5"], writes=["st6"])
                  S.op("dve", lambda e, bo=bo, i=i: e.scalar_tensor_tensor(
                      out=g_G[:, i, :], in0=ps[bo][:, :], scalar=stat[:, 6:7], in1=rs_[:, i, :], op0=ALU.mult,
                      op1=ALU.mult), reads=[("ps", bo), "st6", (pf + "rs", i)], writes=[("g_G", i)])
                  half = st["tr"] % 2
                  st["tr"] += 1

                  def trg(e, i=i, half=half):
                      ins = None
                      for j in range(4):
                          ins = e.transpose(out=ptrs[half][:, j, :], in_=g_G[:, i, j * 128:(j + 1) * 128],
                                            identity=identb[:])
                      return ins
                  S.op("pe", trg, reads=[("g_G", i), "identb"], writes=[("ptr", half)])
                  S.op("dve", lambda e, i=i, half=half, h=h: e.tensor_tensor(
                      out=mT[:, h * 4:(h + 1) * 4, i * 128:(i + 1) * 128], in0=ptrs[half][:, 0:4, :],
                      in1=mT[:, h * 4:(h + 1) * 4, i * 128:(i + 1) * 128], op=ALU.add),
                      reads=[("ptr", half), ("mT", h, i)], writes=[("mT", h, i)])
                  yield

          for _ in proj_gen(0):
              pass
          for h in range(4):
              pg = proj_gen(h + 1) if h < 3 else None
              for _ in chain_gen(h):
                  if pg is not None:
                      for _k in range(2):
                          next(pg, None)
              if pg is not None:
                  for _ in pg:
                      pass
              if h == 2:
                  S.op("dve", lambda e: e.memset(bar2[:], 0.0),
                       writes=["XTS", "bar2"] + [("x", i_) for i_ in range(NT)])
                  for i in range(NT):
                      S.op("sp", lambda e, i=i, t0=t0: e.dma_start(out=xt[:, i, :],
                                                                   in_=x[t0 + i * 128:t0 + (i + 1) * 128, :]),
                           writes=[("x", i)], dma=True, chan=("x", i))
          barrier()
          if debug and c == 0:
              S.op("pool", lambda e: e.dma_start(out=dbg["mT"], in_=mT[:]),
                   reads=[("mT", g_, i_) for g_ in range(4) for i_ in range(NT)], writes=["dbg_mT"], dma=True)

          ckpt('D')
          for g in range(4):
              slot = load_w([((lambda w_: w_[:, :, :]), w_out_v[:, :, g * 512:(g + 1) * 512])])
              for i in range(NT):
                  b = nps()
                  mm([(ps[b][:, :], mT[:, k, i * 128:(i + 1) * 128], wsl[slot][:, k, :], k == 0, k == KC - 1)
                      for k in range(KC)], reads=wres(slot) + [("mT", g_, i) for g_ in range(4)], writes=[("ps", b)])
                  S.op("dve", lambda e, b=b, i=i, g=g: e.tensor_tensor(
                      out=xt[:, i, g * 512:(g + 1) * 512], in0=ps[b][:, :], in1=xt[:, i, g * 512:(g + 1) * 512],
                      op=ALU.add), reads=[("ps", b), ("x", i)], writes=[("x", i)])
          if debug and c == 0:
              S.op("sp", lambda e: e.dma_start(out=dbg["h1"], in_=xt[:]), reads=[("x", i_) for i_ in range(NT)],
                   writes=["dbg_h1"], dma=True)
          for i in range(NT):
              rms_to_actT(i, n2_sb, "n2")

          ckpt('E')
          for j in range(HC // 2):
              slot = load_w([((lambda w_: w_[:, :, 0:256]), w_g_v[:, :, j * 256:(j + 1) * 256]),
                             ((lambda w_: w_[:, :, 256:512]), w_u_v[:, :, j * 256:(j + 1) * 256])])
              for sub in range(2):
                  hc = j * 2 + sub
                  bg = nps()
                  mm([(ps[bg][:, :], wsl[slot][:, k, sub * 128:(sub + 1) * 128], actT[:, k, :], k == 0, k == KC - 1)
                      for k in range(KC)], reads=wres(slot) + actT_all, writes=[("ps", bg)])
                  bu = nps()
                  mm([(ps[bu][:, :], wsl[slot][:, k, 256 + sub * 128:256 + (sub + 1) * 128], actT[:, k, :], k == 0,
                       k == KC - 1) for k in range(KC)], reads=wres(slot) + actT_all, writes=[("ps", bu)])
                  S.op("act", lambda e, bg=bg, hc=hc: e.activation(out=ffT[:, hc, :], in_=ps[bg][:, :], func=AF.Silu),
                       reads=[("ps", bg)], writes=ar(("ffT", hc)))
                  S.op("dve", lambda e, bu=bu, hc=hc: e.tensor_tensor(out=ffT[:, hc, :], in0=ps[bu][:, :],
                                                                      in1=ffT[:, hc, :], op=ALU.mult),
                       reads=[("ps", bu), ("ffT", hc)], writes=[("ffT", hc)])

          ckpt('F')
          NPIECE = 4
          PH = HC // NPIECE
          for g in range(4):
              banks = [nps() for _ in range(NT)]
              for p in range(NPIECE):
                  slot = load_w([((lambda w_: w_[:, 0:PH, :]), w_d_v[:, p * PH:(p + 1) * PH, g * 512:(g + 1) * 512])],
                              ring=(0, 1, 2))
                  for i in range(NT):
                      b = banks[i]
                      mm([(ps[b][:, :], ffT[:, p * PH + q, i * 128:(i + 1) * 128], wsl[slot][:, q, :],
                           (p == 0 and q == 0), (p == NPIECE - 1 and q == PH - 1)) for q in range(PH)],
                         reads=wres(slot) + [("ffT", p * PH + q) for q in range(PH)], writes=[("ps", b)])
              for i in range(NT):
                  b = banks[i]
                  S.op("dve", lambda e, b=b, i=i, g=g: e.tensor_tensor(
                      out=xt[:, i, g * 512:(g + 1) * 512], in0=ps[b][:, :], in1=xt[:, i, g * 512:(g + 1) * 512],
                      op=ALU.add), reads=[("ps", b), ("x", i)], writes=[("x", i)])
              if c + 1 < nchunk:
                  norm1_from_dram(c + 1, g)

          ckpt('G')
          for i in range(NT):
              pending.append((c, i))
          if c == nchunk - 1:
              while pending:
                  drain_final()
          barrier()

    except _Stop:
        pass
    S.finalize_on("sp", out_dmas)
    if stage is not None:
        S.finalize_on("sp", [o for o in S.streams["sp"] if o.is_dma])
        S.finalize_on("pool", [o for o in S.streams["pool"] if o.is_dma])
    if debug:
        dd = [o for o in S.streams["pool"] if o.is_dma and o.chan[1] in ("dbg_mT", "dbg_xnT")]
        S.finalize_on("pool", dd)
        dd2 = [o for o in S.streams["sp"] if o.is_dma and o.chan[1] == "dbg_h1"]
        S.finalize_on("sp", dd2)
    S.emit()
    return nc


def _consts():
    j = np.arange(128)[:, None]
    i = np.arange(128)[None, :]
    cst = np.zeros((128, 5, 128), np.float32)
    cst[:, 0, :] = np.eye(128, dtype=np.float32)
    cst[:, 1, :] = np.where(j <= i, -1.0 / 16.0, 0.0)
    cst[:, 2, :] = np.where(j > i, -1.0 / 16.0, 0.0)
    cst[:, 3, :] = np.where(j <= i, 1.0, 0.0)
    mk = np.zeros((128, 2, 512), np.float32)
    mprev = np.where(j > i, 0.0, NEG).astype(np.float32)
    mcur = np.where(j <= i, 0.0, NEG).astype(np.float32)
    mk[:, 0, :] = np.tile(mprev, (1, 4))
    mk[:, 1, :] = np.tile(mcur, (1, 4))
    return cst, mk


def _col_layout(v):
    return np.ascontiguousarray(np.asarray(v, np.float32).reshape(KC, 128).T)


def make_in_maps(inputs, ncores=8):
    f = lambda a: np.ascontiguousarray(np.asarray(a, dtype=np.float32))
    x = f(inputs["x"])
    cst, mk = _consts()
    sinks = f(inputs["attn_sinks"])[0]
    sink_l = np.zeros((128, 16), np.float32)
    for cch in range(16):
        sink_l[0:64, cch] = sinks[2 * cch]
        sink_l[64:128, cch] = sinks[2 * cch + 1]
    shared = {
        "w_in": f(inputs["w_in"])[0],
        "w_out": f(inputs["w_out"])[0],
        "w_g": f(inputs["w_ffn_gate"])[0],
        "w_u": f(inputs["w_ffn_up"])[0],
        "w_d": f(inputs["w_ffn_down"])[0],
        "n1": _col_layout(f(inputs["norm1_w"])[0]),
        "n2": _col_layout(f(inputs["norm2_w"])[0]),
        "sink_l": sink_l,
        "fnw": np.ascontiguousarray(np.broadcast_to(f(inputs["final_norm_w"])[None, :], (128, D))),
        "gnw": np.ascontiguousarray(np.broadcast_to(f(inputs["gla_norm_w"])[0][None, :], (128, 512))),
        "w2b": np.ascontiguousarray(np.concatenate([f(inputs["gla_gate_w2"])[0], f(inputs["gla_gate_b"])[0][None, :]],
                                                   axis=0)),
        "cst": cst,
        "mk": mk,
    }
    maps = []
    for b in range(ncores):
        m = dict(shared)
        m["x"] = np.ascontiguousarray(x[b])
        maps.append(m)
    return maps


def kernel(**inputs):
    nc = build_nc()
    in_maps = make_in_maps(inputs, 8)
    res = run_bass_kernel_spmd(nc, in_maps, core_ids=list(range(8)))
    return np.stack([np.asarray(r["out"], dtype=np.float32) for r in res.results], axis=0)
```

```python
import numpy as np
import concourse.bass as bass
import concourse.mybir as mybir
from concourse.bass_utils import run_bass_kernel_spmd

F32 = mybir.dt.float32
BF16 = mybir.dt.bfloat16
AF = mybir.ActivationFunctionType
ALU = mybir.AluOpType

D = 2048
T = 2048
TT = 512
NT = TT // 128
NCHUNK = T // TT
KC = D // 128
FF = 5632
HC = FF // 128
DIN = 12816
O_AQ, O_AK, O_AV, O_GQ, O_GK, O_GV, O_GLR, O_GR, O_GA, O_GB = (
    0, 2048, 2304, 2560, 3584, 4608, 6656, 6672, 8720, 10768)
EPS = 1e-6
NEG = -30000.0
NSLOT = 2
NPS = 8

ENGS = ("pe", "act", "dve", "pool", "sp")


class Op:
    __slots__ = ("eng", "fn", "deps", "chan", "inc", "signal", "count", "is_dma", "glast")


class Sched:
    def __init__(self, nc):
        self.nc = nc
        self.streams = {e: [] for e in ENGS}
        self.last_w = {}
        self.readers = {}
        self.chan_ops = {}
        self.final_waits = []

    @staticmethod
    def _is_arena(r):
        key = r[0] if isinstance(r, tuple) else r
        return isinstance(key, str) and (key.startswith("a_") or key.startswith("g_") or key in ("ffT", "mk_tmp"))

    @staticmethod
    def _is_xts(r):
        key = r[0] if isinstance(r, tuple) else r
        return isinstance(key, str) and key.startswith("gx_")

    def op(self, eng, fn, reads=(), writes=(), dma=False, chan=None):
        reads = list(reads)
        writes = list(writes)
        if any(self._is_arena(r) for r in reads) or any(self._is_arena(r) for r in writes):
            if "ARENA" not in writes:
                reads.append("ARENA")
        if any(self._is_xts(r) for r in reads) or any(self._is_xts(r) for r in writes):
            if "XTS" not in writes:
                reads.append("XTS")
        o = Op()
        o.eng = eng
        o.fn = fn
        o.is_dma = dma
        o.inc = 16 if dma else 1
        if dma:
            o.chan = ("dma", chan if chan is not None else tuple(writes)[0])
        else:
            o.chan = eng
        o.signal = bool(dma)
        o.count = None
        o.glast = None
        deps = []
        for r in reads:
            w = self.last_w.get(r)
            if w is not None:
                deps.append((w, "raw"))
        for r in writes:
            w = self.last_w.get(r)
            if w is not None:
                deps.append((w, "waw"))
            for rd in self.readers.get(r, ()):
                deps.append((rd, "war"))
        keep = []
        seen = set()
        for d, kind in deps:
            if d is o or id(d) in seen:
                continue
            if (not d.is_dma) and (not dma) and d.eng == eng:
                if eng == "pe":
                    continue
            seen.add(id(d))
            keep.append(d)
        o.deps = keep
        wset = set(writes)
        for r in writes:
            self.last_w[r] = o
            self.readers[r] = []
        for r in reads:
            if r in wset:
                continue
            self.readers.setdefault(r, []).append(o)
        self.streams[eng].append(o)
        self.chan_ops.setdefault(o.chan, []).append(o)
        return o

    def finalize_on(self, eng, ops):
        self.final_waits.append((eng, list(ops)))

    def emit(self):
        nc = self.nc
        for e in ENGS:
            for o in self.streams[e]:
                for d in o.deps:
                    d.signal = True
        for eng, ops in self.final_waits:
            for d in ops:
                d.signal = True
        sems = {}
        nsem = 0
        for chan, ops in self.chan_ops.items():
            c = 0
            any_sig = False
            for o in ops:
                if o.signal:
                    c += o.inc
                    o.count = c
                    any_sig = True
            if any_sig:
                sems[chan] = nc.alloc_semaphore(name="sm%d" % nsem)
                nsem += 1
        finals = {}
        for eng, ops in self.final_waits:
            finals.setdefault(eng, []).extend(ops)
        handles = {"pe": "tensor", "act": "scalar", "dve": "vector", "pool": "gpsimd", "sp": "sync"}
        with nc.Block() as block:
            for e in ENGS:
                stream = self.streams[e]
                fin = finals.get(e, [])
                if not stream and not fin:
                    continue

                def body(engine, stream=stream, fin=fin):
                    known = {}
                    for o in stream:
                        need = {}
                        for d in o.deps:
                            dc = d.glast.count if d.glast is not None else d.count
                            if dc > need.get(d.chan, 0):
                                need[d.chan] = dc
                        for ch, cnt in need.items():
                            if known.get(ch, 0) >= cnt:
                                continue
                            engine.wait_ge(sems[ch], cnt)
                            known[ch] = cnt
                        ins = o.fn(engine)
                        if o.signal:
                            ins.then_inc(sems[o.chan], o.inc)
                    for d in fin:
                        if known.get(d.chan, 0) >= d.count:
                            continue
                        engine.wait_ge(sems[d.chan], d.count)
                        known[d.chan] = d.count

                getattr(block, handles[e])(body)


class _Stop(Exception):
    pass


def build_nc(nchunk=NCHUNK, debug=False, stage=None):
    nc = bass.Bass("TRN2", target_bir_lowering=False)
    S = Sched(nc)

    def din(name, shape):
        return nc.dram_tensor(name, list(shape), F32, kind="ExternalInput").ap()

    x = din("x", [T, D])
    w_in = din("w_in", [D, DIN])
    w_out = din("w_out", [D, D])
    w_g = din("w_g", [D, FF])
    w_u = din("w_u", [D, FF])
    w_d = din("w_d", [FF, D])
    n1 = din("n1", [128, KC])
    n2 = din("n2", [128, KC])
    sink_l = din("sink_l", [128, 16])
    fnw = din("fnw", [128, D])
    gnw = din("gnw", [128, 512])
    w2b_d = din("w2b", [17, 1024])
    cst = din("cst", [128, 5, 128])
    mk = din("mk", [128, 2, 512])
    out = nc.dram_tensor("out", [T, D], F32, kind="ExternalOutput").ap()
    dbg = {}
    if debug:
        dbg["mT"] = nc.dram_tensor("dbg_mT", [128, KC, TT], F32, kind="ExternalOutput").ap()
        dbg["h1"] = nc.dram_tensor("dbg_h1", [128, NT, D], F32, kind="ExternalOutput").ap()
        dbg["xnT"] = nc.dram_tensor("dbg_xnT", [128, KC, TT], F32, kind="ExternalOutput").ap()

    w_in_v = w_in.rearrange("(k p) c -> p k c", p=128)
    w_out_v = w_out.rearrange("(k p) c -> p k c", p=128)
    w_g_v = w_g.rearrange("(k p) c -> p k c", p=128)
    w_u_v = w_u.rearrange("(k p) c -> p k c", p=128)
    w_d_v = w_d.rearrange("(k p) c -> p k c", p=128)

    sb = nc.alloc_sbuf_tensor
    xt = sb("xt", [128, NT, D], F32)
    xn = sb("xn", [128, D], BF16)
    actT = sb("actT", [128, KC, TT], BF16)
    wsl = [sb("wsl%d" % i, [128, KC, 512], BF16) for i in range(NSLOT)]
    mT = sb("mT", [128, KC, TT], BF16)
    kT = sb("kT", [128, 4, 128 + TT], BF16)
    vAB = sb("vAB", [128, 4, NT + 1, 2, 128], BF16)
    Sst = sb("Sst", [128, 4, 2, 512], F32)
    Sbf = sb("Sbf", [128, 4, 2, 512], BF16)
    glrT = sb("glrT", [17, TT], F32)
    c_f32 = sb("c_f32", [128, 5, 128], F32)
    identb = sb("identb", [128, 128], BF16)
    maskb = sb("maskb", [128, 2, 512], BF16)
    onesAB = sb("onesAB", [128, 2, 128], BF16)
    sinkexp = sb("sinkexp", [128, 16], F32)
    fnw_sb = sb("fnw_sb", [128, D], F32)
    gnw_sb = sb("gnw_sb", [128, 512], F32)
    w2b = sb("w2b_sb", [17, 1024], F32)
    n1_sb = sb("n1_sb", [128, KC], F32)
    n2_sb = sb("n2_sb", [128, KC], F32)
    epsc = sb("epsc", [128, 1], F32)
    stat = sb("stat", [128, 8], F32)
    bar = sb("bar", [128, 1], F32)
    bar2 = sb("bar2", [128, 1], F32)
    ffT = sb("ffT", [128, HC, TT], BF16)
    ARENA = HC * TT * 2
    off = [0]
    ffT_off = None

    def carve(name, shape, dt, base):
        n = 1
        for s_ in shape[1:]:
            n *= s_
        nbytes = n * (4 if dt == F32 else 2)
        o_ = base[0]
        base[0] += (nbytes + 63) // 64 * 64
        return (name, shape, dt, o_, nbytes)

    def bview(col0, shape):
        n = 1
        for s_ in shape[1:]:
            n *= s_
        flat = ffT[:, :, :].rearrange("p a b -> p (a b)")[:, col0:col0 + n]
        if len(shape) == 2:
            return flat
        if len(shape) == 3:
            return flat.rearrange("p (a b) -> p a b", a=shape[1])
        raise ValueError

    def fview(col0, shape):
        n = 1
        for s_ in shape[1:]:
            n *= s_
        flat = ffT[:, :, :].rearrange("p a b -> p (a b)")[:, col0:col0 + 2 * n].bitcast(F32)
        if len(shape) == 2:
            return flat
        if len(shape) == 3:
            return flat.rearrange("p (a b) -> p a b", a=shape[1])
        raise ValueError

    a_qT = bview(0, [128, 4, TT])
    a_sga = bview(2048, [128, 4, TT])
    a_pT = [[bview(4096 + (b * 4 + j) * 512, [128, 512]) for j in range(4)] for b in range(2)]
    a_t1 = fview(8192, [128, 512])
    a_t2 = fview(9216, [128, 512])
    g_qT = bview(0, [128, 2, TT])
    g_kT = bview(1024, [128, 2, TT])
    g_ktok = bview(2048, [128, NT, 256])
    g_v = bview(3072, [128, NT, 512])
    g_rs = bview(5120, [128, NT, 512])
    g_bs = bview(7168, [128, NT, 512])
    g_attm = bview(9216, [128, NT, 128])
    g_G = bview(9728, [128, NT, 512])
    g_la = fview(11776, [128, NT, 256])
    g_eg = fview(13824, [128, 2, TT])
    g_ei = fview(15872, [128, 2, TT])
    g_es = fview(17920, [128, NT, 256])

    xt_b = xt[:, :, :].rearrange("p a b -> p (a b)")

    def xbview(col0, shape):
        n = 1
        for s_ in shape[1:]:
            n *= s_
        flat = xt_b[:, col0 // 2:(col0 + n) // 2].bitcast(BF16)
        return flat.rearrange("p (a b) -> p a b", a=shape[1])

    gsets = [
        {"pf": "g_", "qT": g_qT, "kT": g_kT, "ktok": g_ktok, "v": g_v, "rs": g_rs, "bs": g_bs},
        {"pf": "gx_", "qT": xbview(0, [128, 2, TT]), "kT": xbview(1024, [128, 2, TT]),
         "ktok": xbview(2048, [128, NT, 256]), "v": xbview(3072, [128, NT, 512]),
         "rs": xbview(5120, [128, NT, 512]), "bs": xbview(7168, [128, NT, 512])},
    ]

    ps = [nc.alloc_psum_tensor("ps%d" % i, [128, 512], F32) for i in range(NPS)]
    st = {"ps": 0, "w": 0, "ev": 0, "tr": 0}

    def nps():
        b = st["ps"] % NPS
        st["ps"] += 1
        return b

    def trview(b):
        return ps[b][:, 0:256].bitcast(BF16).rearrange("p (a b) -> p a b", a=4)

    WSUB = 8
    wsl.append(mT)
    MT_ALL = [("mT", g_, i_) for g_ in range(4) for i_ in range(NT)]

    def wres(slot):
        if slot == 2:
            return MT_ALL
        return [("w", slot, q) for q in range(WSUB)]

    def load_w(parts, ring=(0, 1)):
        slot = ring[st["w"] % len(ring)]
        st["w"] += 1
        P = len(parts)
        ops = []
        alln = wres(slot)
        for i, (dst_fn, src) in enumerate(parts):
            names = [alln[q] for q in range(len(alln)) if q % P == i]
            ops.append(S.op("pool", (lambda e, dst_fn=dst_fn, src=src, slot=slot: e.dma_start(out=dst_fn(wsl[slot]),
                                                                                           in_=src)),
                            writes=names, dma=True, chan=("w", slot)))
        for o_ in ops:
            o_.glast = ops[-1]
        return slot

    def mm(mms, reads, writes):
        def fn(e):
            ins = None
            for (o_, l_, r_, s0, s1) in mms:
                ins = e.matmul(o_, lhsT=l_, rhs=r_, start=s0, stop=s1)
            return ins
        return S.op("pe", fn, reads=reads, writes=writes)

    def ev_engine():
        st["ev"] += 1
        return "act" if st["ev"] % 2 == 0 else "dve"

    def copy_ev(out_ap, in_ap, reads, writes, scale=None, eng=None):
        eng = eng or ev_engine()
        if eng == "act":
            if scale is None:
                S.op("act", lambda e: e.copy(out=out_ap, in_=in_ap), reads=reads, writes=writes)
            else:
                S.op("act", lambda e: e.activation(out=out_ap, in_=in_ap, func=AF.Copy, scale=scale),
                     reads=reads, writes=writes)
        else:
            if scale is None:
                S.op("dve", lambda e: e.tensor_copy(out=out_ap, in_=in_ap), reads=reads, writes=writes)
            else:
                S.op("dve", lambda e: e.tensor_scalar(out=out_ap, in0=in_ap, scalar1=scale, scalar2=None,
                                                        op0=ALU.mult), reads=reads, writes=writes)

    arena_names = set()

    def ar(*names):
        for n_ in names:
            arena_names.add(n_)
        return list(names)

    def barrier():
        S.op("dve", lambda e: e.memset(bar[:], 0.0), writes=["ARENA", "bar"])

    S.op("sp", lambda e: e.dma_start(out=c_f32[:], in_=cst), writes=["c_f32"], dma=True)
    S.op("sp", lambda e: e.dma_start(out=n1_sb[:], in_=n1), writes=["n1"], dma=True)
    S.op("sp", lambda e: e.dma_start(out=n2_sb[:], in_=n2), writes=["n2"], dma=True)
    S.op("sp", lambda e: e.dma_start(out=sinkexp[:], in_=sink_l), writes=["sinkexp"], dma=True)
    S.op("sp", lambda e: e.dma_start(out=fnw_sb[:], in_=fnw), writes=["fnw"], dma=True)
    S.op("sp", lambda e: e.dma_start(out=gnw_sb[:], in_=gnw), writes=["gnw"], dma=True)
    S.op("sp", lambda e: e.dma_start(out=w2b[:], in_=w2b_d), writes=["w2b"], dma=True)
    mk_tmp = fview(0, [128, 2, 512])
    S.op("sp", lambda e: e.dma_start(out=mk_tmp, in_=mk), writes=ar("mk_tmp"), dma=True)
    S.op("dve", lambda e: e.tensor_copy(out=maskb[:], in_=mk_tmp), reads=["mk_tmp"], writes=["maskb"])
    S.op("dve", lambda e: e.tensor_copy(out=identb[:], in_=c_f32[:, 0, :]), reads=["c_f32"], writes=["identb"])
    S.op("act", lambda e: e.activation(out=sinkexp[:], in_=sinkexp[:], func=AF.Exp), reads=["sinkexp"],
         writes=["sinkexp"])
    S.op("dve", lambda e: e.memset(epsc[:], EPS), writes=["epsc"])
    S.op("dve", lambda e: e.memset(onesAB[:].rearrange("p a b -> p (a b)"), 0.0), writes=["onesAB"])
    S.op("dve", lambda e: e.memset(onesAB[:, 0, 0:64], 1.0), writes=["onesAB"])
    S.op("dve", lambda e: e.memset(onesAB[:, 1, 64:128], 1.0), writes=["onesAB"])
    S.op("dve", lambda e: e.memset(vAB[:].rearrange("p a b c d -> p (a b c d)"), 0.0), writes=[("v", k_, t_) for k_ in range(4) for t_ in range(NT + 1)])
    S.op("dve", lambda e: e.memset(kT[:].rearrange("p a b -> p (a b)"), 0.0), writes=[("kT", k_, t_) for k_ in range(4) for t_ in range(NT + 1)])
    S.op("dve", lambda e: e.memset(Sst[:].rearrange("p a b c -> p (a b c)"), 0.0), writes=[("S", h_, d_) for h_ in range(4) for d_ in range(2)])
    S.op("dve", lambda e: e.memset(Sbf[:].rearrange("p a b c -> p (a b c)"), 0.0), writes=[("Sb", h_, d_) for h_ in range(4) for d_ in range(2)])
    S.op("dve", lambda e: e.memset(glrT[:], 1.0), writes=["glrT"])
    barrier()

    identf = c_f32[:, 0, :]
    Uc = c_f32[:, 1, :]
    Usuf = c_f32[:, 2, :]
    tril = c_f32[:, 3, :]

    def transposes_to_actT(i, nw_sb, nwname):
        for g in range(4):
            half = nps()
            tv = trview(half)

            def tr(e, g=g, tv=tv):
                ins = None
                for j in range(4):
                    k = g * 4 + j
                    ins = e.transpose(out=tv[:, j, :], in_=xn[:, k * 128:(k + 1) * 128], identity=identb[:])
                return ins
            S.op("pe", tr, reads=["xn", "identb"], writes=[("ps", half)])
            for j in range(4):
                k = g * 4 + j
                eng = ev_engine()
                o_ap = actT[:, k, i * 128:(i + 1) * 128]
                i_ap = tv[:, j, :]
                sc = nw_sb[:, k:k + 1]
                if eng == "act":
                    S.op("act", lambda e, o_ap=o_ap, i_ap=i_ap, sc=sc: e.activation(out=o_ap, in_=i_ap, func=AF.Copy,
                                                                                     scale=sc),
                         reads=[("ps", half), nwname], writes=[("actT", i)])
                else:
                    S.op("dve", lambda e, o_ap=o_ap, i_ap=i_ap, sc=sc: e.tensor_scalar(out=o_ap, in0=i_ap, scalar1=sc,
                                                                                        scalar2=None, op0=ALU.mult),
                         reads=[("ps", half), nwname], writes=[("actT", i)])

    def rstd_from(ss_col, out_col, n):
        S.op("act", lambda e: e.activation(out=stat[:, 1:2], in_=ss_col, func=AF.Sqrt, scale=1.0 / n,
                                           bias=epsc[:]), reads=["st0", "epsc"], writes=["st1"])
        S.op("dve", lambda e: e.reciprocal(out=out_col, in_=stat[:, 1:2]), reads=["st1"], writes=["st2"])

    def rms_to_actT(i, nw_sb, nwname):
        S.op("act", lambda e: e.activation(out=xn[:], in_=xt[:, i, :], func=AF.Square, accum_out=stat[:, 0:1]),
             reads=[("x", i)], writes=["xn", "st0"])
        rstd_from(stat[:, 0:1], stat[:, 2:3], D)
        S.op("act", lambda e: e.activation(out=xn[:], in_=xt[:, i, :], func=AF.Copy, scale=stat[:, 2:3]),
             reads=[("x", i), "st2"], writes=["xn"])
        transposes_to_actT(i, nw_sb, nwname)

    def norm1_from_dram(cc, i):
        r0 = cc * TT + i * 128
        S.op("pool", lambda e: e.dma_start(out=xn[:], in_=x[r0:r0 + 128, :]), writes=["xn"], dma=True, chan="xn")
        S.op("act", lambda e: e.activation(out=actT[:, :, i * 128:(i + 1) * 128],
                                           in_=xn[:].rearrange("p (a b) -> p a b", a=KC), func=AF.Square,
                                           accum_out=stat[:, 0:1]),
             reads=["xn"], writes=[("actT", i), "st0"])
        rstd_from(stat[:, 0:1], stat[:, 2:3], D)
        S.op("act", lambda e: e.activation(out=xn[:], in_=xn[:], func=AF.Copy, scale=stat[:, 2:3]),
             reads=["xn", "st2"], writes=["xn"])
        transposes_to_actT(i, n1_sb, "n1")

    actT_all = [("actT", i) for i in range(NT)]

    def proj_feat(slot, col0, ncols, evac):
        b = nps()
        mm([(ps[b][0:ncols, :], wsl[slot][:, k, col0:col0 + ncols], actT[:, k, :], k == 0, k == KC - 1)
            for k in range(KC)], reads=wres(slot) + actT_all, writes=[("ps", b)])
        evac(b)

    def proj_tok(slot, col0, ncols, i, evac):
        b = nps()
        mm([(ps[b][:, 0:ncols], actT[:, k, i * 128:(i + 1) * 128], wsl[slot][:, k, col0:col0 + ncols], k == 0,
             k == KC - 1) for k in range(KC)], reads=wres(slot) + [("actT", i)], writes=[("ps", b)])
        evac(b)

    pending = []
    out_dmas = []

    def drain_final():
        if not pending:
            return
        cc, i = pending.pop(0)
        r0 = cc * TT + i * 128
        S.op("act", lambda e: e.activation(out=xn[:], in_=xt[:, i, :], func=AF.Square, accum_out=stat[:, 0:1]),
             reads=[("x", i)], writes=["xn", "st0"])
        rstd_from(stat[:, 0:1], stat[:, 2:3], D)
        S.op("act", lambda e: e.activation(out=xt[:, i, :], in_=xt[:, i, :], func=AF.Copy, scale=stat[:, 2:3]),
             reads=[("x", i), "st2"], writes=[("x", i)])
        S.op("dve", lambda e: e.tensor_tensor(out=xt[:, i, :], in0=xt[:, i, :], in1=fnw_sb[:], op=ALU.mult),
             reads=[("x", i), "fnw"], writes=[("x", i)])
        od = S.op("sp", lambda e: e.dma_start(out=out[r0:r0 + 128, :], in_=xt[:, i, :]),
                  reads=[("x", i)], writes=[("out", cc, i)], dma=True, chan=("out", i))
        out_dmas.append(od)

    def ckpt(name):
        if stage == name:
            raise _Stop()

    try:
      for c in range(nchunk):
          t0 = c * TT
          if c == 0:
              for i in range(NT):
                  norm1_from_dram(0, i)
          if debug and c == 0:
              S.op("pool", lambda e: e.dma_start(out=dbg["xnT"], in_=actT[:]), reads=actT_all, writes=["dbg_xnT"],
                   dma=True)

          ckpt('A')
          kparts = []
          for kvh in range(4):
              for dup in range(2):
                  kparts.append((
                      (lambda w_, kvh=kvh, dup=dup: w_[:, :, kvh * 128 + dup * 64:kvh * 128 + dup * 64 + 64]),
                      w_in_v[:, :, O_AK + kvh * 64:O_AK + (kvh + 1) * 64]))
          slot = load_w(kparts)
          for kvh in range(4):
              def evk(b, kvh=kvh):
                  copy_ev(kT[:, kvh, 128:128 + TT], ps[b][:, :], reads=[("ps", b)],
                          writes=[("kT", kvh, t_) for t_ in range(1, NT + 1)])
              proj_feat(slot, kvh * 128, 128, evk)
              drain_final()
          slot = load_w([((lambda w_: w_[:, :, 0:256]), w_in_v[:, :, O_AV:O_AV + 256]),
                         ((lambda w_: w_[:, :, 256:272]), w_in_v[:, :, O_GLR:O_GLR + 16])])
          for i in range(NT):
              def evv(b, i=i):
                  src = ps[b][:, 0:256].rearrange("p (a b) -> p a b", a=4)
                  wr = [("v", k_, 1 + i) for k_ in range(4)]
                  S.op("act", lambda e: e.copy(out=vAB[:, :, 1 + i, 0, 0:64], in_=src), reads=[("ps", b)], writes=wr)
                  S.op("dve", lambda e: e.tensor_copy(out=vAB[:, :, 1 + i, 1, 64:128], in_=src), reads=[("ps", b)],
                       writes=wr)
              proj_tok(slot, 0, 256, i, evv)

          def evg(b):
              copy_ev(glrT[0:16, :], ps[b][0:16, :], reads=[("ps", b)], writes=["glrT"])
          proj_feat(slot, 256, 16, evg)
          while pending:
              drain_final()

          ckpt('B')
          for kvh in range(4):
              slot = load_w([((lambda w_: w_[:, :, :]), w_in_v[:, :, O_AQ + kvh * 512:O_AQ + (kvh + 1) * 512])])
              for pr in range(4):
                  def evq(b, pr=pr):
                      copy_ev(a_qT[:, pr, :], ps[b][:, :], reads=[("ps", b)], writes=ar(("a_qT", pr)))
                  proj_feat(slot, pr * 128, 128, evq)
              slot = load_w([((lambda w_: w_[:, :, :]), w_in_v[:, :, O_GA + kvh * 512:O_GA + (kvh + 1) * 512])])
              for pr in range(4):
                  def evga(b, pr=pr):
                      S.op("act", lambda e: e.activation(out=a_sga[:, pr, :], in_=ps[b][:, :], func=AF.Sigmoid),
                           reads=[("ps", b)], writes=ar(("a_sga", pr)))
                  proj_feat(slot, pr * 128, 128, evga)
              qres = [("a_qT", pr) for pr in range(4)]
              gres = [("a_sga", pr) for pr in range(4)]
              for n in range(NT):
                  gn = c * NT + n
                  kbs = ([0] if gn > 0 else []) + [1]
                  buf = n % 2
                  for half in range(2):
                      hs = slice(half * 64, half * 64 + 64)
                      for kb in kbs:
                          kt_idx = n + kb
                          b = nps()
                          mm([(ps[b][:, :], kT[hs, kvh, kt_idx * 128:(kt_idx + 1) * 128],
                               a_qT[hs, :, n * 128:(n + 1) * 128], True, False),
                              (ps[b][:, :], identb[:], maskb[:, kb, :], False, True)],
                             reads=[("kT", kvh, kt_idx), "identb", "maskb"] + qres, writes=[("ps", b)])
                          pt = a_pT[buf][half * 2 + kb]
                          S.op("act", lambda e, pt=pt, b=b: e.activation(out=pt, in_=ps[b][:, :], func=AF.Exp,
                                                                          scale=0.125),
                               reads=[("ps", b)], writes=ar(("a_pT", buf, half * 2 + kb)))
                  bo = nps()
                  bd = nps()
                  seq = [(half, kb) for half in range(2) for kb in kbs]
                  mm([(ps[bo][:, :], vAB[:, kvh, n + kb, half, :], a_pT[buf][half * 2 + kb], idx == 0,
                       idx == len(seq) - 1) for idx, (half, kb) in enumerate(seq)],
                     reads=[("v", kvh, n + kb) for kb in kbs] + [("a_pT", buf, half * 2 + kb) for half, kb in seq],
                     writes=[("ps", bo)])
                  mm([(ps[bd][:, :], onesAB[:, half, :], a_pT[buf][half * 2 + kb], idx == 0, idx == len(seq) - 1)
                      for idx, (half, kb) in enumerate(seq)],
                     reads=["onesAB"] + [("a_pT", buf, half * 2 + kb) for half, kb in seq], writes=[("ps", bd)])
                  sk = sinkexp[:, kvh * 4:(kvh + 1) * 4].unsqueeze(2).to_broadcast([128, 4, 128])
                  t1v = a_t1.rearrange("p (a b) -> p a b", a=4)
                  S.op("dve", lambda e, bd=bd, sk=sk, t1v=t1v: e.tensor_tensor(
                      out=t1v, in0=ps[bd][:, :].rearrange("p (a b) -> p a b", a=4), in1=sk, op=ALU.add),
                      reads=[("ps", bd), "sinkexp"], writes=ar("a_t1"))
                  S.op("dve", lambda e: e.reciprocal(out=a_t1, in_=a_t1), reads=["a_t1"], writes=["a_t1"])
                  S.op("dve", lambda e, bo=bo: e.tensor_tensor(out=a_t2, in0=ps[bo][:, :], in1=a_t1, op=ALU.mult),
                       reads=[("ps", bo), "a_t1"], writes=ar("a_t2"))
                  S.op("dve", lambda e, n=n, kvh=kvh: e.tensor_tensor(
                      out=mT[:, kvh * 4:(kvh + 1) * 4, n * 128:(n + 1) * 128],
                      in0=a_t2.rearrange("p (a b) -> p a b", a=4), in1=a_sga[:, :, n * 128:(n + 1) * 128], op=ALU.mult),
                      reads=["a_t2"] + gres, writes=[("mT", kvh, n)])
          ckpt('C')
          S.op("act", lambda e: e.copy(out=kT[:, :, 0:128], in_=kT[:, :, TT:TT + 128]),
               reads=[("kT", k_, NT) for k_ in range(4)], writes=[("kT", k_, 0) for k_ in range(4)])
          S.op("act", lambda e: e.copy(out=vAB[:, :, 0, :, :], in_=vAB[:, :, NT, :, :]),
               reads=[("v", k_, NT) for k_ in range(4)], writes=[("v", k_, 0) for k_ in range(4)])
          barrier()

          ckpt('C2')
          S.op("dve", lambda e: e.memset(bar2[:], 0.0), writes=["XTS", "bar2"] + [("x", i_) for i_ in range(NT)])

          def proj_gen(h):
              sv = gsets[(h + 1) % 2]
              pf = sv["pf"]
              slot = load_w([((lambda w_: w_[:, :, 0:256]), w_in_v[:, :, O_GQ + h * 256:O_GQ + (h + 1) * 256]),
                             ((lambda w_: w_[:, :, 256:512]), w_in_v[:, :, O_GK + h * 256:O_GK + (h + 1) * 256])])
              for dk in range(2):
                  def evq(b, dk=dk):
                      copy_ev(sv["qT"][:, dk, :], ps[b][:, :], reads=[("ps", b)], writes=[(pf + "qT", dk)],
                              scale=1.0 / 16.0)
                  proj_feat(slot, dk * 128, 128, evq)
                  yield

                  def evk(b, dk=dk):
                      copy_ev(sv["kT"][:, dk, :], ps[b][:, :], reads=[("ps", b)], writes=[(pf + "kT", dk)])
                  proj_feat(slot, 256 + dk * 128, 128, evk)
                  yield
              for i in range(NT):
                  def evkt(b, i=i):
                      copy_ev(sv["ktok"][:, i, :], ps[b][:, 0:256], reads=[("ps", b)], writes=[(pf + "ktok", i)])
                  proj_tok(slot, 256, 256, i, evkt)
                  yield
              slot = load_w([((lambda w_: w_[:, :, :]), w_in_v[:, :, O_GV + h * 512:O_GV + (h + 1) * 512])])
              for i in range(NT):
                  def evv(b, i=i):
                      copy_ev(sv["v"][:, i, :], ps[b][:, :], reads=[("ps", b)], writes=[(pf + "v", i)])
                  proj_tok(slot, 0, 512, i, evv)
                  yield
              slot = load_w([((lambda w_: w_[:, :, :]), w_in_v[:, :, O_GR + h * 512:O_GR + (h + 1) * 512])])
              for i in range(NT):
                  def evr(b, i=i):
                      S.op("act", lambda e: e.activation(out=sv["rs"][:, i, :], in_=ps[b][:, :], func=AF.Silu),
                           reads=[("ps", b)], writes=[(pf + "rs", i)])
                  proj_tok(slot, 0, 512, i, evr)
                  yield
              slot = load_w([((lambda w_: w_[:, :, :]), w_in_v[:, :, O_GB + h * 512:O_GB + (h + 1) * 512])])
              for i in range(NT):
                  def evb(b, i=i):
                      S.op("act", lambda e: e.activation(out=sv["bs"][:, i, :], in_=ps[b][:, :], func=AF.Sigmoid),
                           reads=[("ps", b)], writes=[(pf + "bs", i)])
                      S.op("dve", lambda e: e.tensor_tensor(out=sv["rs"][:, i, :], in0=sv["rs"][:, i, :],
                                                            in1=sv["bs"][:, i, :], op=ALU.mult),
                           reads=[(pf + "rs", i), (pf + "bs", i)], writes=[(pf + "rs", i)])
                      S.op("dve", lambda e: e.tensor_tensor(out=sv["rs"][:, i, :], in0=sv["rs"][:, i, :],
                                                            in1=gnw_sb[:], op=ALU.mult),
                           reads=[(pf + "rs", i), "gnw"], writes=[(pf + "rs", i)])
                  proj_tok(slot, 0, 512, i, evb)
                  yield

          def chain_gen(h):
              sv = gsets[(h + 1) % 2]
              pf = sv["pf"]
              qT_, kT_, ktok_, v_, rs_ = sv["qT"], sv["kT"], sv["ktok"], sv["v"], sv["rs"]
              for i in range(NT):
                  b = nps()
                  mm([(ps[b][:, 0:256], glrT[0:17, i * 128:(i + 1) * 128], w2b[0:17, h * 256:(h + 1) * 256], True, True)],
                     reads=["glrT", "w2b"], writes=[("ps", b)])
                  S.op("act", lambda e, b=b, i=i: e.activation(out=g_la[:, i, :], in_=ps[b][:, 0:256], func=AF.Exp,
                                                               scale=-1.0),
                       reads=[("ps", b)], writes=[("g_la", i)])
                  S.op("act", lambda e, i=i: e.activation(out=g_la[:, i, :], in_=g_la[:, i, :], func=AF.Ln, bias=1.0),
                       reads=[("g_la", i)], writes=[("g_la", i)])
              la_all = [("g_la", i) for i in range(NT)]
              for dk in range(2):
                  b = nps()
                  mm([(ps[b][:, i * 128:(i + 1) * 128], g_la[:, i, dk * 128:(dk + 1) * 128], Uc, True, True)
                      for i in range(NT)], reads=la_all + ["c_f32"], writes=[("ps", b)])
                  S.op("act", lambda e, b=b, dk=dk: e.activation(out=g_eg[:, dk, :], in_=ps[b][:, :], func=AF.Exp),
                       reads=[("ps", b)], writes=[("g_eg", dk)])
                  S.op("act", lambda e, b=b, dk=dk: e.activation(out=g_ei[:, dk, :], in_=ps[b][:, :], func=AF.Exp,
                                                                 scale=-1.0),
                       reads=[("ps", b)], writes=[("g_ei", dk)])
              for i2 in range(NT // 2):
                  b = nps()
                  mm([(ps[b][:, j * 256:(j + 1) * 256], Usuf, g_la[:, i2 * 2 + j, :], True, True) for j in range(2)],
                     reads=la_all + ["c_f32"], writes=[("ps", b)])
                  S.op("act", lambda e, b=b, i2=i2: e.activation(
                      out=g_es[:, i2 * 2:i2 * 2 + 2, :], in_=ps[b][:, :].rearrange("p (a b) -> p a b", a=2), func=AF.Exp),
                      reads=[("ps", b)], writes=[("g_es", i2)])
              yield
              for dk in range(2):
                  S.op("dve", lambda e, dk=dk: e.tensor_tensor(out=qT_[:, dk, :], in0=qT_[:, dk, :],
                                                               in1=g_eg[:, dk, :], op=ALU.mult),
                       reads=[(pf + "qT", dk), ("g_eg", dk)], writes=[(pf + "qT", dk)])
                  S.op("dve", lambda e, dk=dk: e.tensor_tensor(out=kT_[:, dk, :], in0=kT_[:, dk, :],
                                                               in1=g_ei[:, dk, :], op=ALU.mult),
                       reads=[(pf + "kT", dk), ("g_ei", dk)], writes=[(pf + "kT", dk)])
              for i2 in range(NT // 2):
                  S.op("dve", lambda e, i2=i2: e.tensor_tensor(out=ktok_[:, i2 * 2:i2 * 2 + 2, :],
                                                               in0=ktok_[:, i2 * 2:i2 * 2 + 2, :],
                                                               in1=g_es[:, i2 * 2:i2 * 2 + 2, :], op=ALU.mult),
                       reads=[(pf + "ktok", i2 * 2), (pf + "ktok", i2 * 2 + 1), ("g_es", i2)],
                       writes=[(pf + "ktok", i2 * 2), (pf + "ktok", i2 * 2 + 1)])
              b = nps()
              mm([(ps[b][:, i * 128:(i + 1) * 128], kT_[:, dk, i * 128:(i + 1) * 128],
                   qT_[:, dk, i * 128:(i + 1) * 128], dk == 0, dk == 1) for i in range(NT) for dk in range(2)],
                 reads=[(pf + "qT", 0), (pf + "qT", 1), (pf + "kT", 0), (pf + "kT", 1)], writes=[("ps", b)])
              trb = tril.unsqueeze(1).to_broadcast([128, NT, 128])
              S.op("dve", lambda e, b=b, trb=trb: e.tensor_tensor(
                  out=g_attm[:, :, :], in0=ps[b][:, :].rearrange("p (a b) -> p a b", a=NT), in1=trb, op=ALU.mult),
                  reads=[("ps", b), "c_f32"], writes=["g_attm"])
              yield
              for i in range(NT):
                  bo = nps()
                  mm([(ps[bo][:, :], g_attm[:, i, :], v_[:, i, :], True, False),
                      (ps[bo][:, :], qT_[:, 0, i * 128:(i + 1) * 128], Sbf[:, h, 0, :], False, False),
                      (ps[bo][:, :], qT_[:, 1, i * 128:(i + 1) * 128], Sbf[:, h, 1, :], False, True)],
                     reads=["g_attm", (pf + "v", i), (pf + "qT", 0), (pf + "qT", 1), ("Sb", h, 0), ("Sb", h, 1)],
                     writes=[("ps", bo)])
                  S.op("act", lambda e, bo=bo, i=i: e.activation(out=g_G[:, i, :], in_=ps[bo][:, :], func=AF.Square,
                                                                 accum_out=stat[:, 4:5]),
                       reads=[("ps", bo)], writes=[("g_G", i), "st4"])
                  S.op("act", lambda e: e.activation(out=stat[:, 5:6], in_=stat[:, 4:5], func=AF.Sqrt, scale=1.0 / 512,
                                                     bias=epsc[:]), reads=["st4", "epsc"], writes=["st5"])
                  S.op("dve", lambda e: e.reciprocal(out=stat[:, 6:7], in_=stat[:, 5:6]), reads=["st5"], writes=["st6"])
                  S.op("dve", lambda e, bo=bo, i=i: e.scalar_tensor_tensor(
                      out=g_G[:, i, :], in0=ps[bo][:, :], scalar=stat[:, 6:7], in1=rs_[:, i, :], op0=ALU.mult,
                      op1=ALU.mult), reads=[("ps", bo), "st6", (pf + "rs", i)], writes=[("g_G", i)])
                  half = nps()
                  tvg = trview(half)

                  def trg(e, i=i, tvg=tvg):
                      ins = None
                      for j in range(4):
                          ins = e.transpose(out=tvg[:, j, :], in_=g_G[:, i, j * 128:(j + 1) * 128],
                                            identity=identb[:])
                      return ins
                  S.op("pe", trg, reads=[("g_G", i), "identb"], writes=[("ps", half)])
                  S.op("dve", lambda e, i=i, tvg=tvg, h=h: e.tensor_tensor(
                      out=mT[:, h * 4:(h + 1) * 4, i * 128:(i + 1) * 128], in0=tvg,
                      in1=mT[:, h * 4:(h + 1) * 4, i * 128:(i + 1) * 128], op=ALU.add),
                      reads=[("ps", half), ("mT", h, i)], writes=[("mT", h, i)])
                  yield
                  for dk in range(2):
                      bu = nps()
                      mm([(ps[bu][:, :], ktok_[:, i, dk * 128:(dk + 1) * 128], v_[:, i, :], True, True)],
                         reads=[(pf + "ktok", i), (pf + "v", i)], writes=[("ps", bu)])
                      S.op("dve", lambda e, bu=bu, dk=dk, i=i, h=h: e.scalar_tensor_tensor(
                          out=Sst[:, h, dk, :], in0=Sst[:, h, dk, :], scalar=g_eg[:, dk, i * 128 + 127:i * 128 + 128],
                          in1=ps[bu][:, :], op0=ALU.mult, op1=ALU.add),
                          reads=[("ps", bu), ("S", h, dk), ("g_eg", dk)], writes=[("S", h, dk)])
                      S.op("act", lambda e, dk=dk, h=h: e.copy(out=Sbf[:, h, dk, :], in_=Sst[:, h, dk, :]),
                           reads=[("S", h, dk)], writes=[("Sb", h, dk)])
                  yield

          for _ in proj_gen(0):
              pass
          for h in range(4):
              pg = proj_gen(h + 1) if h < 3 else None
              for _ in chain_gen(h):
                  if pg is not None:
                      for _k in range(2):
                          next(pg, None)
              if pg is not None:
                  for _ in pg:
                      pass
              if h == 2:
                  S.op("dve", lambda e: e.memset(bar2[:], 0.0),
                       writes=["XTS", "bar2"] + [("x", i_) for i_ in range(NT)])
                  for i in range(NT):
                      S.op("sp", lambda e, i=i, t0=t0: e.dma_start(out=xt[:, i, :],
                                                                   in_=x[t0 + i * 128:t0 + (i + 1) * 128, :]),
                           writes=[("x", i)], dma=True, chan=("x", i))
          barrier()
          if debug and c == 0:
              S.op("pool", lambda e: e.dma_start(out=dbg["mT"], in_=mT[:]),
                   reads=[("mT", g_, i_) for g_ in range(4) for i_ in range(NT)], writes=["dbg_mT"], dma=True)

          ckpt('D')
          for g in range(4):
              slot = load_w([((lambda w_: w_[:, :, :]), w_out_v[:, :, g * 512:(g + 1) * 512])])
              for i in range(NT):
                  b = nps()
                  mm([(ps[b][:, :], mT[:, k, i * 128:(i + 1) * 128], wsl[slot][:, k, :], k == 0, k == KC - 1)
                      for k in range(KC)], reads=wres(slot) + [("mT", g_, i) for g_ in range(4)], writes=[("ps", b)])
                  S.op("dve", lambda e, b=b, i=i, g=g: e.tensor_tensor(
                      out=xt[:, i, g * 512:(g + 1) * 512], in0=ps[b][:, :], in1=xt[:, i, g * 512:(g + 1) * 512],
                      op=ALU.add), reads=[("ps", b), ("x", i)], writes=[("x", i)])
          if debug and c == 0:
              S.op("sp", lambda e: e.dma_start(out=dbg["h1"], in_=xt[:]), reads=[("x", i_) for i_ in range(NT)],
                   writes=["dbg_h1"], dma=True)
          for i in range(NT):
              rms_to_actT(i, n2_sb, "n2")

          ckpt('E')
          for j in range(HC // 2):
              slot = load_w([((lambda w_: w_[:, :, 0:256]), w_g_v[:, :, j * 256:(j + 1) * 256]),
                             ((lambda w_: w_[:, :, 256:512]), w_u_v[:, :, j * 256:(j + 1) * 256])])
              for sub in range(2):
                  hc = j * 2 + sub
                  bg = nps()
                  mm([(ps[bg][:, :], wsl[slot][:, k, sub * 128:(sub + 1) * 128], actT[:, k, :], k == 0, k == KC - 1)
                      for k in range(KC)], reads=wres(slot) + actT_all, writes=[("ps", bg)])
                  bu = nps()
                  mm([(ps[bu][:, :], wsl[slot][:, k, 256 + sub * 128:256 + (sub + 1) * 128], actT[:, k, :], k == 0,
                       k == KC - 1) for k in range(KC)], reads=wres(slot) + actT_all, writes=[("ps", bu)])
                  S.op("act", lambda e, bg=bg, hc=hc: e.activation(out=ffT[:, hc, :], in_=ps[bg][:, :], func=AF.Silu),
                       reads=[("ps", bg)], writes=ar(("ffT", hc)))
                  S.op("dve", lambda e, bu=bu, hc=hc: e.tensor_tensor(out=ffT[:, hc, :], in0=ps[bu][:, :],
                                                                      in1=ffT[:, hc, :], op=ALU.mult),
                       reads=[("ps", bu), ("ffT", hc)], writes=[("ffT", hc)])

          ckpt('F')
          NPIECE = 4
          PH = HC // NPIECE
          for g in range(4):
              banks = [nps() for _ in range(NT)]
              for p in range(NPIECE):
                  slot = load_w([((lambda w_: w_[:, 0:PH, :]), w_d_v[:, p * PH:(p + 1) * PH, g * 512:(g + 1) * 512])],
                              ring=(0, 1, 2))
                  for i in range(NT):
                      b = banks[i]
                      mm([(ps[b][:, :], ffT[:, p * PH + q, i * 128:(i + 1) * 128], wsl[slot][:, q, :],
                           (p == 0 and q == 0), (p == NPIECE - 1 and q == PH - 1)) for q in range(PH)],
                         reads=wres(slot) + [("ffT", p * PH + q) for q in range(PH)], writes=[("ps", b)])
              for i in range(NT):
                  b = banks[i]
                  S.op("dve", lambda e, b=b, i=i, g=g: e.tensor_tensor(
                      out=xt[:, i, g * 512:(g + 1) * 512], in0=ps[b][:, :], in1=xt[:, i, g * 512:(g + 1) * 512],
                      op=ALU.add), reads=[("ps", b), ("x", i)], writes=[("x", i)])
              if c + 1 < nchunk:
                  norm1_from_dram(c + 1, g)

          ckpt('G')
          for i in range(NT):
              pending.append((c, i))
          if c == nchunk - 1:
              while pending:
                  drain_final()
          barrier()

    except _Stop:
        pass
    S.finalize_on("sp", out_dmas)
    if stage is not None:
        S.finalize_on("sp", [o for o in S.streams["sp"] if o.is_dma])
        S.finalize_on("pool", [o for o in S.streams["pool"] if o.is_dma])
    if debug:
        dd = [o for o in S.streams["pool"] if o.is_dma and o.chan[1] in ("dbg_mT", "dbg_xnT")]
        S.finalize_on("pool", dd)
        dd2 = [o for o in S.streams["sp"] if o.is_dma and o.chan[1] == "dbg_h1"]
        S.finalize_on("sp", dd2)
    S.emit()
    return nc


def _consts():
    j = np.arange(128)[:, None]
    i = np.arange(128)[None, :]
    cst = np.zeros((128, 5, 128), np.float32)
    cst[:, 0, :] = np.eye(128, dtype=np.float32)
    cst[:, 1, :] = np.where(j <= i, -1.0 / 16.0, 0.0)
    cst[:, 2, :] = np.where(j > i, -1.0 / 16.0, 0.0)
    cst[:, 3, :] = np.where(j <= i, 1.0, 0.0)
    mk = np.zeros((128, 2, 512), np.float32)
    mprev = np.where(j > i, 0.0, NEG).astype(np.float32)
    mcur = np.where(j <= i, 0.0, NEG).astype(np.float32)
    mk[:, 0, :] = np.tile(mprev, (1, 4))
    mk[:, 1, :] = np.tile(mcur, (1, 4))
    return cst, mk


def _col_layout(v):
    return np.ascontiguousarray(np.asarray(v, np.float32).reshape(KC, 128).T)


def make_in_maps(inputs, ncores=8):
    f = lambda a: np.ascontiguousarray(np.asarray(a, dtype=np.float32))
    x = f(inputs["x"])
    cst, mk = _consts()
    sinks = f(inputs["attn_sinks"])[0]
    sink_l = np.zeros((128, 16), np.float32)
    for cch in range(16):
        sink_l[0:64, cch] = sinks[2 * cch]
        sink_l[64:128, cch] = sinks[2 * cch + 1]
    shared = {
        "w_in": f(inputs["w_in"])[0],
        "w_out": f(inputs["w_out"])[0],
        "w_g": f(inputs["w_ffn_gate"])[0],
        "w_u": f(inputs["w_ffn_up"])[0],
        "w_d": f(inputs["w_ffn_down"])[0],
        "n1": _col_layout(f(inputs["norm1_w"])[0]),
        "n2": _col_layout(f(inputs["norm2_w"])[0]),
        "sink_l": sink_l,
        "fnw": np.ascontiguousarray(np.broadcast_to(f(inputs["final_norm_w"])[None, :], (128, D))),
        "gnw": np.ascontiguousarray(np.broadcast_to(f(inputs["gla_norm_w"])[0][None, :], (128, 512))),
        "w2b": np.ascontiguousarray(np.concatenate([f(inputs["gla_gate_w2"])[0], f(inputs["gla_gate_b"])[0][None, :]],
                                                   axis=0)),
        "cst": cst,
        "mk": mk,
    }
    maps = []
    for b in range(ncores):
        m = dict(shared)
        m["x"] = np.ascontiguousarray(x[b])
        maps.append(m)
    return maps


def kernel(**inputs):
    nc = build_nc()
    in_maps = make_in_maps(inputs, 8)
    res = run_bass_kernel_spmd(nc, in_maps, core_ids=list(range(8)))
    return np.stack([np.asarray(r["out"], dtype=np.float32) for r in res.results], axis=0)
```

```python
import numpy as np
import concourse.bass as bass
import concourse.mybir as mybir
from concourse.bass_utils import run_bass_kernel_spmd

F32 = mybir.dt.float32
BF16 = mybir.dt.bfloat16
AF = mybir.ActivationFunctionType
ALU = mybir.AluOpType

D = 2048
T = 2048
TT = 512
NT = TT // 128
NCHUNK = T // TT
KC = D // 128
FF = 5632
HC = FF // 128
DIN = 12816
O_AQ, O_AK, O_AV, O_GQ, O_GK, O_GV, O_GLR, O_GR, O_GA, O_GB = (
    0, 2048, 2304, 2560, 3584, 4608, 6656, 6672, 8720, 10768)
EPS = 1e-6
NEG = -30000.0
NSLOT = 2
NPS = 8

ENGS = ("pe", "act", "dve", "pool", "sp")


class Op:
    __slots__ = ("eng", "fn", "deps", "chan", "inc", "signal", "count", "is_dma", "glast")


class Sched:
    def __init__(self, nc):
        self.nc = nc
        self.streams = {e: [] for e in ENGS}
        self.last_w = {}
        self.readers = {}
        self.chan_ops = {}
        self.final_waits = []

    @staticmethod
    def _is_arena(r):
        key = r[0] if isinstance(r, tuple) else r
        return isinstance(key, str) and (key.startswith("a_") or key.startswith("g_") or key in ("ffT", "mk_tmp"))

    @staticmethod
    def _is_xts(r):
        key = r[0] if isinstance(r, tuple) else r
        return isinstance(key, str) and key.startswith("gx_")

    def op(self, eng, fn, reads=(), writes=(), dma=False, chan=None):
        reads = list(reads)
        writes = list(writes)
        if any(self._is_arena(r) for r in reads) or any(self._is_arena(r) for r in writes):
            if "ARENA" not in writes:
                reads.append("ARENA")
        if any(self._is_xts(r) for r in reads) or any(self._is_xts(r) for r in writes):
            if "XTS" not in writes:
                reads.append("XTS")
        o = Op()
        o.eng = eng
        o.fn = fn
        o.is_dma = dma
        o.inc = 16 if dma else 1
        if dma:
            o.chan = ("dma", chan if chan is not None else tuple(writes)[0])
        else:
            o.chan = eng
        o.signal = bool(dma)
        o.count = None
        o.glast = None
        deps = []
        for r in reads:
            w = self.last_w.get(r)
            if w is not None:
                deps.append((w, "raw"))
        for r in writes:
            w = self.last_w.get(r)
            if w is not None:
                deps.append((w, "waw"))
            for rd in self.readers.get(r, ()):
                deps.append((rd, "war"))
        keep = []
        seen = set()
        for d, kind in deps:
            if d is o or id(d) in seen:
                continue
            if (not d.is_dma) and (not dma) and d.eng == eng:
                if eng == "pe":
                    continue
            seen.add(id(d))
            keep.append(d)
        o.deps = keep
        wset = set(writes)
        for r in writes:
            self.last_w[r] = o
            self.readers[r] = []
        for r in reads:
            if r in wset:
                continue
            self.readers.setdefault(r, []).append(o)
        self.streams[eng].append(o)
        self.chan_ops.setdefault(o.chan, []).append(o)
        return o

    def finalize_on(self, eng, ops):
        self.final_waits.append((eng, list(ops)))

    def emit(self):
        nc = self.nc
        for e in ENGS:
            for o in self.streams[e]:
                for d in o.deps:
                    d.signal = True
        for eng, ops in self.final_waits:
            for d in ops:
                d.signal = True
        sems = {}
        nsem = 0
        for chan, ops in self.chan_ops.items():
            c = 0
            any_sig = False
            for o in ops:
                if o.signal:
                    c += o.inc
                    o.count = c
                    any_sig = True
            if any_sig:
                sems[chan] = nc.alloc_semaphore(name="sm%d" % nsem)
                nsem += 1
        finals = {}
        for eng, ops in self.final_waits:
            finals.setdefault(eng, []).extend(ops)
        handles = {"pe": "tensor", "act": "scalar", "dve": "vector", "pool": "gpsimd", "sp": "sync"}
        with nc.Block() as block:
            for e in ENGS:
                stream = self.streams[e]
                fin = finals.get(e, [])
                if not stream and not fin:
                    continue

                def body(engine, stream=stream, fin=fin):
                    known = {}
                    for o in stream:
                        need = {}
                        for d in o.deps:
                            dc = d.glast.count if d.glast is not None else d.count
                            if dc > need.get(d.chan, 0):
                                need[d.chan] = dc
                        for ch, cnt in need.items():
                            if known.get(ch, 0) >= cnt:
                                continue
                            engine.wait_ge(sems[ch], cnt)
                            known[ch] = cnt
                        ins = o.fn(engine)
                        if o.signal:
                            ins.then_inc(sems[o.chan], o.inc)
                    for d in fin:
                        if known.get(d.chan, 0) >= d.count:
                            continue
                        engine.wait_ge(sems[d.chan], d.count)
                        known[d.chan] = d.count

                getattr(block, handles[e])(body)


class _Stop(Exception):
    pass


def build_nc(nchunk=NCHUNK, debug=False, stage=None):
    nc = bass.Bass("TRN2", target_bir_lowering=False)
    S = Sched(nc)

    def din(name, shape):
        return nc.dram_tensor(name, list(shape), F32, kind="ExternalInput").ap()

    x = din("x", [T, D])
    w_in = din("w_in", [D, DIN])
    w_out = din("w_out", [D, D])
    w_g = din("w_g", [D, FF])
    w_u = din("w_u", [D, FF])
    w_d = din("w_d", [FF, D])
    n1 = din("n1", [128, KC])
    n2 = din("n2", [128, KC])
    sink_l = din("sink_l", [128, 16])
    fnw = din("fnw", [128, D])
    gnw = din("gnw", [128, 512])
    w2b_d = din("w2b", [17, 1024])
    cst = din("cst", [128, 5, 128])
    mk = din("mk", [128, 2, 512])
    out = nc.dram_tensor("out", [T, D], F32, kind="ExternalOutput").ap()
    dbg = {}
    if debug:
        dbg["mT"] = nc.dram_tensor("dbg_mT", [128, KC, TT], F32, kind="ExternalOutput").ap()
        dbg["h1"] = nc.dram_tensor("dbg_h1", [128, NT, D], F32, kind="ExternalOutput").ap()
        dbg["xnT"] = nc.dram_tensor("dbg_xnT", [128, KC, TT], F32, kind="ExternalOutput").ap()

    w_in_v = w_in.rearrange("(k p) c -> p k c", p=128)
    w_out_v = w_out.rearrange("(k p) c -> p k c", p=128)
    w_g_v = w_g.rearrange("(k p) c -> p k c", p=128)
    w_u_v = w_u.rearrange("(k p) c -> p k c", p=128)
    w_d_v = w_d.rearrange("(k p) c -> p k c", p=128)

    sb = nc.alloc_sbuf_tensor
    xt = sb("xt", [128, NT, D], F32)
    xn = sb("xn", [128, D], BF16)
    actT = sb("actT", [128, KC, TT], BF16)
    wsl = [sb("wsl%d" % i, [128, KC, 512], BF16) for i in range(NSLOT)]
    mT = sb("mT", [128, KC, TT], BF16)
    kT = sb("kT", [128, 4, 128 + TT], BF16)
    vAB = sb("vAB", [128, 4, NT + 1, 2, 128], BF16)
    Sst = sb("Sst", [128, 4, 2, 512], F32)
    Sbf = sb("Sbf", [128, 4, 2, 512], BF16)
    glrT = sb("glrT", [17, TT], F32)
    c_f32 = sb("c_f32", [128, 5, 128], F32)
    identb = sb("identb", [128, 128], BF16)
    maskb = sb("maskb", [128, 2, 512], BF16)
    onesAB = sb("onesAB", [128, 2, 128], BF16)
    sinkexp = sb("sinkexp", [128, 16], F32)
    fnw_sb = sb("fnw_sb", [128, D], F32)
    gnw_sb = sb("gnw_sb", [128, 512], F32)
    w2b = sb("w2b_sb", [17, 1024], F32)
    n1_sb = sb("n1_sb", [128, KC], F32)
    n2_sb = sb("n2_sb", [128, KC], F32)
    epsc = sb("epsc", [128, 1], F32)
    stat = sb("stat", [128, 8], F32)
    bar = sb("bar", [128, 1], F32)
    bar2 = sb("bar2", [128, 1], F32)
    ffT = sb("ffT", [128, HC, TT], BF16)
    ARENA = HC * TT * 2
    off = [0]
    ffT_off = None

    def carve(name, shape, dt, base):
        n = 1
        for s_ in shape[1:]:
            n *= s_
        nbytes = n * (4 if dt == F32 else 2)
        o_ = base[0]
        base[0] += (nbytes + 63) // 64 * 64
        return (name, shape, dt, o_, nbytes)

    def bview(col0, shape):
        n = 1
        for s_ in shape[1:]:
            n *= s_
        flat = ffT[:, :, :].rearrange("p a b -> p (a b)")[:, col0:col0 + n]
        if len(shape) == 2:
            return flat
        if len(shape) == 3:
            return flat.rearrange("p (a b) -> p a b", a=shape[1])
        raise ValueError

    def fview(col0, shape):
        n = 1
        for s_ in shape[1:]:
            n *= s_
        flat = ffT[:, :, :].rearrange("p a b -> p (a b)")[:, col0:col0 + 2 * n].bitcast(F32)
        if len(shape) == 2:
            return flat
        if len(shape) == 3:
            return flat.rearrange("p (a b) -> p a b", a=shape[1])
        raise ValueError

    a_qT = bview(0, [128, 4, TT])
    a_sga = bview(2048, [128, 4, TT])
    a_pT = [[bview(4096 + (b * 4 + j) * 512, [128, 512]) for j in range(4)] for b in range(2)]
    a_t1 = fview(8192, [128, 512])
    a_t2 = fview(9216, [128, 512])
    g_qT = bview(0, [128, 2, TT])
    g_kT = bview(1024, [128, 2, TT])
    g_ktok = bview(2048, [128, NT, 256])
    g_v = bview(3072, [128, NT, 512])
    g_rs = bview(5120, [128, NT, 512])
    g_bs = bview(7168, [128, NT, 512])
    g_attm = bview(9216, [128, NT, 128])
    g_G = bview(9728, [128, NT, 512])
    g_la = fview(11776, [128, NT, 256])
    g_eg = fview(13824, [128, 2, TT])
    g_ei = fview(15872, [128, 2, TT])
    g_es = fview(17920, [128, NT, 256])

    xt_b = xt[:, :, :].rearrange("p a b -> p (a b)")

    def xbview(col0, shape):
        n = 1
        for s_ in shape[1:]:
            n *= s_
        flat = xt_b[:, col0 // 2:(col0 + n) // 2].bitcast(BF16)
        return flat.rearrange("p (a b) -> p a b", a=shape[1])

    gsets = [
        {"pf": "g_", "qT": g_qT, "kT": g_kT, "ktok": g_ktok, "v": g_v, "rs": g_rs, "bs": g_bs},
        {"pf": "gx_", "qT": xbview(0, [128, 2, TT]), "kT": xbview(1024, [128, 2, TT]),
         "ktok": xbview(2048, [128, NT, 256]), "v": xbview(3072, [128, NT, 512]),
         "rs": xbview(5120, [128, NT, 512]), "bs": xbview(7168, [128, NT, 512])},
    ]

    ps = [nc.alloc_psum_tensor("ps%d" % i, [128, 512], F32) for i in range(NPS)]
    st = {"ps": 0, "w": 0, "ev": 0, "tr": 0}

    def nps():
        b = st["ps"] % NPS
        st["ps"] += 1
        return b

    def trview(b):
        return ps[b][:, 0:256].bitcast(BF16).rearrange("p (a b) -> p a b", a=4)

    WSUB = 8
    wsl.append(mT)
    MT_ALL = [("mT", g_, i_) for g_ in range(4) for i_ in range(NT)]

    def wres(slot):
        if slot == 2:
            return MT_ALL
        return [("w", slot, q) for q in range(WSUB)]

    def load_w(parts, ring=(0, 1)):
        slot = ring[st["w"] % len(ring)]
        st["w"] += 1
        P = len(parts)
        ops = []
        alln = wres(slot)
        for i, (dst_fn, src) in enumerate(parts):
            names = [alln[q] for q in range(len(alln)) if q % P == i]
            ops.append(S.op("pool", (lambda e, dst_fn=dst_fn, src=src, slot=slot: e.dma_start(out=dst_fn(wsl[slot]),
                                                                                           in_=src)),
                            writes=names, dma=True, chan=("w", slot)))
        for o_ in ops:
            o_.glast = ops[-1]
        return slot

    def mm(mms, reads, writes):
        def fn(e):
            ins = None
            for (o_, l_, r_, s0, s1) in mms:
                ins = e.matmul(o_, lhsT=l_, rhs=r_, start=s0, stop=s1)
            return ins
        return S.op("pe", fn, reads=reads, writes=writes)

    def ev_engine():
        st["ev"] += 1
        return "act" if st["ev"] % 2 == 0 else "dve"

    def copy_ev(out_ap, in_ap, reads, writes, scale=None, eng=None):
        eng = eng or ev_engine()
        if eng == "act":
            if scale is None:
                S.op("act", lambda e: e.copy(out=out_ap, in_=in_ap), reads=reads, writes=writes)
            else:
                S.op("act", lambda e: e.activation(out=out_ap, in_=in_ap, func=AF.Copy, scale=scale),
                     reads=reads, writes=writes)
        else:
            if scale is None:
                S.op("dve", lambda e: e.tensor_copy(out=out_ap, in_=in_ap), reads=reads, writes=writes)
            else:
                S.op("dve", lambda e: e.tensor_scalar(out=out_ap, in0=in_ap, scalar1=scale, scalar2=None,
                                                        op0=ALU.mult), reads=reads, writes=writes)

    arena_names = set()

    def ar(*names):
        for n_ in names:
            arena_names.add(n_)
        return list(names)

    def barrier():
        S.op("dve", lambda e: e.memset(bar[:], 0.0), writes=["ARENA", "bar"])

    S.op("sp", lambda e: e.dma_start(out=c_f32[:], in_=cst), writes=["c_f32"], dma=True)
    S.op("sp", lambda e: e.dma_start(out=n1_sb[:], in_=n1), writes=["n1"], dma=True)
    S.op("sp", lambda e: e.dma_start(out=n2_sb[:], in_=n2), writes=["n2"], dma=True)
    S.op("sp", lambda e: e.dma_start(out=sinkexp[:], in_=sink_l), writes=["sinkexp"], dma=True)
    S.op("sp", lambda e: e.dma_start(out=fnw_sb[:], in_=fnw), writes=["fnw"], dma=True)
    S.op("sp", lambda e: e.dma_start(out=gnw_sb[:], in_=gnw), writes=["gnw"], dma=True)
    S.op("sp", lambda e: e.dma_start(out=w2b[:], in_=w2b_d), writes=["w2b"], dma=True)
    mk_tmp = fview(0, [128, 2, 512])
    S.op("sp", lambda e: e.dma_start(out=mk_tmp, in_=mk), writes=ar("mk_tmp"), dma=True)
    S.op("dve", lambda e: e.tensor_copy(out=maskb[:], in_=mk_tmp), reads=["mk_tmp"], writes=["maskb"])
    S.op("dve", lambda e: e.tensor_copy(out=identb[:], in_=c_f32[:, 0, :]), reads=["c_f32"], writes=["identb"])
    S.op("act", lambda e: e.activation(out=sinkexp[:], in_=sinkexp[:], func=AF.Exp), reads=["sinkexp"],
         writes=["sinkexp"])
    S.op("dve", lambda e: e.memset(epsc[:], EPS), writes=["epsc"])
    S.op("dve", lambda e: e.memset(onesAB[:].rearrange("p a b -> p (a b)"), 0.0), writes=["onesAB"])
    S.op("dve", lambda e: e.memset(onesAB[:, 0, 0:64], 1.0), writes=["onesAB"])
    S.op("dve", lambda e: e.memset(onesAB[:, 1, 64:128], 1.0), writes=["onesAB"])
    S.op("dve", lambda e: e.memset(vAB[:].rearrange("p a b c d -> p (a b c d)"), 0.0), writes=[("v", k_, t_) for k_ in range(4) for t_ in range(NT + 1)])
    S.op("dve", lambda e: e.memset(kT[:].rearrange("p a b -> p (a b)"), 0.0), writes=[("kT", k_, t_) for k_ in range(4) for t_ in range(NT + 1)])
    S.op("dve", lambda e: e.memset(Sst[:].rearrange("p a b c -> p (a b c)"), 0.0), writes=[("S", h_, d_) for h_ in range(4) for d_ in range(2)])
    S.op("dve", lambda e: e.memset(Sbf[:].rearrange("p a b c -> p (a b c)"), 0.0), writes=[("Sb", h_, d_) for h_ in range(4) for d_ in range(2)])
    S.op("dve", lambda e: e.memset(glrT[:], 1.0), writes=["glrT"])
    barrier()

    identf = c_f32[:, 0, :]
    Uc = c_f32[:, 1, :]
    Usuf = c_f32[:, 2, :]
    tril = c_f32[:, 3, :]

    def transposes_to_actT(i, nw_sb, nwname):
        for g in range(4):
            half = nps()
            tv = trview(half)

            def tr(e, g=g, tv=tv):
                ins = None
                for j in range(4):
                    k = g * 4 + j
                    ins = e.transpose(out=tv[:, j, :], in_=xn[:, k * 128:(k + 1) * 128], identity=identb[:])
                return ins
            S.op("pe", tr, reads=["xn", "identb"], writes=[("ps", half)])
            for j in range(4):
                k = g * 4 + j
                eng = ev_engine()
                o_ap = actT[:, k, i * 128:(i + 1) * 128]
                i_ap = tv[:, j, :]
                sc = nw_sb[:, k:k + 1]
                if eng == "act":
                    S.op("act", lambda e, o_ap=o_ap, i_ap=i_ap, sc=sc: e.activation(out=o_ap, in_=i_ap, func=AF.Copy,
                                                                                     scale=sc),
                         reads=[("ps", half), nwname], writes=[("actT", i)])
                else:
                    S.op("dve", lambda e, o_ap=o_ap, i_ap=i_ap, sc=sc: e.tensor_scalar(out=o_ap, in0=i_ap, scalar1=sc,
                                                                                        scalar2=None, op0=ALU.mult),
                         reads=[("ps", half), nwname], writes=[("actT", i)])

    def rstd_from(ss_col, out_col, n):
        S.op("act", lambda e: e.activation(out=stat[:, 1:2], in_=ss_col, func=AF.Sqrt, scale=1.0 / n,
                                           bias=epsc[:]), reads=["st0", "epsc"], writes=["st1"])
        S.op("dve", lambda e: e.reciprocal(out=out_col, in_=stat[:, 1:2]), reads=["st1"], writes=["st2"])

    def rms_to_actT(i, nw_sb, nwname):
        S.op("act", lambda e: e.activation(out=xn[:], in_=xt[:, i, :], func=AF.Square, accum_out=stat[:, 0:1]),
             reads=[("x", i)], writes=["xn", "st0"])
        rstd_from(stat[:, 0:1], stat[:, 2:3], D)
        S.op("act", lambda e: e.activation(out=xn[:], in_=xt[:, i, :], func=AF.Copy, scale=stat[:, 2:3]),
             reads=[("x", i), "st2"], writes=["xn"])
        transposes_to_actT(i, nw_sb, nwname)

    def norm1_from_dram(cc, i):
        r0 = cc * TT + i * 128
        S.op("pool", lambda e: e.dma_start(out=xn[:], in_=x[r0:r0 + 128, :]), writes=["xn"], dma=True, chan="xn")
        S.op("act", lambda e: e.activation(out=actT[:, :, i * 128:(i + 1) * 128],
                                           in_=xn[:].rearrange("p (a b) -> p a b", a=KC), func=AF.Square,
                                           accum_out=stat[:, 0:1]),
             reads=["xn"], writes=[("actT", i), "st0"])
        rstd_from(stat[:, 0:1], stat[:, 2:3], D)
        S.op("act", lambda e: e.activation(out=xn[:], in_=xn[:], func=AF.Copy, scale=stat[:, 2:3]),
             reads=["xn", "st2"], writes=["xn"])
        transposes_to_actT(i, n1_sb, "n1")

    actT_all = [("actT", i) for i in range(NT)]

    def proj_feat(slot, col0, ncols, evac):
        b = nps()
        mm([(ps[b][0:ncols, :], wsl[slot][:, k, col0:col0 + ncols], actT[:, k, :], k == 0, k == KC - 1)
            for k in range(KC)], reads=wres(slot) + actT_all, writes=[("ps", b)])
        evac(b)

    def proj_tok(slot, col0, ncols, i, evac):
        b = nps()
        mm([(ps[b][:, 0:ncols], actT[:, k, i * 128:(i + 1) * 128], wsl[slot][:, k, col0:col0 + ncols], k == 0,
             k == KC - 1) for k in range(KC)], reads=wres(slot) + [("actT", i)], writes=[("ps", b)])
        evac(b)

    pending = []
    out_dmas = []

    def drain_final():
        if not pending:
            return
        cc, i = pending.pop(0)
        r0 = cc * TT + i * 128
        S.op("act", lambda e: e.activation(out=xn[:], in_=xt[:, i, :], func=AF.Square, accum_out=stat[:, 0:1]),
             reads=[("x", i)], writes=["xn", "st0"])
        rstd_from(stat[:, 0:1], stat[:, 2:3], D)
        S.op("act", lambda e: e.activation(out=xt[:, i, :], in_=xt[:, i, :], func=AF.Copy, scale=stat[:, 2:3]),
             reads=[("x", i), "st2"], writes=[("x", i)])
        S.op("dve", lambda e: e.tensor_tensor(out=xt[:, i, :], in0=xt[:, i, :], in1=fnw_sb[:], op=ALU.mult),
             reads=[("x", i), "fnw"], writes=[("x", i)])
        od = S.op("sp", lambda e: e.dma_start(out=out[r0:r0 + 128, :], in_=xt[:, i, :]),
                  reads=[("x", i)], writes=[("out", cc, i)], dma=True, chan=("out", i))
        out_dmas.append(od)

    def ckpt(name):
        if stage == name:
            raise _Stop()

    try:
      for c in range(nchunk):
          t0 = c * TT
          if c == 0:
              for i in range(NT):
                  norm1_from_dram(0, i)
          if debug and c == 0:
              S.op("pool", lambda e: e.dma_start(out=dbg["xnT"], in_=actT[:]), reads=actT_all, writes=["dbg_xnT"],
                   dma=True)

          ckpt('A')
          kparts = []
          for kvh in range(4):
              for dup in range(2):
                  kparts.append((
                      (lambda w_, kvh=kvh, dup=dup: w_[:, :, kvh * 128 + dup * 64:kvh * 128 + dup * 64 + 64]),
                      w_in_v[:, :, O_AK + kvh * 64:O_AK + (kvh + 1) * 64]))
          slot = load_w(kparts)
          for kvh in range(4):
              def evk(b, kvh=kvh):
                  copy_ev(kT[:, kvh, 128:128 + TT], ps[b][:, :], reads=[("ps", b)],
                          writes=[("kT", kvh, t_) for t_ in range(1, NT + 1)])
              proj_feat(slot, kvh * 128, 128, evk)
              drain_final()
          slot = load_w([((lambda w_: w_[:, :, 0:256]), w_in_v[:, :, O_AV:O_AV + 256]),
                         ((lambda w_: w_[:, :, 256:272]), w_in_v[:, :, O_GLR:O_GLR + 16])])
          for i in range(NT):
              def evv(b, i=i):
                  src = ps[b][:, 0:256].rearrange("p (a b) -> p a b", a=4)
                  wr = [("v", k_, 1 + i) for k_ in range(4)]
                  S.op("act", lambda e: e.copy(out=vAB[:, :, 1 + i, 0, 0:64], in_=src), reads=[("ps", b)], writes=wr)
                  S.op("dve", lambda e: e.tensor_copy(out=vAB[:, :, 1 + i, 1, 64:128], in_=src), reads=[("ps", b)],
                       writes=wr)
              proj_tok(slot, 0, 256, i, evv)

          def evg(b):
              copy_ev(glrT[0:16, :], ps[b][0:16, :], reads=[("ps", b)], writes=["glrT"])
          proj_feat(slot, 256, 16, evg)
          while pending:
              drain_final()

          ckpt('B')
          for kvh in range(4):
              slot = load_w([((lambda w_: w_[:, :, :]), w_in_v[:, :, O_AQ + kvh * 512:O_AQ + (kvh + 1) * 512])])
              for pr in range(4):
                  def evq(b, pr=pr):
                      copy_ev(a_qT[:, pr, :], ps[b][:, :], reads=[("ps", b)], writes=ar(("a_qT", pr)))
                  proj_feat(slot, pr * 128, 128, evq)
              slot = load_w([((lambda w_: w_[:, :, :]), w_in_v[:, :, O_GA + kvh * 512:O_GA + (kvh + 1) * 512])])
              for pr in range(4):
                  def evga(b, pr=pr):
                      S.op("act", lambda e: e.activation(out=a_sga[:, pr, :], in_=ps[b][:, :], func=AF.Sigmoid),
                           reads=[("ps", b)], writes=ar(("a_sga", pr)))
                  proj_feat(slot, pr * 128, 128, evga)
              qres = [("a_qT", pr) for pr in range(4)]
              gres = [("a_sga", pr) for pr in range(4)]
              def att_front(n, kvh=kvh, qres=qres):
                  gn = c * NT + n
                  kbs = ([0] if gn > 0 else []) + [1]
                  buf = n % 2
                  for half in range(2):
                      hs = slice(half * 64, half * 64 + 64)
                      for kb in kbs:
                          kt_idx = n + kb
                          b = nps()
                          mm([(ps[b][:, :], kT[hs, kvh, kt_idx * 128:(kt_idx + 1) * 128],
                               a_qT[hs, :, n * 128:(n + 1) * 128], True, False),
                              (ps[b][:, :], identb[:], maskb[:, kb, :], False, True)],
                             reads=[("kT", kvh, kt_idx), "identb", "maskb"] + qres, writes=[("ps", b)])
                          pt = a_pT[buf][half * 2 + kb]
                          S.op("act", lambda e, pt=pt, b=b: e.activation(out=pt, in_=ps[b][:, :], func=AF.Exp,
                                                                          scale=0.125),
                               reads=[("ps", b)], writes=ar(("a_pT", buf, half * 2 + kb)))

              def att_back(n, kvh=kvh, gres=gres):
                  gn = c * NT + n
                  kbs = ([0] if gn > 0 else []) + [1]
                  buf = n % 2
                  bo = nps()
                  bd = nps()
                  seq = [(half, kb) for half in range(2) for kb in kbs]
                  mm([(ps[bo][:, :], vAB[:, kvh, n + kb, half, :], a_pT[buf][half * 2 + kb], idx == 0,
                       idx == len(seq) - 1) for idx, (half, kb) in enumerate(seq)],
                     reads=[("v", kvh, n + kb) for kb in kbs] + [("a_pT", buf, half * 2 + kb) for half, kb in seq],
                     writes=[("ps", bo)])
                  mm([(ps[bd][:, :], onesAB[:, half, :], a_pT[buf][half * 2 + kb], idx == 0, idx == len(seq) - 1)
                      for idx, (half, kb) in enumerate(seq)],
                     reads=["onesAB"] + [("a_pT", buf, half * 2 + kb) for half, kb in seq], writes=[("ps", bd)])
                  sk = sinkexp[:, kvh * 4:(kvh + 1) * 4].unsqueeze(2).to_broadcast([128, 4, 128])
                  t1v = a_t1.rearrange("p (a b) -> p a b", a=4)
                  S.op("dve", lambda e, bd=bd, sk=sk, t1v=t1v: e.tensor_tensor(
                      out=t1v, in0=ps[bd][:, :].rearrange("p (a b) -> p a b", a=4), in1=sk, op=ALU.add),
                      reads=[("ps", bd), "sinkexp"], writes=ar("a_t1"))
                  S.op("dve", lambda e: e.reciprocal(out=a_t1, in_=a_t1), reads=["a_t1"], writes=["a_t1"])
                  S.op("dve", lambda e, bo=bo: e.tensor_tensor(out=a_t2, in0=ps[bo][:, :], in1=a_t1, op=ALU.mult),
                       reads=[("ps", bo), "a_t1"], writes=ar("a_t2"))
                  S.op("dve", lambda e, n=n, kvh=kvh: e.tensor_tensor(
                      out=mT[:, kvh * 4:(kvh + 1) * 4, n * 128:(n + 1) * 128],
                      in0=a_t2.rearrange("p (a b) -> p a b", a=4), in1=a_sga[:, :, n * 128:(n + 1) * 128], op=ALU.mult),
                      reads=["a_t2"] + gres, writes=[("mT", kvh, n)])

              att_front(0)
              for n in range(NT):
                  if n + 1 < NT:
                      att_front(n + 1)
                  att_back(n)
          ckpt('C')
          S.op("act", lambda e: e.copy(out=kT[:, :, 0:128], in_=kT[:, :, TT:TT + 128]),
               reads=[("kT", k_, NT) for k_ in range(4)], writes=[("kT", k_, 0) for k_ in range(4)])
          S.op("act", lambda e: e.copy(out=vAB[:, :, 0, :, :], in_=vAB[:, :, NT, :, :]),
               reads=[("v", k_, NT) for k_ in range(4)], writes=[("v", k_, 0) for k_ in range(4)])
          barrier()

          ckpt('C2')
          S.op("dve", lambda e: e.memset(bar2[:], 0.0), writes=["XTS", "bar2"] + [("x", i_) for i_ in range(NT)])

          def proj_gen(h):
              sv = gsets[(h + 1) % 2]
              pf = sv["pf"]
              slot = load_w([((lambda w_: w_[:, :, 0:256]), w_in_v[:, :, O_GQ + h * 256:O_GQ + (h + 1) * 256]),
                             ((lambda w_: w_[:, :, 256:512]), w_in_v[:, :, O_GK + h * 256:O_GK + (h + 1) * 256])])
              for dk in range(2):
                  def evq(b, dk=dk):
                      copy_ev(sv["qT"][:, dk, :], ps[b][:, :], reads=[("ps", b)], writes=[(pf + "qT", dk)],
                              scale=1.0 / 16.0)
                  proj_feat(slot, dk * 128, 128, evq)
                  yield

                  def evk(b, dk=dk):
                      copy_ev(sv["kT"][:, dk, :], ps[b][:, :], reads=[("ps", b)], writes=[(pf + "kT", dk)])
                  proj_feat(slot, 256 + dk * 128, 128, evk)
                  yield
              for i in range(NT):
                  def evkt(b, i=i):
                      copy_ev(sv["ktok"][:, i, :], ps[b][:, 0:256], reads=[("ps", b)], writes=[(pf + "ktok", i)])
                  proj_tok(slot, 256, 256, i, evkt)
                  yield
              slot = load_w([((lambda w_: w_[:, :, :]), w_in_v[:, :, O_GV + h * 512:O_GV + (h + 1) * 512])])
              for i in range(NT):
                  def evv(b, i=i):
                      copy_ev(sv["v"][:, i, :], ps[b][:, :], reads=[("ps", b)], writes=[(pf + "v", i)])
                  proj_tok(slot, 0, 512, i, evv)
                  yield
              slot = load_w([((lambda w_: w_[:, :, :]), w_in_v[:, :, O_GR + h * 512:O_GR + (h + 1) * 512])])
              for i in range(NT):
                  def evr(b, i=i):
                      S.op("act", lambda e: e.activation(out=sv["rs"][:, i, :], in_=ps[b][:, :], func=AF.Silu),
                           reads=[("ps", b)], writes=[(pf + "rs", i)])
                  proj_tok(slot, 0, 512, i, evr)
                  yield
              slot = load_w([((lambda w_: w_[:, :, :]), w_in_v[:, :, O_GB + h * 512:O_GB + (h + 1) * 512])])
              for i in range(NT):
                  def evb(b, i=i):
                      S.op("act", lambda e: e.activation(out=sv["bs"][:, i, :], in_=ps[b][:, :], func=AF.Sigmoid),
                           reads=[("ps", b)], writes=[(pf + "bs", i)])
                      S.op("dve", lambda e: e.tensor_tensor(out=sv["rs"][:, i, :], in0=sv["rs"][:, i, :],
                                                            in1=sv["bs"][:, i, :], op=ALU.mult),
                           reads=[(pf + "rs", i), (pf + "bs", i)], writes=[(pf + "rs", i)])
                      S.op("dve", lambda e: e.tensor_tensor(out=sv["rs"][:, i, :], in0=sv["rs"][:, i, :],
                                                            in1=gnw_sb[:], op=ALU.mult),
                           reads=[(pf + "rs", i), "gnw"], writes=[(pf + "rs", i)])
                  proj_tok(slot, 0, 512, i, evb)
                  yield

          def chain_gen(h):
              sv = gsets[(h + 1) % 2]
              pf = sv["pf"]
              qT_, kT_, ktok_, v_, rs_ = sv["qT"], sv["kT"], sv["ktok"], sv["v"], sv["rs"]
              for i in range(NT):
                  b = nps()
                  mm([(ps[b][:, 0:256], glrT[0:17, i * 128:(i + 1) * 128], w2b[0:17, h * 256:(h + 1) * 256], True, True)],
                     reads=["glrT", "w2b"], writes=[("ps", b)])
                  S.op("act", lambda e, b=b, i=i: e.activation(out=g_la[:, i, :], in_=ps[b][:, 0:256], func=AF.Exp,
                                                               scale=-1.0),
                       reads=[("ps", b)], writes=[("g_la", i)])
                  S.op("act", lambda e, i=i: e.activation(out=g_la[:, i, :], in_=g_la[:, i, :], func=AF.Ln, bias=1.0),
                       reads=[("g_la", i)], writes=[("g_la", i)])
              la_all = [("g_la", i) for i in range(NT)]
              for dk in range(2):
                  b = nps()
                  mm([(ps[b][:, i * 128:(i + 1) * 128], g_la[:, i, dk * 128:(dk + 1) * 128], Uc, True, True)
                      for i in range(NT)], reads=la_all + ["c_f32"], writes=[("ps", b)])
                  S.op("act", lambda e, b=b, dk=dk: e.activation(out=g_eg[:, dk, :], in_=ps[b][:, :], func=AF.Exp),
                       reads=[("ps", b)], writes=[("g_eg", dk)])
                  S.op("act", lambda e, b=b, dk=dk: e.activation(out=g_ei[:, dk, :], in_=ps[b][:, :], func=AF.Exp,
                                                                 scale=-1.0),
                       reads=[("ps", b)], writes=[("g_ei", dk)])
              for i2 in range(NT // 2):
                  b = nps()
                  mm([(ps[b][:, j * 256:(j + 1) * 256], Usuf, g_la[:, i2 * 2 + j, :], True, True) for j in range(2)],
                     reads=la_all + ["c_f32"], writes=[("ps", b)])
                  S.op("act", lambda e, b=b, i2=i2: e.activation(
                      out=g_es[:, i2 * 2:i2 * 2 + 2, :], in_=ps[b][:, :].rearrange("p (a b) -> p a b", a=2), func=AF.Exp),
                      reads=[("ps", b)], writes=[("g_es", i2)])
              yield
              for dk in range(2):
                  S.op("dve", lambda e, dk=dk: e.tensor_tensor(out=qT_[:, dk, :], in0=qT_[:, dk, :],
                                                               in1=g_eg[:, dk, :], op=ALU.mult),
                       reads=[(pf + "qT", dk), ("g_eg", dk)], writes=[(pf + "qT", dk)])
                  S.op("dve", lambda e, dk=dk: e.tensor_tensor(out=kT_[:, dk, :], in0=kT_[:, dk, :],
                                                               in1=g_ei[:, dk, :], op=ALU.mult),
                       reads=[(pf + "kT", dk), ("g_ei", dk)], writes=[(pf + "kT", dk)])
              for i2 in range(NT // 2):
                  S.op("dve", lambda e, i2=i2: e.tensor_tensor(out=ktok_[:, i2 * 2:i2 * 2 + 2, :],
                                                               in0=ktok_[:, i2 * 2:i2 * 2 + 2, :],
                                                               in1=g_es[:, i2 * 2:i2 * 2 + 2, :], op=ALU.mult),
                       reads=[(pf + "ktok", i2 * 2), (pf + "ktok", i2 * 2 + 1), ("g_es", i2)],
                       writes=[(pf + "ktok", i2 * 2), (pf + "ktok", i2 * 2 + 1)])
              b = nps()
              mm([(ps[b][:, i * 128:(i + 1) * 128], kT_[:, dk, i * 128:(i + 1) * 128],
                   qT_[:, dk, i * 128:(i + 1) * 128], dk == 0, dk == 1) for i in range(NT) for dk in range(2)],
                 reads=[(pf + "qT", 0), (pf + "qT", 1), (pf + "kT", 0), (pf + "kT", 1)], writes=[("ps", b)])
              trb = tril.unsqueeze(1).to_broadcast([128, NT, 128])
              S.op("dve", lambda e, b=b, trb=trb: e.tensor_tensor(
                  out=g_attm[:, :, :], in0=ps[b][:, :].rearrange("p (a b) -> p a b", a=NT), in1=trb, op=ALU.mult),
                  reads=[("ps", b), "c_f32"], writes=["g_attm"])
              yield
              for i in range(NT):
                  bo = nps()
                  mm([(ps[bo][:, :], g_attm[:, i, :], v_[:, i, :], True, False),
                      (ps[bo][:, :], qT_[:, 0, i * 128:(i + 1) * 128], Sbf[:, h, 0, :], False, False),
                      (ps[bo][:, :], qT_[:, 1, i * 128:(i + 1) * 128], Sbf[:, h, 1, :], False, True)],
                     reads=["g_attm", (pf + "v", i), (pf + "qT", 0), (pf + "qT", 1), ("Sb", h, 0), ("Sb", h, 1)],
                     writes=[("ps", bo)])
                  S.op("act", lambda e, bo=bo, i=i: e.activation(out=g_G[:, i, :], in_=ps[bo][:, :], func=AF.Square,
                                                                 accum_out=stat[:, 4:5]),
                       reads=[("ps", bo)], writes=[("g_G", i), "st4"])
                  S.op("act", lambda e: e.activation(out=stat[:, 5:6], in_=stat[:, 4:5], func=AF.Sqrt, scale=1.0 / 512,
                                                     bias=epsc[:]), reads=["st4", "epsc"], writes=["st5"])
                  S.op("dve", lambda e: e.reciprocal(out=stat[:, 6:7], in_=stat[:, 5:6]), reads=["st5"], writes=["st6"])
                  S.op("dve", lambda e, bo=bo, i=i: e.scalar_tensor_tensor(
                      out=g_G[:, i, :], in0=ps[bo][:, :], scalar=stat[:, 6:7], in1=rs_[:, i, :], op0=ALU.mult,
                      op1=ALU.mult), reads=[("ps", bo), "st6", (pf + "rs", i)], writes=[("g_G", i)])
                  half = nps()
                  tvg = trview(half)

                  def trg(e, i=i, tvg=tvg):
                      ins = None
                      for j in range(4):
                          ins = e.transpose(out=tvg[:, j, :], in_=g_G[:, i, j * 128:(j + 1) * 128],
                                            identity=identb[:])
                      return ins
                  S.op("pe", trg, reads=[("g_G", i), "identb"], writes=[("ps", half)])
                  S.op("dve", lambda e, i=i, tvg=tvg, h=h: e.tensor_tensor(
                      out=mT[:, h * 4:(h + 1) * 4, i * 128:(i + 1) * 128], in0=tvg,
                      in1=mT[:, h * 4:(h + 1) * 4, i * 128:(i + 1) * 128], op=ALU.add),
                      reads=[("ps", half), ("mT", h, i)], writes=[("mT", h, i)])
                  yield
                  for dk in range(2):
                      bu = nps()
                      mm([(ps[bu][:, :], ktok_[:, i, dk * 128:(dk + 1) * 128], v_[:, i, :], True, True)],
                         reads=[(pf + "ktok", i), (pf + "v", i)], writes=[("ps", bu)])
                      S.op("dve", lambda e, bu=bu, dk=dk, i=i, h=h: e.scalar_tensor_tensor(
                          out=Sst[:, h, dk, :], in0=Sst[:, h, dk, :], scalar=g_eg[:, dk, i * 128 + 127:i * 128 + 128],
                          in1=ps[bu][:, :], op0=ALU.mult, op1=ALU.add),
                          reads=[("ps", bu), ("S", h, dk), ("g_eg", dk)], writes=[("S", h, dk)])
                      S.op("act", lambda e, dk=dk, h=h: e.copy(out=Sbf[:, h, dk, :], in_=Sst[:, h, dk, :]),
                           reads=[("S", h, dk)], writes=[("Sb", h, dk)])
                  yield

          for _ in proj_gen(0):
              pass
          for h in range(4):
              pg = proj_gen(h + 1) if h < 3 else None
              for _ in chain_gen(h):
                  if pg is not None:
                      for _k in range(2):
                          next(pg, None)
              if pg is not None:
                  for _ in pg:
                      pass
              if h == 2:
                  S.op("dve", lambda e: e.memset(bar2[:], 0.0),
                       writes=["XTS", "bar2"] + [("x", i_) for i_ in range(NT)])
                  for i in range(NT):
                      S.op("sp", lambda e, i=i, t0=t0: e.dma_start(out=xt[:, i, :],
                                                                   in_=x[t0 + i * 128:t0 + (i + 1) * 128, :]),
                           writes=[("x", i)], dma=True, chan=("x", i))
          barrier()
          if debug and c == 0:
              S.op("pool", lambda e: e.dma_start(out=dbg["mT"], in_=mT[:]),
                   reads=[("mT", g_, i_) for g_ in range(4) for i_ in range(NT)], writes=["dbg_mT"], dma=True)

          ckpt('D')
          for g in range(4):
              slot = load_w([((lambda w_: w_[:, :, :]), w_out_v[:, :, g * 512:(g + 1) * 512])])
              for i in range(NT):
                  b = nps()
                  mm([(ps[b][:, :], mT[:, k, i * 128:(i + 1) * 128], wsl[slot][:, k, :], k == 0, k == KC - 1)
                      for k in range(KC)], reads=wres(slot) + [("mT", g_, i) for g_ in range(4)], writes=[("ps", b)])
                  S.op("dve", lambda e, b=b, i=i, g=g: e.tensor_tensor(
                      out=xt[:, i, g * 512:(g + 1) * 512], in0=ps[b][:, :], in1=xt[:, i, g * 512:(g + 1) * 512],
                      op=ALU.add), reads=[("ps", b), ("x", i)], writes=[("x", i)])
          if debug and c == 0:
              S.op("sp", lambda e: e.dma_start(out=dbg["h1"], in_=xt[:]), reads=[("x", i_) for i_ in range(NT)],
                   writes=["dbg_h1"], dma=True)
          for i in range(NT):
              rms_to_actT(i, n2_sb, "n2")

          ckpt('E')
          for j in range(HC // 2):
              slot = load_w([((lambda w_: w_[:, :, 0:256]), w_g_v[:, :, j * 256:(j + 1) * 256]),
                             ((lambda w_: w_[:, :, 256:512]), w_u_v[:, :, j * 256:(j + 1) * 256])])
              for sub in range(2):
                  hc = j * 2 + sub
                  bg = nps()
                  mm([(ps[bg][:, :], wsl[slot][:, k, sub * 128:(sub + 1) * 128], actT[:, k, :], k == 0, k == KC - 1)
                      for k in range(KC)], reads=wres(slot) + actT_all, writes=[("ps", bg)])
                  bu = nps()
                  mm([(ps[bu][:, :], wsl[slot][:, k, 256 + sub * 128:256 + (sub + 1) * 128], actT[:, k, :], k == 0,
                       k == KC - 1) for k in range(KC)], reads=wres(slot) + actT_all, writes=[("ps", bu)])
                  S.op("act", lambda e, bg=bg, hc=hc: e.activation(out=ffT[:, hc, :], in_=ps[bg][:, :], func=AF.Silu),
                       reads=[("ps", bg)], writes=ar(("ffT", hc)))
                  S.op("dve", lambda e, bu=bu, hc=hc: e.tensor_tensor(out=ffT[:, hc, :], in0=ps[bu][:, :],
                                                                      in1=ffT[:, hc, :], op=ALU.mult),
                       reads=[("ps", bu), ("ffT", hc)], writes=[("ffT", hc)])

          ckpt('F')
          NPIECE = 4
          PH = HC // NPIECE
          for g in range(4):
              banks = [nps() for _ in range(NT)]
              for p in range(NPIECE):
                  slot = load_w([((lambda w_: w_[:, 0:PH, :]), w_d_v[:, p * PH:(p + 1) * PH, g * 512:(g + 1) * 512])],
                              ring=(0, 1, 2))
                  for i in range(NT):
                      b = banks[i]
                      mm([(ps[b][:, :], ffT[:, p * PH + q, i * 128:(i + 1) * 128], wsl[slot][:, q, :],
                           (p == 0 and q == 0), (p == NPIECE - 1 and q == PH - 1)) for q in range(PH)],
                         reads=wres(slot) + [("ffT", p * PH + q) for q in range(PH)], writes=[("ps", b)])
              for i in range(NT):
                  b = banks[i]
                  S.op("dve", lambda e, b=b, i=i, g=g: e.tensor_tensor(
                      out=xt[:, i, g * 512:(g + 1) * 512], in0=ps[b][:, :], in1=xt[:, i, g * 512:(g + 1) * 512],
                      op=ALU.add), reads=[("ps", b), ("x", i)], writes=[("x", i)])
              if c + 1 < nchunk:
                  norm1_from_dram(c + 1, g)

          ckpt('G')
          for i in range(NT):
              pending.append((c, i))
          if c == nchunk - 1:
              while pending:
                  drain_final()
          barrier()

    except _Stop:
        pass
    S.finalize_on("sp", out_dmas)
    if stage is not None:
        S.finalize_on("sp", [o for o in S.streams["sp"] if o.is_dma])
        S.finalize_on("pool", [o for o in S.streams["pool"] if o.is_dma])
    if debug:
        dd = [o for o in S.streams["pool"] if o.is_dma and o.chan[1] in ("dbg_mT", "dbg_xnT")]
        S.finalize_on("pool", dd)
        dd2 = [o for o in S.streams["sp"] if o.is_dma and o.chan[1] == "dbg_h1"]
        S.finalize_on("sp", dd2)
    S.emit()
    return nc


def _consts():
    j = np.arange(128)[:, None]
    i = np.arange(128)[None, :]
    cst = np.zeros((128, 5, 128), np.float32)
    cst[:, 0, :] = np.eye(128, dtype=np.float32)
    cst[:, 1, :] = np.where(j <= i, -1.0 / 16.0, 0.0)
    cst[:, 2, :] = np.where(j > i, -1.0 / 16.0, 0.0)
    cst[:, 3, :] = np.where(j <= i, 1.0, 0.0)
    mk = np.zeros((128, 2, 512), np.float32)
    mprev = np.where(j > i, 0.0, NEG).astype(np.float32)
    mcur = np.where(j <= i, 0.0, NEG).astype(np.float32)
    mk[:, 0, :] = np.tile(mprev, (1, 4))
    mk[:, 1, :] = np.tile(mcur, (1, 4))
    return cst, mk


def _col_layout(v):
    return np.ascontiguousarray(np.asarray(v, np.float32).reshape(KC, 128).T)


def make_in_maps(inputs, ncores=8):
    f = lambda a: np.ascontiguousarray(np.asarray(a, dtype=np.float32))
    x = f(inputs["x"])
    cst, mk = _consts()
    sinks = f(inputs["attn_sinks"])[0]
    sink_l = np.zeros((128, 16), np.float32)
    for cch in range(16):
        sink_l[0:64, cch] = sinks[2 * cch]
        sink_l[64:128, cch] = sinks[2 * cch + 1]
    shared = {
        "w_in": f(inputs["w_in"])[0],
        "w_out": f(inputs["w_out"])[0],
        "w_g": f(inputs["w_ffn_gate"])[0],
        "w_u": f(inputs["w_ffn_up"])[0],
        "w_d": f(inputs["w_ffn_down"])[0],
        "n1": _col_layout(f(inputs["norm1_w"])[0]),
        "n2": _col_layout(f(inputs["norm2_w"])[0]),
        "sink_l": sink_l,
        "fnw": np.ascontiguousarray(np.broadcast_to(f(inputs["final_norm_w"])[None, :], (128, D))),
        "gnw": np.ascontiguousarray(np.broadcast_to(f(inputs["gla_norm_w"])[0][None, :], (128, 512))),
        "w2b": np.ascontiguousarray(np.concatenate([f(inputs["gla_gate_w2"])[0], f(inputs["gla_gate_b"])[0][None, :]],
                                                   axis=0)),
        "cst": cst,
        "mk": mk,
    }
    maps = []
    for b in range(ncores):
        m = dict(shared)
        m["x"] = np.ascontiguousarray(x[b])
        maps.append(m)
    return maps


def kernel(**inputs):
    nc = build_nc()
    in_maps = make_in_maps(inputs, 8)
    res = run_bass_kernel_spmd(nc, in_maps, core_ids=list(range(8)))
    return np.stack([np.asarray(r["out"], dtype=np.float32) for r in res.results], axis=0)
```
